# Optimizing a Trainium2 kernel written in Bass

```python
import math
import jax
import jax.numpy as jnp
from jax import lax
import numpy as np

D_MODEL = 2048
BATCH = 4
SEQ = 2048
DEPTH = 4
DEC_BATCH = 128
DEC_SEQ = 8
PAST_LEN = 16384
PAGE_SIZE = 128

N_MIXERS = 2
N_S5 = (DEPTH + 1) // 2
N_HG = DEPTH // 2
S5_GROUP = 16
S5_GROUPS = D_MODEL // S5_GROUP
S5_STATE = 64
S5_DT_MIN = 1e-3
S5_DT_MAX = 1e-1
HG_EXPAND = 128
HG_HEADS = D_MODEL // HG_EXPAND
HG_DK = HG_EXPAND
HG_DV = D_MODEL // HG_HEADS
HG_FDIM = HG_HEADS * HG_DK
HG_CHUNK = 64
N_MEM = 256
X_HEADS = 4
X_HEAD_DIM = D_MODEL // X_HEADS
D_FF = ((-(-8 * D_MODEL // 3) + 255) // 256) * 256
EPS = 1e-6

kernel_name = 'hybrid_s5_hgrn2_memxattn_step'


def rmsnorm(x, g):
    xf = x.astype(jnp.float32)
    y = xf * lax.rsqrt(jnp.mean(xf * xf, axis=-1, keepdims=True) + EPS)
    return (y * g.astype(jnp.float32)).astype(x.dtype)


def s5_mixer(u, s0_re, s0_im, lam_re, lam_im, log_dt, b_re, b_im, c_re, c_im, d_skip, w_glu):
    f32 = jnp.float32
    bn, length, _ = u.shape
    lam = lax.complex(lam_re.astype(f32), lam_im.astype(f32))
    dt = jnp.exp(log_dt.astype(f32))[:, None]
    lam_bar = jnp.exp(lam * dt)
    b_bar = ((lam_bar - 1.0) / lam)[..., None] * lax.complex(b_re.astype(f32), b_im.astype(f32))
    uf = u.astype(f32)
    ug = uf.reshape(bn, length, S5_GROUPS, S5_GROUP).astype(jnp.complex64)
    bu = jnp.einsum('gnp,blgp->blgn', b_bar, ug)
    s0 = lax.complex(s0_re.astype(f32), s0_im.astype(f32))
    bu = bu.at[:, 0].add(lam_bar * s0)
    a = jnp.broadcast_to(lam_bar, bu.shape)

    def combine(left, right):
        a_l, b_l = left
        a_r, b_r = right
        return a_r * a_l, a_r * b_l + b_r

    _, states = lax.associative_scan(combine, (a, bu), axis=1)
    cmat = lax.complex(c_re.astype(f32), c_im.astype(f32))
    y = jnp.real(jnp.einsum('gpn,blgn->blgp', cmat, states)).reshape(bn, length, D_MODEL)
    y = y + d_skip.astype(f32) * uf
    h = jax.nn.gelu(y)
    out = h * jax.nn.sigmoid(h @ w_glu.astype(f32))
    s_last = states[:, -1]
    return out.astype(u.dtype), jnp.real(s_last), jnp.imag(s_last)


def hgrn_lower_bounds(raw):
    p = jax.nn.softmax(raw.astype(jnp.float32), axis=0)
    return jnp.cumsum(p, axis=0) - p[0:1]


def gated_linear_recurrence(q, k, v, logf, s0):
    bn, length, nh, _ = q.shape
    csz = math.gcd(length, HG_CHUNK)
    nch = length // csz

    def to_chunks(t):
        return jnp.moveaxis(t.reshape(bn, nch, csz, nh, t.shape[-1]), 1, 0)

    causal = jnp.tril(jnp.ones((csz, csz), dtype=bool))[None, :, :, None, None]

    def step(state, xs):
        qc, kc, vc, lc = xs
        g = jnp.cumsum(lc, axis=1)
        o_inter = jnp.einsum('bthk,bhkv->bthv', qc * jnp.exp(g), state)
        diff = g[:, :, None] - g[:, None, :]
        decay = jnp.exp(jnp.where(causal, diff, -jnp.inf))
        scores = jnp.einsum('bthk,bshk,btshk->bhts', qc, kc, decay)
        o_intra = jnp.einsum('bhts,bshv->bthv', scores, vc)
        g_last = g[:, -1]
        k_dec = kc * jnp.exp(g_last[:, None] - g)
        new_state = jnp.exp(g_last)[..., None] * state + jnp.einsum('bshk,bshv->bhkv', k_dec, vc)
        return new_state, o_inter + o_intra

    s_last, o = lax.scan(step, s0, (to_chunks(q), to_chunks(k), to_chunks(v), to_chunks(logf)))
    o = jnp.moveaxis(o, 0, 1).reshape(bn, length, nh, v.shape[-1])
    return o, s_last


def hgrn2_mixer(xn, s0, w_in, lb, g_norm, w_out):
    f32 = jnp.float32
    bn, length, _ = xn.shape
    proj = (xn @ w_in).astype(f32)
    q, f, i, g = jnp.split(proj, [HG_FDIM, 2 * HG_FDIM, 2 * HG_FDIM + D_MODEL], axis=-1)
    q = jax.nn.silu(q).reshape(bn, length, HG_HEADS, HG_DK)
    fgate = lb + (1.0 - lb) * jax.nn.sigmoid(f)
    logf = jnp.log(fgate).reshape(bn, length, HG_HEADS, HG_DK)
    k = (1.0 - fgate).reshape(bn, length, HG_HEADS, HG_DK)
    v = i.reshape(bn, length, HG_HEADS, HG_DV)
    o, s_last = gated_linear_recurrence(q, k, v, logf, s0.astype(f32))
    o = o * lax.rsqrt(jnp.mean(o * o, axis=-1, keepdims=True) + EPS)
    o = o * g_norm.astype(f32).reshape(HG_HEADS, HG_DV)
    o = o.reshape(bn, length, D_MODEL) * jax.nn.silu(g)
    return (o @ w_out.astype(f32)).astype(xn.dtype), s_last


def cross_attend(xn, mem_k, mem_v, w_q, w_o):
    f32 = jnp.float32
    bn, length, _ = xn.shape
    q = (xn @ w_q).astype(f32).reshape(bn, length, X_HEADS, X_HEAD_DIM)
    s = jnp.einsum('blhd,bmhd->bhlm', q, mem_k.astype(f32)) * (1.0 / math.sqrt(X_HEAD_DIM))
    p = jax.nn.softmax(s, axis=-1)
    o = jnp.einsum('bhlm,bmhd->blhd', p, mem_v.astype(f32)).reshape(bn, length, D_MODEL)
    return (o @ w_o.astype(f32)).astype(xn.dtype)


def swiglu(xn, w_in, w_out):
    h = (xn @ w_in).astype(jnp.float32)
    gate, up = jnp.split(h, 2, axis=-1)
    return ((jax.nn.silu(gate) * up) @ w_out.astype(jnp.float32)).astype(xn.dtype)


def trunk(x, s5_re, s5_im, hg_s, mem_k, mem_v, p):
    lb_all = hgrn_lower_bounds(p['hg_lower_bounds'])
    re_out, im_out, hg_out = [], [], []
    for layer in range(DEPTH):
        j = layer // N_MIXERS
        h = rmsnorm(x, p['norm_mix'][layer])
        if layer % N_MIXERS == 0:
            out, s_re, s_im = s5_mixer(h, s5_re[j], s5_im[j], p['s5_lam_re'][j], p['s5_lam_im'][j],
                                       p['s5_log_dt'][j], p['s5_b_re'][j], p['s5_b_im'][j],
                                       p['s5_c_re'][j], p['s5_c_im'][j], p['s5_d'][j], p['s5_w_glu'][j])
            re_out.append(s_re)
            im_out.append(s_im)
        else:
            out, s_new = hgrn2_mixer(h, hg_s[j], p['hg_w_in'][j], lb_all[layer],
                                     p['hg_g_norm'][j], p['hg_w_out'][j])
            hg_out.append(s_new)
        x = x + out
        x = x + cross_attend(rmsnorm(x, p['norm_xattn'][layer]), mem_k[layer], mem_v[layer],
                             p['x_w_q'][layer], p['x_w_o'][layer])
        x = x + swiglu(rmsnorm(x, p['norm_ffn'][layer]), p['ffn_w_in'][layer], p['ffn_w_out'][layer])
    y = rmsnorm(x, p['norm_final'])
    return y, jnp.stack(re_out), jnp.stack(im_out), jnp.stack(hg_out)


def setup_inputs(seed: int = 0) -> dict:
    key = jax.random.key(seed)
    ks = iter(jax.random.split(key, 48))
    f32 = jnp.float32

    def nrm(shape, scale):
        return scale * jax.random.normal(next(ks), shape, f32)

    d = D_MODEL
    return {
        'x_prompt': nrm((BATCH, SEQ, d), 1.0),
        'x_sample': nrm((DEC_BATCH, DEC_SEQ, d), 1.0),
        'state_s5_re': nrm((N_S5, DEC_BATCH, S5_GROUPS, S5_STATE), 0.1),
        'state_s5_im': nrm((N_S5, DEC_BATCH, S5_GROUPS, S5_STATE), 0.1),
        'state_hgrn': nrm((N_HG, DEC_BATCH, HG_HEADS, HG_DK, HG_DV), 0.5),
        'cache_mem_k': nrm((DEPTH, DEC_BATCH, N_MEM, X_HEADS, X_HEAD_DIM), 1.0),
        'cache_mem_v': nrm((DEPTH, DEC_BATCH, N_MEM, X_HEADS, X_HEAD_DIM), 1.0),
        'mem_prompt': nrm((BATCH, N_MEM, d), 1.0),
        'norm_mix': 1.0 + nrm((DEPTH, d), 0.02),
        'norm_xattn': 1.0 + nrm((DEPTH, d), 0.02),
        'norm_mem_in': 1.0 + nrm((DEPTH, d), 0.02),
        'norm_ffn': 1.0 + nrm((DEPTH, d), 0.02),
        'norm_final': 1.0 + nrm((d,), 0.02),
        's5_lam_re': -0.5 + nrm((N_S5, S5_GROUPS, S5_STATE), 0.01),
        's5_lam_im': jnp.pi * jnp.arange(S5_STATE, dtype=f32) + nrm((N_S5, S5_GROUPS, S5_STATE), 0.01),
        's5_log_dt': jax.random.uniform(next(ks), (N_S5, S5_GROUPS), f32,
                                        math.log(S5_DT_MIN), math.log(S5_DT_MAX)),
        's5_b_re': nrm((N_S5, S5_GROUPS, S5_STATE, S5_GROUP), (2 * S5_GROUP) ** -0.5),
        's5_b_im': nrm((N_S5, S5_GROUPS, S5_STATE, S5_GROUP), (2 * S5_GROUP) ** -0.5),
        's5_c_re': nrm((N_S5, S5_GROUPS, S5_GROUP, S5_STATE), S5_STATE ** -0.5),
        's5_c_im': nrm((N_S5, S5_GROUPS, S5_GROUP, S5_STATE), S5_STATE ** -0.5),
        's5_d': nrm((N_S5, d), 1.0),
        's5_w_glu': nrm((N_S5, d, d), d ** -0.5),
        'hg_w_in': nrm((N_HG, d, 2 * HG_FDIM + 2 * d), d ** -0.5),
        'hg_lower_bounds': nrm((DEPTH, HG_FDIM), 0.1),
        'hg_g_norm': 1.0 + nrm((N_HG, d), 0.02),
        'hg_w_out': nrm((N_HG, d, d), d ** -0.5),
        'x_w_q': nrm((DEPTH, d, d), d ** -0.5),
        'x_w_k': nrm((DEPTH, d, d), d ** -0.5),
        'x_w_v': nrm((DEPTH, d, d), d ** -0.5),
        'x_w_o': nrm((DEPTH, d, d), d ** -0.5),
        'ffn_w_in': nrm((DEPTH, d, 2 * D_FF), d ** -0.5),
        'ffn_w_out': nrm((DEPTH, D_FF, d), D_FF ** -0.5),
    }


def reference(x_prompt, x_sample, state_s5_re, state_s5_im, state_hgrn, cache_mem_k, cache_mem_v,
              mem_prompt, norm_mix, norm_xattn, norm_mem_in, norm_ffn, norm_final,
              s5_lam_re, s5_lam_im, s5_log_dt, s5_b_re, s5_b_im, s5_c_re, s5_c_im, s5_d, s5_w_glu,
              hg_w_in, hg_lower_bounds, hg_g_norm, hg_w_out,
              x_w_q, x_w_k, x_w_v, x_w_o, ffn_w_in, ffn_w_out):
    p = dict(norm_mix=norm_mix, norm_xattn=norm_xattn, norm_ffn=norm_ffn, norm_final=norm_final,
             s5_lam_re=s5_lam_re, s5_lam_im=s5_lam_im, s5_log_dt=s5_log_dt,
             s5_b_re=s5_b_re, s5_b_im=s5_b_im, s5_c_re=s5_c_re, s5_c_im=s5_c_im,
             s5_d=s5_d, s5_w_glu=s5_w_glu, hg_w_in=hg_w_in, hg_lower_bounds=hg_lower_bounds,
             hg_g_norm=hg_g_norm, hg_w_out=hg_w_out, x_w_q=x_w_q, x_w_o=x_w_o,
             ffn_w_in=ffn_w_in, ffn_w_out=ffn_w_out)
    f32 = jnp.float32
    bp = x_prompt.shape[0]

    mem_n = rmsnorm(mem_prompt[None], norm_mem_in[:, None, None, :])
    mem_k_prompt = jnp.einsum('lbmd,lde->lbme', mem_n, x_w_k).reshape(DEPTH, bp, N_MEM, X_HEADS, X_HEAD_DIM)
    mem_v_prompt = jnp.einsum('lbmd,lde->lbme', mem_n, x_w_v).reshape(DEPTH, bp, N_MEM, X_HEADS, X_HEAD_DIM)
    zeros_s5 = jnp.zeros((N_S5, bp, S5_GROUPS, S5_STATE), f32)
    zeros_hg = jnp.zeros((N_HG, bp, HG_HEADS, HG_DK, HG_DV), f32)
    y_prompt, s5_re_prompt, s5_im_prompt, hgrn_prompt = trunk(
        x_prompt, zeros_s5, zeros_s5, zeros_hg, mem_k_prompt, mem_v_prompt, p)

    y_sample, s5_re_sample, s5_im_sample, hgrn_sample = trunk(
        x_sample, state_s5_re, state_s5_im, state_hgrn, cache_mem_k, cache_mem_v, p)

    return (y_prompt, y_sample, s5_re_prompt, s5_im_prompt, hgrn_prompt, mem_k_prompt, mem_v_prompt,
            s5_re_sample, s5_im_sample, hgrn_sample)
```

```python
import math
import numpy as np
from contextlib import ExitStack
import concourse.bass as bass
import concourse.mybir as mybir
from concourse.bass_utils import run_bass_kernel_spmd

F32 = mybir.dt.float32
BF16 = mybir.dt.bfloat16
I32 = mybir.dt.int32
AF = mybir.ActivationFunctionType
ALU = mybir.AluOpType
PE, ACT, DVE, POOL, SP = "tensor", "scalar", "vector", "gpsimd", "sync"

D = 2048
KC = 16
DFF = 5632
NMEM = 256
EPS = 1e-6
NCORE = 8
SPC = 16
TP = 512
NPASS = 4
TWO_PI = 2.0 * math.pi
CFG = dict(s5lvl=9, npass=4, s5=True, hg=True, xa=True, ffn=True, sample=True, setup=True)


class Buf:
    __slots__ = ("name", "last_w", "readers", "sem", "cnt")

    def __init__(self, name):
        self.name = name
        self.last_w = None
        self.readers = []
        self.sem = None
        self.cnt = 0


class Op:
    __slots__ = ("eng", "fn", "deps", "dma", "ev", "signal")

    def __init__(self, eng, fn, dma):
        self.eng = eng
        self.fn = fn
        self.deps = []
        self.dma = dma
        self.ev = None
        self.signal = False


class Prog:
    def __init__(self, nc, stack, n_dma_sems=80):
        self.nc = nc
        self.stack = stack
        self.ops = []
        self.esem = {e: [stack.enter_context(nc.semaphore("es%d_" % k + e)) for k in range(4)] for e in (PE, ACT, DVE, POOL)}
        self.free_sems = []
        for i in range(n_dma_sems):
            try:
                self.free_sems.append([stack.enter_context(nc.semaphore("ds%d" % i)), 0])
            except KeyError:
                break
        print("dma sems", len(self.free_sems))
        self.all_chans = list(self.free_sems)
        self.live = []
        self.pending = {e: [] for e in (PE, ACT, DVE, POOL, SP)}
        self.dma_bufs = []

    def buf(self, name="b"):
        return Buf(name)

    def sb(self, name, shape, dtype):
        return self.stack.enter_context(self.nc.sbuf_tensor(name, list(shape), dtype))

    def ps(self, name, shape, dtype=F32):
        return self.stack.enter_context(self.nc.psum_tensor(name, list(shape), dtype))

    def op(self, eng, fn, reads=(), writes=(), dma=False):
        o = Op(eng, fn, dma)
        deps = list(self.pending[eng])
        self.pending[eng] = []
        for r in reads:
            if r.last_w is not None:
                deps.append(r.last_w)
        for w in writes:
            if w.last_w is not None:
                deps.append(w.last_w)
            lastr = {}
            for rd in w.readers:
                if rd.dma:
                    deps.append(rd)
                else:
                    lastr[rd.eng] = rd
            deps.extend(lastr.values())
        for r in reads:
            r.readers.append(o)
        for w in writes:
            w.last_w = o
            w.readers = []
        if dma:
            tgt = (list(writes) + list(reads))[0]
            if tgt.sem is not None and tgt.sem[1] >= 1800:
                tgt.sem = None
                if tgt in self.live:
                    self.live.remove(tgt)
            if tgt.sem is None:
                tgt.sem = self.free_sems.pop()
                tgt.cnt = tgt.sem[1]
                self.live.append(tgt)
            tgt.cnt += 1
            tgt.sem[1] = tgt.cnt
            o.ev = (tgt.sem[0], 16 * tgt.cnt)
        seen = set()
        for d in deps:
            if d is o or id(d) in seen:
                continue
            seen.add(id(d))
            if (not d.dma) and d.eng == PE and eng == PE and not dma:
                continue
            o.deps.append(d)
            d.signal = True
        self.ops.append(o)
        return o

    def dma(self, eng, out, in_, reads=(), writes=(), **kw):
        return self.op(eng, lambda e: e.dma_start(out=out, in_=in_, **kw), reads=reads, writes=writes, dma=True)

    def barrier(self):
        last = {}
        lastd = {}
        for o in self.ops:
            if o.dma:
                lastd[id(o.ev[0])] = o
            else:
                last[o.eng] = o
        deps = list(last.values()) + list(lastd.values())
        for e in self.pending:
            self.pending[e] = list(deps)
        keep = []
        for b in self.live:
            if getattr(b, "name", "").startswith("keep"):
                keep.append(b)
            else:
                if b.sem[1] < 1700:
                    self.free_sems.append(b.sem)
                b.sem = None
        self.live = keep

    def emit(self):
        nc = self.nc
        cnt = {e: 0 for e in self.esem}
        for o in self.ops:
            if (not o.dma) and o.signal:
                cnt[o.eng] += 1
                ep, v = divmod(cnt[o.eng] - 1, 30000)
                o.ev = (self.esem[o.eng][ep], v + 1)
        print("signal counts", cnt)
        per = {e: [] for e in (PE, ACT, DVE, POOL, SP)}
        for o in self.ops:
            per[o.eng].append(o)
        finals = [(c[0], 16 * c[1]) for c in self.all_chans if c[1] > 0]

        def run(engname, eng):
            waited = {}
            for o in per[engname]:
                for d in o.deps:
                    sem, val = d.ev
                    k = id(sem)
                    if waited.get(k, 0) >= val:
                        continue
                    eng.wait_ge(sem, val)
                    waited[k] = val
                ins = o.fn(eng)
                if o.dma:
                    ins.then_inc(o.ev[0], 16)
                elif o.signal:
                    ins.then_inc(o.ev[0], 1)
            if engname == SP:
                for sem, val in finals:
                    eng.wait_ge(sem, val)

        with nc.Block() as block:
            @block.tensor
            def _(e):
                run(PE, e)

            @block.scalar
            def _(e):
                run(ACT, e)

            @block.vector
            def _(e):
                run(DVE, e)

            @block.gpsimd
            def _(e):
                run(POOL, e)

            @block.sync
            def _(e):
                run(SP, e)
        return {e: len(per[e]) for e in per}


class Arena:
    def __init__(self, P, name, n, dtype):
        self.t = P.sb(name, [128, n], dtype)
        self.n = n
        self.off = 0

    def reset(self):
        self.off = 0

    def take(self, n):
        assert self.off + n <= self.n, (self.off, n, self.n)
        v = self.t[:, self.off:self.off + n]
        self.off += n
        return v


V_MIX, V_XA, V_MEM, V_FFN, V_FIN, V_S5D, V_GN, V_LB = 0, 4, 8, 12, 16, 17, 19, 21
NV = 25


def build_program():
    nc = bass.Bass("TRN2", target_bir_lowering=False)

    def din(name, shape):
        return nc.dram_tensor(name, list(shape), F32, kind="ExternalInput").ap()

    def dout(name, shape):
        return nc.dram_tensor(name, list(shape), F32, kind="ExternalOutput").ap()

    xp = din("xp", [NPASS * TP, D])
    xsm = din("xsm", [128, D])
    memp = din("memp", [NMEM, D])
    ck = din("ck", [4, SPC, NMEM, D])
    cv = din("cv", [4, SPC, NMEM, D])
    hgst = din("hgst", [2, SPC, 16, 128, 128])
    vecs = din("vecs", [128, NV, 16])
    ident_d = din("ident", [128, 128])
    maskT_d = din("maskT", [64, 64])
    lam_sl = din("lam_sl", [2, 128, 3, 64])
    lam_pl = din("lam_pl", [2, 128, 3, 8192])
    b_pl = din("b_pl", [2, 128, 2, 8192])
    c_pl = din("c_pl", [2, 128, 2, 8192])
    s5init = din("s5init", [2, 128, 2, 64, SPC])
    w_glu = din("s5_w_glu", [2, D, D])
    hg_w_in = din("hg_w_in", [2, D, 4 * D])
    hg_w_out = din("hg_w_out", [2, D, D])
    x_w_q = din("x_w_q", [4, D, D])
    x_w_k = din("x_w_k", [4, D, D])
    x_w_v = din("x_w_v", [4, D, D])
    x_w_o = din("x_w_o", [4, D, D])
    ffn_w_in = din("ffn_w_in", [4, D, 2 * DFF])
    ffn_w_out = din("ffn_w_out", [4, DFF, D])

    o_yp = dout("o_yp", [NPASS * TP, D])
    o_ys = dout("o_ys", [128, D])
    o_s5p = dout("o_s5p", [2, 128, 2, 64])
    o_hgp = dout("o_hgp", [2, 16, 128, 128])
    o_mk = dout("o_mk", [4, NMEM, D])
    o_mv = dout("o_mv", [4, NMEM, D])
    o_s5s = dout("o_s5s", [2, 128, 2, 64, SPC])
    o_hgs = dout("o_hgs", [2, SPC, 16, 128, 128])

    with ExitStack() as st:
        P = Prog(nc, st)
        x = P.sb("x", [128, KC, TP], F32)
        bx = P.buf("x")
        xn = P.sb("xn", [128, KC, TP], BF16)
        bxn = P.buf("xn")
        AFa = Arena(P, "AF", 16384, F32)
        AHa = Arena(P, "AH", 23552, BF16)
        NSLOT = 3
        WSL = 4096
        wslots = [P.sb("w%d" % i, [128, WSL], BF16) for i in range(NSLOT)]
        wbufs = [P.buf("keepw%d" % i) for i in range(NSLOT)]
        wctr = [0]
        ident = P.sb("ident_sb", [128, 128], F32)
        identb = P.sb("identb", [128, 128], BF16)
        onesb = P.sb("onesb", [128, 128], BF16)
        maskT = P.sb("maskT_sb", [64, 64], F32)
        vec = P.sb("vec", [128, NV, 16], F32)
        lbv = P.sb("lbv", [128, 4, 16], F32)
        oml = P.sb("oml", [128, 4, 16], F32)
        bconst = P.buf("keepconst")
        s5S = P.sb("s5S", [128, 2, 2, 64], F32)
        bs5S = [P.buf("s5S0"), P.buf("s5S1")]
        rstd = P.sb("rstd", [128, TP], F32)
        brstd = P.buf("rstd")
        psb = [P.ps("ps%d" % i, [128, 512], F32) for i in range(8)]
        bps = [P.buf("ps%d" % i) for i in range(8)]
        psctr = [0]

        def next_ps():
            i = psctr[0] % 6
            psctr[0] += 1
            return psb[i], bps[i]

        def phase():
            P.barrier()
            AFa.reset()
            AHa.reset()

        P.dma(SP, ident[:], ident_d, writes=[bconst])
        P.dma(SP, maskT[:], maskT_d, writes=[bconst])
        P.dma(SP, vec[:], vecs, writes=[bconst])
        P.op(DVE, lambda e: e.tensor_copy(out=identb[:], in_=ident[:]), reads=[bconst], writes=[bconst])
        P.op(DVE, lambda e: e.memset(onesb[:], 1.0), writes=[bconst])
        P.op(DVE, lambda e: e.memset(s5S[:], 0.0), writes=bs5S)
        P.op(ACT, lambda e: e.activation(out=lbv[:], in_=vec[:, V_LB:V_LB + 4, :], func=AF.Exp), reads=[bconst], writes=[bconst])
        P.op(DVE, lambda e: e.tensor_tensor(out=oml[:, 0, :], in0=lbv[:, 0, :], in1=lbv[:, 1, :], op=ALU.add), reads=[bconst], writes=[bconst])
        P.op(DVE, lambda e: e.tensor_tensor(out=oml[:, 1, :], in0=lbv[:, 2, :], in1=lbv[:, 3, :], op=ALU.add), reads=[bconst], writes=[bconst])
        P.op(DVE, lambda e: e.tensor_tensor(out=oml[:, 0, :], in0=oml[:, 0, :], in1=oml[:, 1, :], op=ALU.add), reads=[bconst], writes=[bconst])
        P.op(DVE, lambda e: e.reciprocal(out=oml[:, 0, :], in_=oml[:, 0, :]), reads=[bconst], writes=[bconst])
        for l in range(4):
            P.op(DVE, lambda e, l=l: e.tensor_tensor(out=lbv[:, l, :], in0=lbv[:, l, :], in1=oml[:, 0, :], op=ALU.mult), reads=[bconst], writes=[bconst])
        P.op(DVE, lambda e: e.memset(lbv[:, 0, :], 0.0), writes=[bconst])
        P.op(DVE, lambda e: e.tensor_tensor(out=lbv[:, 2, :], in0=lbv[:, 2, :], in1=lbv[:, 1, :], op=ALU.add), reads=[bconst], writes=[bconst])
        P.op(DVE, lambda e: e.tensor_tensor(out=lbv[:, 3, :], in0=lbv[:, 3, :], in1=lbv[:, 2, :], op=ALU.add), reads=[bconst], writes=[bconst])
        P.op(DVE, lambda e: e.tensor_scalar(out=oml[:], in0=lbv[:], scalar1=-1.0, scalar2=1.0, op0=ALU.mult, op1=ALU.add), reads=[bconst], writes=[bconst])

        def wload(src2d, kcn, ncols):
            i = wctr[0] % NSLOT
            wctr[0] += 1
            assert kcn * ncols <= WSL
            view = wslots[i][:, 0:kcn * ncols].rearrange("p (k n) -> p k n", k=kcn)
            P.dma(POOL, view, src2d.rearrange("(k p) n -> p k n", p=128), writes=[wbufs[i]])
            return view, wbufs[i]

        def proj_fm(W2d, kcn, ncols_total, src, srcbuf, T, evac, colblk=256):
            cb = min(colblk, WSL // kcn // 128 * 128)
            j = 0
            for c0 in range(0, ncols_total, cb):
                cw = min(cb, ncols_total - c0)
                wt, wb = wload(W2d[:, c0:c0 + cw], kcn, cw)
                for jj in range(cw // 128):
                    ps, pb = next_ps()
                    for kc in range(kcn):
                        P.op(PE, lambda e, ps=ps, wt=wt, kc=kc, jj=jj: e.matmul(ps[:, 0:T], lhsT=wt[:, kc, jj * 128:(jj + 1) * 128], rhs=src[:, kc, 0:T], start=(kc == 0), stop=(kc == kcn - 1)),
                             reads=[wb, srcbuf], writes=[pb])
                    evac(j, ps, pb)
                    j += 1

        def rmsnorm(T, vslot, out_bf, out_buf, src=None, src_buf=None, out_f32=None, out_f32_buf=None):
            src = x if src is None else src
            src_buf = bx if src_buf is None else src_buf
            P.op(ACT, lambda e: e.activation(out=out_bf[:, :, 0:T], in_=src[:, :, 0:T], func=AF.Square), reads=[src_buf], writes=[out_buf])
            ps, pb = psb[6], bps[6]
            for kc in range(KC):
                P.op(PE, lambda e, kc=kc: e.matmul(ps[:, 0:T], lhsT=onesb[:], rhs=out_bf[:, kc, 0:T], start=(kc == 0), stop=(kc == KC - 1)), reads=[out_buf, bconst], writes=[pb])
            P.op(DVE, lambda e: e.tensor_scalar(out=rstd[:, 0:T], in0=ps[:, 0:T], scalar1=1.0 / D, scalar2=EPS, op0=ALU.mult, op1=ALU.add), reads=[pb], writes=[brstd])
            P.op(ACT, lambda e: e.activation(out=rstd[:, 0:T], in_=rstd[:, 0:T], func=AF.Sqrt), reads=[brstd], writes=[brstd])
            P.op(DVE, lambda e: e.reciprocal(out=rstd[:, 0:T], in_=rstd[:, 0:T]), reads=[brstd], writes=[brstd])
            for kc in range(KC):
                if out_f32 is not None:
                    P.op(DVE, lambda e, kc=kc: e.scalar_tensor_tensor(out=out_f32[:, kc, 0:T], in0=src[:, kc, 0:T], scalar=vec[:, vslot, kc:kc + 1], in1=rstd[:, 0:T], op0=ALU.mult, op1=ALU.mult),
                         reads=[src_buf, brstd, bconst], writes=[out_f32_buf])
                    P.op(POOL, lambda e, kc=kc: e.tensor_copy(out=out_bf[:, kc, 0:T], in_=out_f32[:, kc, 0:T]), reads=[out_f32_buf], writes=[out_buf])
                else:
                    P.op(DVE, lambda e, kc=kc: e.scalar_tensor_tensor(out=out_bf[:, kc, 0:T], in0=src[:, kc, 0:T], scalar=vec[:, vslot, kc:kc + 1], in1=rstd[:, 0:T], op0=ALU.mult, op1=ALU.mult),
                         reads=[src_buf, brstd, bconst], writes=[out_buf])

        def add_to_x(T):
            def ev(j, ps, pb):
                P.op(DVE, lambda e, j=j, ps=ps: e.tensor_tensor(out=x[:, j, 0:T], in0=x[:, j, 0:T], in1=ps[:, 0:T], op=ALU.add), reads=[pb, bx], writes=[bx])
            return ev

        def ffn(layer, T):
            phase()
            rmsnorm(T, V_FFN + layer, xn, bxn)
            hid = AHa.take(44 * TP).rearrange("p (k t) -> p k t", k=44)
            bhid = P.buf("hid")
            sg = AFa.take(2 * TP).rearrange("p (a t) -> p a t", a=2)
            bsg = [P.buf("sg0"), P.buf("sg1")]
            W = ffn_w_in[layer]
            for j in range(44):
                wt, wb = wload(W[:, j * 128:(j + 1) * 128], KC, 128)
                wt2, wb2 = wload(W[:, DFF + j * 128:DFF + (j + 1) * 128], KC, 128)
                psg, pbg = next_ps()
                psu, pbu = next_ps()
                for kc in range(KC):
                    P.op(PE, lambda e, kc=kc, wt=wt, psg=psg: e.matmul(psg[:, 0:T], lhsT=wt[:, kc, :], rhs=xn[:, kc, 0:T], start=(kc == 0), stop=(kc == KC - 1)), reads=[wb, bxn], writes=[pbg])
                for kc in range(KC):
                    P.op(PE, lambda e, kc=kc, wt2=wt2, psu=psu: e.matmul(psu[:, 0:T], lhsT=wt2[:, kc, :], rhs=xn[:, kc, 0:T], start=(kc == 0), stop=(kc == KC - 1)), reads=[wb2, bxn], writes=[pbu])
                s = j % 2
                P.op(ACT, lambda e, s=s, psg=psg: e.activation(out=sg[:, s, 0:T], in_=psg[:, 0:T], func=AF.Silu), reads=[pbg], writes=[bsg[s]])
                P.op(DVE, lambda e, s=s, psu=psu, j=j: e.tensor_tensor(out=hid[:, j, 0:T], in0=sg[:, s, 0:T], in1=psu[:, 0:T], op=ALU.mult), reads=[bsg[s], pbu], writes=[bhid])
            Wo = ffn_w_out[layer]
            for j in range(KC):
                wa, wab = wload(Wo[0:2816, j * 128:(j + 1) * 128], 22, 128)
                wb_, wbb = wload(Wo[2816:5632, j * 128:(j + 1) * 128], 22, 128)
                ps, pb = next_ps()
                for kc in range(44):
                    wt_, wtb_ = (wa, wab) if kc < 22 else (wb_, wbb)
                    P.op(PE, lambda e, kc=kc, wt_=wt_, ps=ps: e.matmul(ps[:, 0:T], lhsT=wt_[:, kc % 22, :], rhs=hid[:, kc, 0:T], start=(kc == 0), stop=(kc == 43)), reads=[wtb_, bhid], writes=[pb])
                add_to_x(T)(j, ps, pb)

        def transpose_blocks(dst_fn, src_fn, nblk, rbuf, wbuf, dt_bf=False, rows=128):
            idn = identb if dt_bf else ident
            for g0 in range(0, nblk, 4):
                n = min(4, nblk - g0)
                pst, pbt = (psb[7], bps[7])
                pv = pst[:].bitcast(BF16) if dt_bf else pst[:]
                for i in range(n):
                    P.op(PE, lambda e, i=i, g0=g0, pv=pv: e.transpose(pv[:, i * 128:i * 128 + rows], src_fn(g0 + i), idn[0:rows, 0:rows]), reads=[rbuf, bconst], writes=[pbt])
                for i in range(n):
                    P.op(ACT, lambda e, i=i, g0=g0, pv=pv: e.activation(out=dst_fn(g0 + i), in_=pv[:, i * 128:i * 128 + rows], func=AF.Copy), reads=[pbt], writes=[wbuf])

        def xattn(layer, T, sample, write_kv):
            phase()
            rmsnorm(T, V_XA + layer, xn, bxn)
            q = AHa.take(KC * T).rearrange("p (k t) -> p k t", k=KC)
            bq = P.buf("q")

            def evq(j, ps, pb):
                P.op(ACT, lambda e, j=j, ps=ps: e.activation(out=q[:, j, :], in_=ps[:, 0:T], func=AF.Copy), reads=[pb], writes=[bq])
            proj_fm(x_w_q[layer], KC, D, xn, bxn, T, evq)
            kf = AHa.take(KC * NMEM).rearrange("p (k m) -> p k m", k=KC)
            bkf = P.buf("kf")
            vt = AHa.take(2 * D).rearrange("p (a d) -> p a d", a=2)
            bvt = P.buf("vt")
            pbf = AHa.take(4 * NMEM).rearrange("p (h m) -> p h m", h=4)
            bpbf = P.buf("pbf")
            pT = AHa.take(4 * 2 * 128).rearrange("p (h a l) -> p h a l", h=4, a=2)
            bpT = P.buf("pT")
            stat = AFa.take(16).rearrange("p (a b) -> p a b", a=4)
            bstat = P.buf("stat")
            scale = 1.0 / math.sqrt(512.0)

            def attend(l0, L):
                ps, pb = next_ps()
                ps2, pb2 = next_ps()
                for h in range(4):
                    pss = (ps if h < 2 else ps2)[0:L, (h % 2) * 256:(h % 2) * 256 + 256]
                    pbb = pb if h < 2 else pb2
                    for kk in range(4):
                        P.op(PE, lambda e, pss=pss, h=h, kk=kk: e.matmul(pss, lhsT=q[:, h * 4 + kk, l0:l0 + L], rhs=kf[:, h * 4 + kk, :], start=(kk == 0), stop=(kk == 3)), reads=[bq, bkf], writes=[pbb])
                for h in range(4):
                    pss = (ps if h < 2 else ps2)[0:L, (h % 2) * 256:(h % 2) * 256 + 256]
                    pbb = pb if h < 2 else pb2
                    P.op(DVE, lambda e, pss=pss, h=h: e.reduce_max(out=stat[0:L, h, 0:1], in_=pss, axis=mybir.AxisListType.X), reads=[pbb], writes=[bstat])
                    P.op(DVE, lambda e, h=h: e.tensor_scalar(out=stat[0:L, h, 1:2], in0=stat[0:L, h, 0:1], scalar1=-scale, scalar2=None, op0=ALU.mult), reads=[bstat], writes=[bstat])
                    P.op(ACT, lambda e, pss=pss, h=h: e.activation(out=pbf[0:L, h, :], in_=pss, func=AF.Exp, bias=stat[0:L, h, 1:2], scale=scale, accum_out=stat[0:L, h, 2:3]), reads=[pbb, bstat], writes=[bpbf, bstat])
                    P.op(DVE, lambda e, h=h: e.reciprocal(out=stat[0:L, h, 3:4], in_=stat[0:L, h, 2:3]), reads=[bstat], writes=[bstat])
                    P.op(DVE, lambda e, h=h: e.tensor_scalar(out=pbf[0:L, h, :], in0=pbf[0:L, h, :], scalar1=stat[0:L, h, 3:4], scalar2=None, op0=ALU.mult), reads=[bstat, bpbf], writes=[bpbf])
                pst, pbt = psb[7], bps[7]
                pv = pst[:].bitcast(BF16)
                for h in range(4):
                    for a in range(2):
                        P.op(PE, lambda e, h=h, a=a: e.transpose(pv[:, (h * 2 + a) * 128:(h * 2 + a) * 128 + L], pbf[0:L, h, a * 128:(a + 1) * 128], identb[0:L, 0:L]), reads=[bpbf, bconst], writes=[pbt])
                P.op(ACT, lambda e: e.activation(out=pT[:, :, :, 0:L], in_=pv[:, 0:1024].rearrange("p (h a l) -> p h a l", h=4, a=2)[:, :, :, 0:L], func=AF.Copy), reads=[pbt], writes=[bpT])
                for cg in range(4):
                    po, pob = next_ps()
                    for ci in range(4):
                        c = cg * 4 + ci
                        for a in range(2):
                            P.op(PE, lambda e, c=c, cg=cg, ci=ci, a=a, po=po: e.matmul(po[:, ci * 128:ci * 128 + L], lhsT=vt[:, a, c * 128:(c + 1) * 128], rhs=pT[:, cg, a, 0:L], start=(a == 0), stop=(a == 1)), reads=[bvt, bpT], writes=[pob])
                    P.op(ACT, lambda e, cg=cg, po=po: e.activation(out=xn[:, cg * 4:cg * 4 + 4, l0:l0 + L], in_=po[:, 0:512].rearrange("p (c l) -> p c l", c=4)[:, :, 0:L], func=AF.Copy), reads=[pob], writes=[bxn])

            if not sample:
                mt = AFa.take(2 * D).rearrange("p (a d) -> p a d", a=2)
                bmt = P.buf("mt")
                P.dma(SP, mt, memp.rearrange("(a p) d -> p a d", p=128), writes=[bmt])
                mst = AFa.take(8).rearrange("p (a b) -> p a b", a=2)
                bmst = P.buf("mst")
                msq = AFa.take(D)
                bmsq = P.buf("msq")
                mnb = AFa.take(2 * D).rearrange("p (a d) -> p a d", a=2)
                bmnb = P.buf("mnb")
                gam = AFa.take(D)
                bgam = P.buf("gam")
                dg = AFa.take(128)
                bdg = P.buf("dg")
                for kc in range(KC):
                    P.op(DVE, lambda e, kc=kc: e.tensor_scalar(out=dg, in0=ident[:], scalar1=vec[:, V_MEM + layer, kc:kc + 1], scalar2=None, op0=ALU.mult), reads=[bconst], writes=[bdg])
                    pg, pbg = next_ps()
                    P.op(PE, lambda e, pg=pg: e.matmul(pg[:, 0:128], lhsT=onesf[:], rhs=dg, start=True, stop=True), reads=[bdg, bconst], writes=[pbg])
                    P.op(ACT, lambda e, kc=kc, pg=pg: e.activation(out=gam[:, kc * 128:(kc + 1) * 128], in_=pg[:, 0:128], func=AF.Copy), reads=[pbg], writes=[bgam])
                for a in range(2):
                    P.op(ACT, lambda e, a=a: e.activation(out=msq, in_=mt[:, a, :], func=AF.Square, accum_out=mst[:, a, 0:1]), reads=[bmt], writes=[bmsq, bmst])
                    P.op(DVE, lambda e, a=a: e.tensor_scalar(out=mst[:, a, 1:2], in0=mst[:, a, 0:1], scalar1=1.0 / D, scalar2=EPS, op0=ALU.mult, op1=ALU.add), reads=[bmst], writes=[bmst])
                    P.op(ACT, lambda e, a=a: e.activation(out=mst[:, a, 1:2], in_=mst[:, a, 1:2], func=AF.Sqrt), reads=[bmst], writes=[bmst])
                    P.op(DVE, lambda e, a=a: e.reciprocal(out=mst[:, a, 2:3], in_=mst[:, a, 1:2]), reads=[bmst], writes=[bmst])
                    P.op(DVE, lambda e, a=a: e.scalar_tensor_tensor(out=mnb[:, a, :], in0=mt[:, a, :], scalar=mst[:, a, 2:3], in1=gam, op0=ALU.mult, op1=ALU.mult), reads=[bmt, bmst, bgam], writes=[bmnb])
                mnf = AHa.take(KC * NMEM).rearrange("p (k m) -> p k m", k=KC)
                bmnf = P.buf("mnf")
                transpose_blocks(lambda i: mnf[:, i // 2, (i % 2) * 128:(i % 2) * 128 + 128], lambda i: mnb[:, i % 2, (i // 2) * 128:(i // 2) * 128 + 128], 32, bmnb, bmnf, dt_bf=False)
                def evk(j, ps, pb):
                    P.op(ACT, lambda e, j=j, ps=ps: e.activation(out=kf[:, j, :], in_=ps[:, 0:NMEM], func=AF.Copy), reads=[pb], writes=[bkf])
                proj_fm(x_w_k[layer], KC, D, mnf, bmnf, NMEM, evk)
                ost = AFa.take(D)
                bost = P.buf("ost")
                for which, W, dst in (("k", x_w_k[layer], o_mk), ("v", x_w_v[layer], o_mv)):
                    if which == "k" and not write_kv:
                        continue
                    for c0 in range(0, D, 256):
                        wt, wb = wload(W[:, c0:c0 + 256], KC, 256)
                        for a in range(2):
                            ps, pb = next_ps()
                            for kc in range(KC):
                                P.op(PE, lambda e, ps=ps, wt=wt, kc=kc, a=a: e.matmul(ps[:, 0:256], lhsT=mnf[:, kc, a * 128:(a + 1) * 128], rhs=wt[:, kc, :], start=(kc == 0), stop=(kc == KC - 1)), reads=[wb, bmnf], writes=[pb])
                            if which == "v":
                                P.op(ACT, lambda e, ps=ps, a=a, c0=c0: e.activation(out=vt[:, a, c0:c0 + 256], in_=ps[:, 0:256], func=AF.Copy), reads=[pb], writes=[bvt])
                            if write_kv:
                                P.op(DVE, lambda e, ps=ps, a=a, c0=c0: e.tensor_copy(out=ost[:, 0:256], in_=ps[:, 0:256]), reads=[pb], writes=[bost])
                                P.dma(SP, dst[layer, a * 128:(a + 1) * 128, c0:c0 + 256], ost[:, 0:256], reads=[bost])
                for l0 in range(0, T, 128):
                    attend(l0, 128)
            else:
                kst = AHa.take(2 * D).rearrange("p (a d) -> p a d", a=2)
                bkst = P.buf("kst")
                for b in range(SPC):
                    P.dma(POOL, kst, ck[layer, b].rearrange("(a p) d -> p a d", p=128), writes=[bkst])
                    P.dma(POOL, vt, cv[layer, b].rearrange("(a p) d -> p a d", p=128), writes=[bvt])
                    transpose_blocks(lambda i: kf[:, i // 2, (i % 2) * 128:(i % 2) * 128 + 128], lambda i: kst[:, i % 2, (i // 2) * 128:(i // 2) * 128 + 128], 32, bkst, bkf, dt_bf=True)
                    attend(b * 8, 8)
            proj_fm(x_w_o[layer], KC, D, xn, bxn, T, add_to_x(T))

        onesf = P.sb("onesf", [128, 128], F32)
        P.op(DVE, lambda e: e.memset(onesf[:], 1.0), writes=[bconst])

        bbar_d = nc.dram_tensor("bbar_scr", [2, 128, 2, 8192], BF16, kind="Internal").ap()
        cpad_d = nc.dram_tensor("cpad_scr", [2, 128, 2, 8192], BF16, kind="Internal").ap()
        hgscr = nc.dram_tensor("hg_scr", [2, 128, 16, 128], F32, kind="Internal").ap()
        bbbar = [P.buf("bbar0"), P.buf("bbar1")]
        bcpad = [P.buf("cpad0"), P.buf("cpad1")]
        bhgscr = [P.buf("hgscr0"), P.buf("hgscr1")]
        lamS = P.sb("lamS", [128, 2, 3, 64], F32)
        blamS = P.buf("lamS")
        tint = P.sb("tint", [128, 512], I32)
        btint = P.buf("tint")

        def sin_of(out, th, shift, n, tmp, rb, wb, tb):
            P.op(DVE, lambda e: e.tensor_scalar(out=tint[:, 0:n], in0=th, scalar1=1.0 / TWO_PI, scalar2=shift / TWO_PI, op0=ALU.mult, op1=ALU.add), reads=rb, writes=[btint])
            P.op(DVE, lambda e: e.tensor_copy(out=tmp, in_=tint[:, 0:n]), reads=[btint], writes=[tb])
            P.op(DVE, lambda e: e.scalar_tensor_tensor(out=tmp, in0=tmp, scalar=-TWO_PI, in1=th, op0=ALU.mult, op1=ALU.add), reads=rb + [tb], writes=[tb])
            P.op(DVE, lambda e: e.tensor_scalar(out=tmp, in0=tmp, scalar1=shift, scalar2=-3.14159, op0=ALU.add, op1=ALU.max), reads=[tb], writes=[tb])
            P.op(DVE, lambda e: e.tensor_scalar(out=tmp, in0=tmp, scalar1=3.14159, scalar2=None, op0=ALU.min), reads=[tb], writes=[tb])
            P.op(ACT, lambda e: e.activation(out=out, in_=tmp, func=AF.Sin), reads=[tb], writes=wb)

        def lam_bar(lr, li, ld, n, t, rb, tb):
            dt_, th, mag, tmp = t[2], t[3], t[4], t[5]
            P.op(ACT, lambda e: e.activation(out=dt_, in_=ld, func=AF.Exp), reads=rb, writes=[tb])
            P.op(DVE, lambda e: e.tensor_tensor(out=th, in0=li, in1=dt_, op=ALU.mult), reads=rb + [tb], writes=[tb])
            P.op(DVE, lambda e: e.tensor_tensor(out=mag, in0=lr, in1=dt_, op=ALU.mult), reads=rb + [tb], writes=[tb])
            P.op(ACT, lambda e: e.activation(out=mag, in_=mag, func=AF.Exp), reads=[tb], writes=[tb])
            sin_of(t[1], th, 0.0, n, tmp, [tb], [tb], tb)
            sin_of(t[0], th, math.pi / 2.0, n, tmp, [tb], [tb], tb)
            P.op(DVE, lambda e: e.tensor_tensor(out=t[0], in0=t[0], in1=mag, op=ALU.mult), reads=[tb], writes=[tb])
            P.op(DVE, lambda e: e.tensor_tensor(out=t[1], in0=t[1], in1=mag, op=ALU.mult), reads=[tb], writes=[tb])

        def s5_setup():
            for j in range(2):
                phase()
                ls = AFa.take(3 * 64).rearrange("p (a n) -> p a n", a=3)
                bls = P.buf("ls")
                P.dma(SP, ls, lam_sl[j], writes=[bls])
                t = [AFa.take(64) for _ in range(6)]
                bt = P.buf("t")
                lam_bar(ls[:, 0, :], ls[:, 1, :], ls[:, 2, :], 64, t, [bls], bt)
                P.op(DVE, lambda e, j=j, t=t: e.tensor_copy(out=lamS[:, j, 0, :], in_=t[0]), reads=[bt], writes=[blamS])
                P.op(DVE, lambda e, j=j, t=t: e.tensor_copy(out=lamS[:, j, 1, :], in_=t[1]), reads=[bt], writes=[blamS])
                P.op(DVE, lambda e, j=j, t=t: e.tensor_scalar(out=lamS[:, j, 2, :], in0=t[1], scalar1=-1.0, scalar2=None, op0=ALU.mult), reads=[bt], writes=[blamS])
                NB = 512
                lp = AFa.take(3 * NB).rearrange("p (a n) -> p a n", a=3)
                blp = P.buf("lp")
                bp = AFa.take(2 * NB).rearrange("p (a n) -> p a n", a=2)
                bbp = P.buf("bp")
                tt = [AFa.take(NB) for _ in range(10)]
                btt = P.buf("tt")
                ob = AHa.take(2 * NB).rearrange("p (a n) -> p a n", a=2)
                bob = P.buf("ob")
                for blk in range(16):
                    cs = slice(blk * NB, (blk + 1) * NB)
                    P.dma(SP, lp, lam_pl[j][:, :, cs], writes=[blp])
                    P.dma(SP, bp, b_pl[j][:, :, cs], writes=[bbp])
                    lr, li = lp[:, 0, :], lp[:, 1, :]
                    lam_bar(lr, li, lp[:, 2, :], NB, tt, [blp], btt)
                    mr, mi, a_, b_, den, kr, ki = tt[0], tt[1], tt[6], tt[7], tt[8], tt[2], tt[3]
                    P.op(DVE, lambda e: e.tensor_scalar(out=mr, in0=mr, scalar1=-1.0, scalar2=None, op0=ALU.add), reads=[btt], writes=[btt])
                    P.op(DVE, lambda e: e.tensor_tensor(out=a_, in0=mr, in1=lr, op=ALU.mult), reads=[btt, blp], writes=[btt])
                    P.op(DVE, lambda e: e.tensor_tensor(out=b_, in0=mi, in1=li, op=ALU.mult), reads=[btt, blp], writes=[btt])
                    P.op(DVE, lambda e: e.tensor_tensor(out=kr, in0=a_, in1=b_, op=ALU.add), reads=[btt], writes=[btt])
                    P.op(DVE, lambda e: e.tensor_tensor(out=a_, in0=mi, in1=lr, op=ALU.mult), reads=[btt, blp], writes=[btt])
                    P.op(DVE, lambda e: e.tensor_tensor(out=b_, in0=mr, in1=li, op=ALU.mult), reads=[btt, blp], writes=[btt])
                    P.op(DVE, lambda e: e.tensor_tensor(out=ki, in0=a_, in1=b_, op=ALU.subtract), reads=[btt], writes=[btt])
                    P.op(DVE, lambda e: e.tensor_tensor(out=a_, in0=lr, in1=lr, op=ALU.mult), reads=[blp], writes=[btt])
                    P.op(DVE, lambda e: e.tensor_tensor(out=b_, in0=li, in1=li, op=ALU.mult), reads=[blp], writes=[btt])
                    P.op(DVE, lambda e: e.tensor_tensor(out=den, in0=a_, in1=b_, op=ALU.add), reads=[btt], writes=[btt])
                    P.op(DVE, lambda e: e.reciprocal(out=den, in_=den), reads=[btt], writes=[btt])
                    P.op(DVE, lambda e: e.tensor_tensor(out=kr, in0=kr, in1=den, op=ALU.mult), reads=[btt], writes=[btt])
                    P.op(DVE, lambda e: e.tensor_tensor(out=ki, in0=ki, in1=den, op=ALU.mult), reads=[btt], writes=[btt])
                    P.op(DVE, lambda e: e.tensor_tensor(out=a_, in0=kr, in1=bp[:, 0, :], op=ALU.mult), reads=[btt, bbp], writes=[btt])
                    P.op(DVE, lambda e: e.tensor_tensor(out=b_, in0=ki, in1=bp[:, 1, :], op=ALU.mult), reads=[btt, bbp], writes=[btt])
                    P.op(DVE, lambda e: e.tensor_tensor(out=ob[:, 0, :], in0=a_, in1=b_, op=ALU.subtract), reads=[btt], writes=[bob])
                    P.op(DVE, lambda e: e.tensor_tensor(out=a_, in0=kr, in1=bp[:, 1, :], op=ALU.mult), reads=[btt, bbp], writes=[btt])
                    P.op(DVE, lambda e: e.tensor_tensor(out=b_, in0=ki, in1=bp[:, 0, :], op=ALU.mult), reads=[btt, bbp], writes=[btt])
                    P.op(DVE, lambda e: e.tensor_tensor(out=ob[:, 1, :], in0=a_, in1=b_, op=ALU.add), reads=[btt], writes=[bob])
                    P.dma(SP, bbar_d[j][:, :, cs], ob, reads=[bob], writes=[bbbar[j]])
                cpf = AFa.take(2 * 2048).rearrange("p (a n) -> p a n", a=2)
                bcpf = P.buf("cpf")
                cpb = AHa.take(2 * 2048).rearrange("p (a n) -> p a n", a=2)
                bcpb = P.buf("cpb")
                for blk in range(4):
                    cs = slice(blk * 2048, (blk + 1) * 2048)
                    P.dma(SP, cpf, c_pl[j][:, :, cs], writes=[bcpf])
                    P.op(ACT, lambda e: e.activation(out=cpb[:, 0, :], in_=cpf[:, 0, :], func=AF.Copy), reads=[bcpf], writes=[bcpb])
                    P.op(DVE, lambda e: e.tensor_scalar(out=cpb[:, 1, :], in0=cpf[:, 1, :], scalar1=-1.0, scalar2=None, op0=ALU.mult), reads=[bcpf], writes=[bcpb])
                    P.dma(SP, cpad_d[j][:, :, cs], cpb, reads=[bcpb], writes=[bcpad[j]])

        if CFG["setup"]:
            s5_setup()

        def s5_mixer(j, T, sample, last):
            layer = 2 * j
            phase()
            TB = 32
            u = AFa.take(KC * T).rearrange("p (k t) -> p k t", k=KC)
            bu = P.buf("u")
            rmsnorm(T, V_MIX + layer, xn, bxn, out_f32=u, out_f32_buf=bu)
            Bp = AHa.take(2 * 8192).rearrange("p (a n) -> p a n", a=2)
            bBp = P.buf("Bp")
            P.dma(SP, Bp, bbar_d[j], reads=[bbbar[j]], writes=[bBp])
            S = AFa.take(2 * 64 * TB).rearrange("p (r g t) -> p r g t", r=2, g=64)
            bS = P.buf("S")
            Sb = AHa.take(2 * 64 * TB).rearrange("p (r g t) -> p r g t", r=2, g=64)
            bSb = P.buf("Sb")
            nb = 4 if sample else 1
            t1 = AFa.take(2 * 64 * nb).rearrange("p (r g b) -> p r g b", r=2, g=64)
            t2 = AFa.take(2 * 64 * nb).rearrange("p (r g b) -> p r g b", r=2, g=64)
            bt1, bt2 = P.buf("t1"), P.buf("t2")
            LRb = lamS[:, j, 0, :].unsqueeze(1).unsqueeze(3).to_broadcast([128, 2, 64, nb])
            LIb = lamS[:, j, 1, :].unsqueeze(2).to_broadcast([128, 64, nb])
            NLIb = lamS[:, j, 2, :].unsqueeze(2).to_broadcast([128, 64, nb])
            if sample:
                sini = AFa.take(2 * 64 * SPC).rearrange("p (r g b) -> p r g b", r=2, g=64)
                bsini = P.buf("sini")
                P.dma(SP, sini, s5init[j], writes=[bsini])
                sfin = AFa.take(2 * 64 * SPC).rearrange("p (r g b) -> p r g b", r=2, g=64)
                bsfin = P.buf("sfin")
            for kc in range(KC):
                P.op(DVE, lambda e, kc=kc: e.tensor_scalar(out=u[:, kc, 0:T], in0=u[:, kc, 0:T], scalar1=vec[:, V_S5D + j, kc:kc + 1], scalar2=None, op0=ALU.mult), reads=[bu, bconst], writes=[bu])
            for blk in range(T // TB):
                t0 = blk * TB
                for gc in range(KC):
                    ps, pb = next_ps()
                    for ri in range(2):
                        for jj in range(4):
                            P.op(PE, lambda e, ps=ps, ri=ri, jj=jj, gc=gc, t0=t0: e.matmul(ps[:, (ri * 4 + jj) * TB:(ri * 4 + jj + 1) * TB], lhsT=Bp[:, ri, (gc * 4 + jj) * 128:(gc * 4 + jj + 1) * 128], rhs=xn[:, gc, t0:t0 + TB], start=True, stop=True), reads=[bBp, bxn], writes=[pb])
                    P.op(ACT, lambda e, ps=ps, gc=gc: e.activation(out=S[:, :, gc * 4:(gc + 1) * 4, :], in_=ps[:, 0:8 * TB].rearrange("p (r g t) -> p r g t", r=2, g=4), func=AF.Copy), reads=[pb], writes=[bS])
                if CFG["s5lvl"] < 2:
                    continue
                if sample:
                    S5v = S.rearrange("p r g (b t) -> p r g b t", t=8)
                    steps = [(S5v[:, :, :, :, tt], (sini[:, :, :, blk * nb:(blk + 1) * nb] if tt == 0 else S5v[:, :, :, :, tt - 1])) for tt in range(8)]
                else:
                    steps = [(S[:, :, :, tt:tt + 1], (s5S[:, j, :, :].unsqueeze(3) if tt == 0 else S[:, :, :, tt - 1:tt])) for tt in range(TB)]
                for si, (cur, prev) in enumerate(steps):
                    rb = [bS, blamS] + ([bsini] if sample else [bs5S[j]])
                    P.op(DVE, lambda e, prev=prev: e.tensor_tensor(out=t1, in0=prev, in1=LRb, op=ALU.mult), reads=rb, writes=[bt1])
                    P.op(DVE, lambda e, prev=prev: e.tensor_tensor(out=t2[:, 0], in0=prev[:, 1], in1=NLIb, op=ALU.mult), reads=rb, writes=[bt2])
                    P.op(DVE, lambda e, prev=prev: e.tensor_tensor(out=t2[:, 1], in0=prev[:, 0], in1=LIb, op=ALU.mult), reads=rb, writes=[bt2])
                    P.op(DVE, lambda e, cur=cur: e.tensor_tensor(out=cur, in0=cur, in1=t1, op=ALU.add), reads=[bS, bt1], writes=[bS])
                    P.op(DVE, lambda e, cur=cur: e.tensor_tensor(out=cur, in0=cur, in1=t2, op=ALU.add), reads=[bS, bt2], writes=[bS])
                if sample:
                    P.op(DVE, lambda e, blk=blk, S5v=S5v: e.tensor_copy(out=sfin[:, :, :, blk * nb:(blk + 1) * nb], in_=S5v[:, :, :, :, 7]), reads=[bS], writes=[bsfin])
                else:
                    P.op(DVE, lambda e: e.tensor_copy(out=s5S[:, j, :, :], in_=S[:, :, :, TB - 1]), reads=[bS], writes=[bs5S[j]])
                if CFG["s5lvl"] < 3:
                    continue
                P.op(POOL, lambda e: e.tensor_copy(out=Sb, in_=S), reads=[bS], writes=[bSb])
                if CFG["s5lvl"] < 4:
                    continue
                psy, pby = next_ps()
                for gq in range(4):
                    i = wctr[0] % NSLOT
                    wctr[0] += 1
                    Cw = wslots[i][:, 0:4096].rearrange("p (a n) -> p a n", a=2)
                    P.dma(SP, Cw, cpad_d[j][:, :, gq * 2048:(gq + 1) * 2048], reads=[bcpad[j]], writes=[wbufs[i]])
                    for g4 in range(4):
                        gc = gq * 4 + g4
                        n = 0
                        for ri in range(2):
                            for jj in range(4):
                                P.op(PE, lambda e, Cw=Cw, ri=ri, jj=jj, g4=g4, gc=gc, n=n, psy=psy: e.matmul(psy[:, gc * TB:(gc + 1) * TB], lhsT=Cw[:, ri, (g4 * 4 + jj) * 128:(g4 * 4 + jj + 1) * 128], rhs=Sb[:, ri, gc * 4 + jj, :], start=(n == 0), stop=(n == 7)), reads=[wbufs[i], bSb], writes=[pby])
                                n += 1
                if CFG["s5lvl"] < 5:
                    continue
                P.op(DVE, lambda e, t0=t0, psy=psy: e.tensor_tensor(out=u[:, :, t0:t0 + TB], in0=u[:, :, t0:t0 + TB], in1=psy[:, 0:KC * TB].rearrange("p (k t) -> p k t", k=KC), op=ALU.add), reads=[bu, pby], writes=[bu])
            if sample:
                P.dma(SP, o_s5s[j], sfin, reads=[bsfin])
            elif last:
                P.dma(SP, o_s5p[j], s5S[:, j, :, :], reads=[bs5S[j]])
            if CFG["s5lvl"] < 6:
                return
            g1 = AFa.take(TP)
            g2 = AFa.take(TP)
            bg1, bg2 = P.buf("g1"), P.buf("g2")
            for kc in range(KC):
                v = u[:, kc, 0:T]
                P.op(ACT, lambda e, v=v: e.activation(out=g1[:, 0:T], in_=v, func=AF.Square), reads=[bu], writes=[bg1])
                P.op(DVE, lambda e: e.tensor_scalar(out=g1[:, 0:T], in0=g1[:, 0:T], scalar1=0.044715, scalar2=1.0, op0=ALU.mult, op1=ALU.add), reads=[bg1], writes=[bg1])
                P.op(DVE, lambda e, v=v: e.tensor_tensor(out=g1[:, 0:T], in0=g1[:, 0:T], in1=v, op=ALU.mult), reads=[bg1, bu], writes=[bg1])
                P.op(ACT, lambda e: e.activation(out=g2[:, 0:T], in_=g1[:, 0:T], func=AF.Sigmoid, scale=1.5957691216057308), reads=[bg1], writes=[bg2])
                P.op(DVE, lambda e, v=v: e.tensor_tensor(out=v, in0=v, in1=g2[:, 0:T], op=ALU.mult), reads=[bg2, bu], writes=[bu])
                P.op(POOL, lambda e, v=v, kc=kc: e.tensor_copy(out=xn[:, kc, 0:T], in_=v), reads=[bu], writes=[bxn])

            def evg(jc, ps, pb):
                P.op(ACT, lambda e, ps=ps: e.activation(out=g2[:, 0:T], in_=ps[:, 0:T], func=AF.Sigmoid), reads=[pb], writes=[bg2])
                P.op(DVE, lambda e, jc=jc: e.tensor_tensor(out=g2[:, 0:T], in0=g2[:, 0:T], in1=u[:, jc, 0:T], op=ALU.mult), reads=[bg2, bu], writes=[bg2])
                P.op(DVE, lambda e, jc=jc: e.tensor_tensor(out=x[:, jc, 0:T], in0=x[:, jc, 0:T], in1=g2[:, 0:T], op=ALU.add), reads=[bg2, bx], writes=[bx])
            proj_fm(w_glu[j], KC, D, xn, bxn, T, evg)

        def hgrn_mixer(j, layer, T, sample, last, first):
            phase()
            rmsnorm(T, V_MIX + layer, xn, bxn)
            CS = 8 if sample else 64
            nch = T // CS
            W = hg_w_in[j]
            Sst = AFa.take(16 * 128).rearrange("p (h v) -> p h v", h=16)
            bSst = P.buf("Sst")
            if not sample:
                if first:
                    P.op(DVE, lambda e: e.memset(Sst, 0.0), writes=[bSst])
                else:
                    P.dma(SP, Sst, hgscr[j], reads=[bhgscr[j]], writes=[bSst])
            sgm = AFa.take(4 * TP).rearrange("p (h t) -> p h t", h=4)
            lgf = AFa.take(4 * TP).rearrange("p (h t) -> p h t", h=4)
            eg = AFa.take(4 * TP).rearrange("p (h t) -> p h t", h=4)
            ob = AFa.take(4 * TP).rearrange("p (h t) -> p h t", h=4)
            gs = AFa.take(4 * TP).rearrange("p (h t) -> p h t", h=4)
            qs = AFa.take(2 * TP).rearrange("p (h t) -> p h t", h=2)
            rr = AFa.take(TP)
            Sld = AFa.take(2 * 128).rearrange("p (a v) -> p a v", a=2)
            stmp = AFa.take(128)
            qt = AHa.take(4 * TP).rearrange("p (h t) -> p h t", h=4)
            kt = AHa.take(4 * TP).rearrange("p (h t) -> p h t", h=4)
            og = AHa.take(4 * TP).rearrange("p (h t) -> p h t", h=4)
            osq = AHa.take(4 * TP).rearrange("p (h t) -> p h t", h=4)
            vtok = AHa.take(nch * 512).rearrange("p (c v) -> p c v", c=nch)
            kTs = AHa.take(128)
            PT = AHa.take(64)
            Sbf = AHa.take(4 * 128).rearrange("p (h v) -> p h v", h=4)
            for hg in range(4):
                bsgm, blgf, beg, bob, bgs, brr = (P.buf(n) for n in ("sgm", "lgf", "eg", "ob", "gs", "rr"))
                bqs = [P.buf("qs0"), P.buf("qs1")]
                bSld = [P.buf("Sld0"), P.buf("Sld1")]
                bstmp, bqt, bkt, bog, bosq, bvtok, bkTs, bPT, bSbf = (P.buf(n) for n in ("stmp", "qt", "kt", "og", "osq", "vtok", "kTs", "PT", "Sbf"))
                h0 = hg * 4
                def evf(jh, ps, pb):
                    P.op(ACT, lambda e, jh=jh, ps=ps: e.activation(out=sgm[:, jh, 0:T], in_=ps[:, 0:T], func=AF.Sigmoid), reads=[pb], writes=[bsgm])
                    P.op(DVE, lambda e, jh=jh, h0=h0: e.tensor_scalar(out=sgm[:, jh, 0:T], in0=sgm[:, jh, 0:T], scalar1=oml[:, layer, h0 + jh:h0 + jh + 1], scalar2=lbv[:, layer, h0 + jh:h0 + jh + 1], op0=ALU.mult, op1=ALU.add), reads=[bsgm, bconst], writes=[bsgm])
                    P.op(ACT, lambda e, jh=jh: e.activation(out=lgf[:, jh, 0:T], in_=sgm[:, jh, 0:T], func=AF.Ln), reads=[bsgm], writes=[blgf])
                    P.op(DVE, lambda e, jh=jh: e.tensor_scalar(out=sgm[:, jh, 0:T], in0=sgm[:, jh, 0:T], scalar1=-1.0, scalar2=1.0, op0=ALU.mult, op1=ALU.add), reads=[bsgm], writes=[bsgm])
                proj_fm(W[:, D + hg * 512:D + (hg + 1) * 512], KC, 512, xn, bxn, T, evf)
                for jh in range(4):
                    for c in range(nch):
                        P.op(DVE, lambda e, jh=jh, c=c: e.tensor_tensor_scan(out=eg[:, jh, c * CS:(c + 1) * CS], data0=onesf[:, 0:CS], data1=lgf[:, jh, c * CS:(c + 1) * CS], initial=0.0, op0=ALU.mult, op1=ALU.add), reads=[blgf, bconst], writes=[beg])
                P.op(ACT, lambda e: e.activation(out=lgf[:, :, 0:T], in_=eg[:, :, 0:T], func=AF.Exp), reads=[beg], writes=[blgf])
                P.op(ACT, lambda e: e.activation(out=eg[:, :, 0:T], in_=eg[:, :, 0:T], func=AF.Exp, scale=-1.0), reads=[beg, blgf], writes=[beg])
                P.op(DVE, lambda e: e.tensor_tensor(out=kt[:, :, 0:T], in0=sgm[:, :, 0:T], in1=eg[:, :, 0:T], op=ALU.mult), reads=[bsgm, beg], writes=[bkt])
                def evq(jh, ps, pb):
                    s = jh % 2
                    P.op(ACT, lambda e, s=s, ps=ps: e.activation(out=qs[:, s, 0:T], in_=ps[:, 0:T], func=AF.Silu), reads=[pb], writes=[bqs[s]])
                    P.op(DVE, lambda e, s=s, jh=jh: e.tensor_tensor(out=qt[:, jh, 0:T], in0=qs[:, s, 0:T], in1=lgf[:, jh, 0:T], op=ALU.mult), reads=[bqs[s], blgf], writes=[bqt])
                proj_fm(W[:, hg * 512:(hg + 1) * 512], KC, 512, xn, bxn, T, evq)
                for half in range(2):
                    c0 = 2 * D + hg * 512 + half * 256
                    wt, wb = wload(W[:, c0:c0 + 256], KC, 256)
                    for c in range(nch):
                        ps, pb = next_ps()
                        for kc in range(KC):
                            P.op(PE, lambda e, ps=ps, wt=wt, kc=kc, c=c: e.matmul(ps[0:CS, 0:256], lhsT=xn[:, kc, c * CS:(c + 1) * CS], rhs=wt[:, kc, :], start=(kc == 0), stop=(kc == KC - 1)), reads=[wb, bxn], writes=[pb])
                        P.op(ACT, lambda e, ps=ps, c=c, half=half: e.activation(out=vtok[0:CS, c, half * 256:(half + 1) * 256], in_=ps[0:CS, 0:256], func=AF.Copy), reads=[pb], writes=[bvtok])
                def evg(jh, ps, pb):
                    P.op(ACT, lambda e, jh=jh, ps=ps: e.activation(out=gs[:, jh, 0:T], in_=ps[:, 0:T], func=AF.Silu), reads=[pb], writes=[bgs])
                proj_fm(W[:, 3 * D + hg * 512:3 * D + (hg + 1) * 512], KC, 512, xn, bxn, T, evg)
                for jh in range(4):
                    h = h0 + jh
                    if not sample:
                        P.op(ACT, lambda e, jh=jh, h=h: e.activation(out=Sbf[:, jh, :], in_=Sst[:, h, :], func=AF.Copy), reads=[bSst], writes=[bSbf])
                    for c in range(nch):
                        cs = slice(c * CS, (c + 1) * CS)
                        if sample:
                            sl = c % 2
                            P.dma(SP, Sld[:, sl, :], hgst[j, c, h], writes=[bSld[sl]])
                            P.op(ACT, lambda e, jh=jh, sl=sl: e.activation(out=Sbf[:, jh, :], in_=Sld[:, sl, :], func=AF.Copy), reads=[bSld[sl]], writes=[bSbf])
                            Scur, bScur = Sld[:, sl, :], bSld[sl]
                        else:
                            Scur, bScur = Sst[:, h, :], bSst
                        ps1, pb1 = next_ps()
                        P.op(PE, lambda e, ps1=ps1, jh=jh, cs=cs: e.matmul(ps1[0:CS, 0:CS], lhsT=kt[:, jh, cs], rhs=qt[:, jh, cs], start=True, stop=True), reads=[bkt, bqt], writes=[pb1])
                        P.op(DVE, lambda e, ps1=ps1: e.tensor_tensor(out=PT[0:CS, 0:CS], in0=ps1[0:CS, 0:CS], in1=maskT[0:CS, 0:CS], op=ALU.mult), reads=[pb1, bconst], writes=[bPT])
                        pv = psb[7][:].bitcast(BF16)
                        P.op(PE, lambda e, jh=jh, cs=cs, pv=pv: e.transpose(pv[0:CS, 0:128], kt[:, jh, cs], identb[:]), reads=[bkt, bconst], writes=[bps[7]])
                        P.op(ACT, lambda e, pv=pv: e.activation(out=kTs[0:CS, :], in_=pv[0:CS, 0:128], func=AF.Copy), reads=[bps[7]], writes=[bkTs])
                        ps2, pb2 = next_ps()
                        P.op(PE, lambda e, ps2=ps2, c=c, jh=jh: e.matmul(ps2[:, 0:CS], lhsT=vtok[0:CS, c, jh * 128:(jh + 1) * 128], rhs=PT[0:CS, 0:CS], start=True, stop=False), reads=[bvtok, bPT], writes=[pb2])
                        P.op(PE, lambda e, ps2=ps2, cs=cs, jh=jh: e.matmul(ps2[:, 0:CS], lhsT=Sbf[:, jh, :], rhs=qt[:, jh, cs], start=False, stop=True), reads=[bSbf, bqt], writes=[pb2])
                        P.op(ACT, lambda e, ps2=ps2, jh=jh, cs=cs: e.activation(out=ob[:, jh, cs], in_=ps2[:, 0:CS], func=AF.Copy), reads=[pb2], writes=[bob])
                        ps3, pb3 = next_ps()
                        P.op(PE, lambda e, ps3=ps3, c=c, jh=jh: e.matmul(ps3[:, 0:128], lhsT=kTs[0:CS, :], rhs=vtok[0:CS, c, jh * 128:(jh + 1) * 128], start=True, stop=True), reads=[bkTs, bvtok], writes=[pb3])
                        P.op(DVE, lambda e, ps3=ps3, Scur=Scur: e.tensor_tensor(out=stmp, in0=Scur, in1=ps3[:, 0:128], op=ALU.add), reads=[pb3, bScur], writes=[bstmp])
                        P.op(DVE, lambda e, Scur=Scur, jh=jh, c=c: e.tensor_scalar(out=Scur, in0=stmp, scalar1=lgf[:, jh, (c + 1) * CS - 1:(c + 1) * CS], scalar2=None, op0=ALU.mult), reads=[bstmp, blgf], writes=[bScur])
                        if sample:
                            P.dma(SP, o_hgs[j, c, h], Scur, reads=[bScur])
                        else:
                            P.op(ACT, lambda e, jh=jh, Scur=Scur: e.activation(out=Sbf[:, jh, :], in_=Scur, func=AF.Copy), reads=[bScur], writes=[bSbf])
                P.op(ACT, lambda e: e.activation(out=osq[:, :, 0:T], in_=ob[:, :, 0:T], func=AF.Square), reads=[bob], writes=[bosq])
                for jh in range(4):
                    h = h0 + jh
                    ps, pb = next_ps()
                    P.op(PE, lambda e, ps=ps, jh=jh: e.matmul(ps[:, 0:T], lhsT=onesb[:], rhs=osq[:, jh, 0:T], start=True, stop=True), reads=[bosq, bconst], writes=[pb])
                    P.op(DVE, lambda e, ps=ps: e.tensor_scalar(out=rr[:, 0:T], in0=ps[:, 0:T], scalar1=1.0 / 128.0, scalar2=EPS, op0=ALU.mult, op1=ALU.add), reads=[pb], writes=[brr])
                    P.op(ACT, lambda e: e.activation(out=rr[:, 0:T], in_=rr[:, 0:T], func=AF.Sqrt), reads=[brr], writes=[brr])
                    P.op(DVE, lambda e: e.reciprocal(out=rr[:, 0:T], in_=rr[:, 0:T]), reads=[brr], writes=[brr])
                    P.op(DVE, lambda e, jh=jh, h=h: e.scalar_tensor_tensor(out=ob[:, jh, 0:T], in0=ob[:, jh, 0:T], scalar=vec[:, V_GN + j, h:h + 1], in1=rr[:, 0:T], op0=ALU.mult, op1=ALU.mult), reads=[bob, brr, bconst], writes=[bob])
                    P.op(DVE, lambda e, jh=jh: e.tensor_tensor(out=og[:, jh, 0:T], in0=ob[:, jh, 0:T], in1=gs[:, jh, 0:T], op=ALU.mult), reads=[bob, bgs], writes=[bog])
                proj_fm(hg_w_out[j][hg * 512:(hg + 1) * 512, :], 4, D, og, bog, T, add_to_x(T))
            if not sample:
                P.dma(SP, hgscr[j], Sst, reads=[bSst], writes=[bhgscr[j]])
                if last:
                    P.dma(SP, o_hgp[j].rearrange("h k v -> k h v"), Sst, reads=[bSst])

        def run_pass(T, src, dst, sample, first, last):
            phase()
            xt = AFa.take(D)
            bxt = P.buf("xt")
            for tt in range(T // 128):
                P.dma(SP, xt, src[tt * 128:(tt + 1) * 128, :], writes=[bxt])
                transpose_blocks(lambda i, tt=tt: x[:, i, tt * 128:(tt + 1) * 128], lambda i: xt[:, i * 128:(i + 1) * 128], KC, bxt, bx)
            for layer in range(4):
                j = layer // 2
                if layer % 2 == 0:
                    if CFG["s5"]:
                        s5_mixer(j, T, sample, last)
                else:
                    if CFG["hg"]:
                        hgrn_mixer(j, layer, T, sample, last, first)
                if CFG["xa"]:
                    xattn(layer, T, sample, first and not sample)
                if CFG["ffn"]:
                    ffn(layer, T)
            phase()
            yf = AFa.take(KC * TP).rearrange("p (k t) -> p k t", k=KC)
            byf = P.buf("yf")
            rmsnorm(T, V_FIN, xn, bxn, out_f32=yf, out_f32_buf=byf)
            yt = AFa.take(D)
            byt = P.buf("yt")
            for tt in range(T // 128):
                transpose_blocks(lambda i: yt[:, i * 128:(i + 1) * 128], lambda i, tt=tt: yf[:, i, tt * 128:(tt + 1) * 128], KC, byf, byt)
                P.dma(SP, dst[tt * 128:(tt + 1) * 128, :], yt, reads=[byt])

        for p in range(CFG["npass"]):
            run_pass(TP, xp[p * TP:(p + 1) * TP, :], o_yp[p * TP:(p + 1) * TP, :], False, p == 0, p == CFG["npass"] - 1)
        if CFG["sample"]:
            run_pass(128, xsm, o_ys, True, False, False)
        print("sbuf remaining", nc.sbuf_bytes_remaining)
        stats = P.emit()
        print("ops", stats)
    return nc


def _vec_layout(v):
    return np.ascontiguousarray(v.reshape(KC, 128).T)


_PROG = None


def kernel(_return_maps=False, **inp):
    f = np.float32
    g = {k: np.asarray(v) for k, v in inp.items()}
    vec = np.zeros((128, NV, 16), f)
    for l in range(4):
        vec[:, V_MIX + l] = _vec_layout(g["norm_mix"][l])
        vec[:, V_XA + l] = _vec_layout(g["norm_xattn"][l])
        vec[:, V_MEM + l] = _vec_layout(g["norm_mem_in"][l])
        vec[:, V_FFN + l] = _vec_layout(g["norm_ffn"][l])
        vec[:, V_LB + l] = _vec_layout(g["hg_lower_bounds"][l])
    vec[:, V_FIN] = _vec_layout(g["norm_final"])
    for j in range(2):
        vec[:, V_S5D + j] = _vec_layout(g["s5_d"][j])
        vec[:, V_GN + j] = _vec_layout(g["hg_g_norm"][j])
    ident = np.eye(128, dtype=f)
    maskT = np.triu(np.ones((64, 64), f))
    lam_sl = np.zeros((2, 128, 3, 64), f)
    lam_pl = np.zeros((2, 128, 3, 8192), f)
    b_pl = np.zeros((2, 128, 2, 8192), f)
    c_pl = np.zeros((2, 128, 2, 8192), f)
    gidx = np.zeros((16, 4, 2), np.int64)
    for gc in range(16):
        for jj in range(4):
            for gh in range(2):
                gidx[gc, jj, gh] = 8 * gc + 4 * gh + jj
    for j in range(2):
        ld = np.broadcast_to(g["s5_log_dt"][j][:, None], (128, 64))
        for i, src in enumerate((g["s5_lam_re"][j], g["s5_lam_im"][j], ld)):
            a = src[gidx]
            lam_sl[j, :, i, :] = a.transpose(3, 2, 0, 1).reshape(128, 64)
            pl = a.transpose(0, 1, 3, 2).reshape(8192)
            lam_pl[j, :, i, :] = pl[None, :]
        for i, src in enumerate((g["s5_b_re"][j], g["s5_b_im"][j])):
            a = src[gidx]
            t = np.zeros((8, 16, 16, 4, 64, 2), f)
            for jj in range(4):
                for gh in range(2):
                    t[4 * gh + jj, :, :, jj, :, gh] = a[:, jj, gh].transpose(2, 0, 1)
            b_pl[j, :, i, :] = t.reshape(128, 8192)
        for i, src in enumerate((g["s5_c_re"][j], g["s5_c_im"][j])):
            a = src[gidx]
            t = np.zeros((64, 2, 16, 4, 8, 16), f)
            for jj in range(4):
                for gh in range(2):
                    t[:, gh, :, jj, 4 * gh + jj, :] = a[:, jj, gh].transpose(2, 0, 1)
            c_pl[j, :, i, :] = t.reshape(128, 8192)

    def s5_state_layout(re, im):
        out = np.zeros((128, 2, 64, re.shape[0]), f)
        for i, s in enumerate((re, im)):
            a = s[:, gidx]
            out[:, i] = a.transpose(4, 3, 1, 2, 0).reshape(128, 64, re.shape[0])
        return out

    def s5_state_unlayout(arr):
        B = arr.shape[-1]
        res = []
        for i in range(2):
            a = arr[:, i].reshape(64, 2, 16, 4, B)
            o = np.zeros((B, 128, 64), f)
            o[:, gidx] = a.transpose(4, 2, 3, 1, 0)
            res.append(o)
        return res

    in_maps = []
    for c in range(NCORE):
        s = c % 4
        sl = slice(c * SPC, (c + 1) * SPC)
        s5i = np.stack([s5_state_layout(g["state_s5_re"][j, sl], g["state_s5_im"][j, sl]) for j in range(2)])
        in_maps.append(dict(
            xp=np.ascontiguousarray(g["x_prompt"][s]), xsm=np.ascontiguousarray(g["x_sample"][sl].reshape(128, D)),
            memp=np.ascontiguousarray(g["mem_prompt"][s]),
            ck=np.ascontiguousarray(g["cache_mem_k"][:, sl].reshape(4, SPC, NMEM, D)),
            cv=np.ascontiguousarray(g["cache_mem_v"][:, sl].reshape(4, SPC, NMEM, D)),
            hgst=np.ascontiguousarray(g["state_hgrn"][:, sl]),
            vecs=vec, ident=ident, maskT=maskT, lam_sl=lam_sl, lam_pl=lam_pl, b_pl=b_pl, c_pl=c_pl, s5init=s5i,
            s5_w_glu=g["s5_w_glu"], hg_w_in=g["hg_w_in"], hg_w_out=g["hg_w_out"],
            x_w_q=g["x_w_q"], x_w_k=g["x_w_k"], x_w_v=g["x_w_v"], x_w_o=g["x_w_o"],
            ffn_w_in=g["ffn_w_in"], ffn_w_out=g["ffn_w_out"]))
    if _return_maps:
        return in_maps
    global _PROG
    if _PROG is None:
        _PROG = build_program()
    nc = _PROG
    res = run_bass_kernel_spmd(nc, in_maps, core_ids=list(range(NCORE))).results
    y_prompt = np.stack([res[s]["o_yp"] for s in range(4)]).astype(f)
    y_sample = np.concatenate([res[c]["o_ys"].reshape(SPC, 8, D) for c in range(NCORE)]).astype(f)
    s5p = [[s5_state_unlayout(res[s]["o_s5p"][j][..., None]) for s in range(4)] for j in range(2)]
    s5_re_p = np.stack([np.concatenate([s5p[j][s][0] for s in range(4)]) for j in range(2)])
    s5_im_p = np.stack([np.concatenate([s5p[j][s][1] for s in range(4)]) for j in range(2)])
    hg_p = np.stack([np.stack([res[s]["o_hgp"][j] for s in range(4)]) for j in range(2)])
    mk = np.stack([np.stack([res[s]["o_mk"][l].reshape(NMEM, 4, 512) for s in range(4)]) for l in range(4)])
    mv = np.stack([np.stack([res[s]["o_mv"][l].reshape(NMEM, 4, 512) for s in range(4)]) for l in range(4)])
    s5s = [[s5_state_unlayout(res[c]["o_s5s"][j]) for c in range(NCORE)] for j in range(2)]
    s5_re_s = np.stack([np.concatenate([s5s[j][c][0] for c in range(NCORE)]) for j in range(2)])
    s5_im_s = np.stack([np.concatenate([s5s[j][c][1] for c in range(NCORE)]) for j in range(2)])
    hg_s = np.stack([np.concatenate([res[c]["o_hgs"][j] for c in range(NCORE)]) for j in range(2)])
    return (y_prompt, y_sample, s5_re_p.astype(f), s5_im_p.astype(f), hg_p.astype(f), mk.astype(f), mv.astype(f),
            s5_re_s.astype(f), s5_im_s.astype(f), hg_s.astype(f))
```

```python
import math
import numpy as np
from contextlib import ExitStack
import concourse.bass as bass
import concourse.mybir as mybir
from concourse.bass_utils import run_bass_kernel_spmd

F32 = mybir.dt.float32
BF16 = mybir.dt.bfloat16
I32 = mybir.dt.int32
AF = mybir.ActivationFunctionType
ALU = mybir.AluOpType
PE, ACT, DVE, POOL, SP = "tensor", "scalar", "vector", "gpsimd", "sync"

D = 2048
KC = 16
DFF = 5632
NMEM = 256
EPS = 1e-6
NCORE = 8
SPC = 16
TP = 512
NPASS = 4
TWO_PI = 2.0 * math.pi
CFG = dict(s5lvl=9, npass=4, s5=True, hg=True, xa=True, ffn=True, sample=True, setup=True)


class Buf:
    __slots__ = ("name", "last_w", "readers", "sem", "cnt")

    def __init__(self, name):
        self.name = name
        self.last_w = None
        self.readers = []
        self.sem = None
        self.cnt = 0


class Op:
    __slots__ = ("eng", "fn", "deps", "dma", "ev", "signal", "chain")

    def __init__(self, eng, fn, dma):
        self.eng = eng
        self.fn = fn
        self.deps = []
        self.dma = dma
        self.ev = None
        self.signal = False
        self.chain = False


class Prog:
    def __init__(self, nc, stack, n_dma_sems=80):
        self.nc = nc
        self.stack = stack
        self.ops = []
        self.esem = {e: [stack.enter_context(nc.semaphore("es%d_" % k + e)) for k in range(4)] for e in (PE, ACT, DVE, POOL)}
        self.free_sems = []
        for i in range(n_dma_sems):
            try:
                self.free_sems.append([stack.enter_context(nc.semaphore("ds%d" % i)), 0])
            except KeyError:
                break
        print("dma sems", len(self.free_sems))
        self.all_chans = list(self.free_sems)
        self.live = []
        self.pending = {e: [] for e in (PE, ACT, DVE, POOL, SP)}
        self.dma_bufs = []

    def buf(self, name="b"):
        return Buf(name)

    def sb(self, name, shape, dtype):
        return self.stack.enter_context(self.nc.sbuf_tensor(name, list(shape), dtype))

    def ps(self, name, shape, dtype=F32):
        return self.stack.enter_context(self.nc.psum_tensor(name, list(shape), dtype))

    def op(self, eng, fn, reads=(), writes=(), dma=False, chain=False):
        o = Op(eng, fn, dma)
        o.chain = chain
        deps = list(self.pending[eng])
        self.pending[eng] = []
        for r in reads:
            if r.last_w is not None:
                deps.append(r.last_w)
        for w in writes:
            if w.last_w is not None:
                deps.append(w.last_w)
            lastr = {}
            for rd in w.readers:
                if rd.dma:
                    deps.append(rd)
                else:
                    lastr[rd.eng] = rd
            deps.extend(lastr.values())
        for r in reads:
            r.readers.append(o)
        for w in writes:
            w.last_w = o
            w.readers = []
        if dma:
            tgt = (list(writes) + list(reads))[0]
            if tgt.sem is not None and tgt.sem[1] >= 1800:
                tgt.sem = None
                if tgt in self.live:
                    self.live.remove(tgt)
            if tgt.sem is None:
                tgt.sem = self.free_sems.pop()
                tgt.cnt = tgt.sem[1]
                self.live.append(tgt)
            tgt.cnt += 1
            tgt.sem[1] = tgt.cnt
            o.ev = (tgt.sem[0], 16 * tgt.cnt)
        seen = set()
        for d in deps:
            if d is o or id(d) in seen:
                continue
            seen.add(id(d))
            if (not d.dma) and d.eng == PE and eng == PE and not dma:
                continue
            if chain and (not d.dma) and d.eng == eng and d.chain:
                continue
            o.deps.append(d)
            d.signal = True
        self.ops.append(o)
        return o

    def dma(self, eng, out, in_, reads=(), writes=(), **kw):
        return self.op(eng, lambda e: e.dma_start(out=out, in_=in_, **kw), reads=reads, writes=writes, dma=True)

    def barrier(self):
        last = {}
        lastd = {}
        for o in self.ops:
            if o.dma:
                lastd[id(o.ev[0])] = o
            else:
                last[o.eng] = o
        deps = list(last.values()) + list(lastd.values())
        for e in self.pending:
            self.pending[e] = list(deps)
        keep = []
        for b in self.live:
            if getattr(b, "name", "").startswith("keep"):
                keep.append(b)
            else:
                if b.sem[1] < 1700:
                    self.free_sems.append(b.sem)
                b.sem = None
        self.live = keep

    def emit(self):
        nc = self.nc
        cnt = {e: 0 for e in self.esem}
        for o in self.ops:
            if (not o.dma) and o.signal:
                cnt[o.eng] += 1
                ep, v = divmod(cnt[o.eng] - 1, 30000)
                o.ev = (self.esem[o.eng][ep], v + 1)
        print("signal counts", cnt)
        per = {e: [] for e in (PE, ACT, DVE, POOL, SP)}
        for o in self.ops:
            per[o.eng].append(o)
        finals = [(c[0], 16 * c[1]) for c in self.all_chans if c[1] > 0]

        def run(engname, eng):
            waited = {}
            for o in per[engname]:
                for d in o.deps:
                    sem, val = d.ev
                    k = id(sem)
                    if waited.get(k, 0) >= val:
                        continue
                    eng.wait_ge(sem, val)
                    waited[k] = val
                ins = o.fn(eng)
                if o.dma:
                    ins.then_inc(o.ev[0], 16)
                elif o.signal:
                    ins.then_inc(o.ev[0], 1)
            if engname == SP:
                for sem, val in finals:
                    eng.wait_ge(sem, val)

        with nc.Block() as block:
            @block.tensor
            def _(e):
                run(PE, e)

            @block.scalar
            def _(e):
                run(ACT, e)

            @block.vector
            def _(e):
                run(DVE, e)

            @block.gpsimd
            def _(e):
                run(POOL, e)

            @block.sync
            def _(e):
                run(SP, e)
        return {e: len(per[e]) for e in per}


class Arena:
    def __init__(self, P, name, n, dtype):
        self.t = P.sb(name, [128, n], dtype)
        self.n = n
        self.off = 0

    def reset(self):
        self.off = 0

    def take(self, n):
        assert self.off + n <= self.n, (self.off, n, self.n)
        v = self.t[:, self.off:self.off + n]
        self.off += n
        return v


V_MIX, V_XA, V_MEM, V_FFN, V_FIN, V_S5D, V_GN, V_LB = 0, 4, 8, 12, 16, 17, 19, 21
NV = 25


def build_program():
    nc = bass.Bass("TRN2", target_bir_lowering=False)

    def din(name, shape):
        return nc.dram_tensor(name, list(shape), F32, kind="ExternalInput").ap()

    def dout(name, shape):
        return nc.dram_tensor(name, list(shape), F32, kind="ExternalOutput").ap()

    xp = din("xp", [NPASS * TP, D])
    xsm = din("xsm", [128, D])
    memp = din("memp", [NMEM, D])
    ck = din("ck", [4, SPC, NMEM, D])
    cv = din("cv", [4, SPC, NMEM, D])
    hgst = din("hgst", [2, SPC, 16, 128, 128])
    vecs = din("vecs", [128, NV, 16])
    ident_d = din("ident", [128, 128])
    maskT_d = din("maskT", [64, 64])
    taus_d = din("taus", [128, 2, 16])
    lam_sl = din("lam_sl", [2, 128, 3, 64])
    lam_pl = din("lam_pl", [2, 128, 3, 8192])
    b_pl = din("b_pl", [2, 128, 2, 8192])
    c_pl = din("c_pl", [2, 128, 2, 8192])
    s5init = din("s5init", [2, 128, 2, 64, SPC])
    w_glu = din("s5_w_glu", [2, D, D])
    hg_w_in = din("hg_w_in", [2, D, 4 * D])
    hg_w_out = din("hg_w_out", [2, D, D])
    x_w_q = din("x_w_q", [4, D, D])
    x_w_k = din("x_w_k", [4, D, D])
    x_w_v = din("x_w_v", [4, D, D])
    x_w_o = din("x_w_o", [4, D, D])
    ffn_w_in = din("ffn_w_in", [4, D, 2 * DFF])
    ffn_w_out = din("ffn_w_out", [4, DFF, D])

    o_yp = dout("o_yp", [NPASS * TP, D])
    o_ys = dout("o_ys", [128, D])
    o_s5p = dout("o_s5p", [2, 128, 2, 64])
    o_hgp = dout("o_hgp", [2, 16, 128, 128])
    o_mk = dout("o_mk", [4, NMEM, D])
    o_mv = dout("o_mv", [4, NMEM, D])
    o_s5s = dout("o_s5s", [2, 128, 2, 64, SPC])
    o_hgs = dout("o_hgs", [2, SPC, 16, 128, 128])

    with ExitStack() as st:
        P = Prog(nc, st)
        x = P.sb("x", [128, KC, TP], F32)
        bx = P.buf("x")
        xn = P.sb("xn", [128, KC, TP], BF16)
        bxn = P.buf("xn")
        AFa = Arena(P, "AF", 16384, F32)
        AHa = Arena(P, "AH", 24576, BF16)
        NSLOT = 3
        WSL = 4096
        wslots = [P.sb("w%d" % i, [128, WSL], BF16) for i in range(NSLOT)]
        wbufs = [P.buf("keepw%d" % i) for i in range(NSLOT)]
        wctr = [0]
        ident = P.sb("ident_sb", [128, 128], F32)
        identb = P.sb("identb", [128, 128], BF16)
        onesb = P.sb("onesb", [128, 128], BF16)
        maskT = P.sb("maskT_sb", [64, 64], F32)
        vec = P.sb("vec", [128, NV, 16], F32)
        lbv = P.sb("lbv", [128, 4, 16], F32)
        oml = P.sb("oml", [128, 4, 16], F32)
        bconst = P.buf("keepconst")
        s5S = P.sb("s5S", [128, 2, 2, 64], F32)
        bs5S = [P.buf("s5S0"), P.buf("s5S1")]
        rstd = P.sb("rstd", [128, TP], F32)
        brstd = P.buf("rstd")
        psb = [P.ps("ps%d" % i, [128, 512], F32) for i in range(8)]
        bps = [P.buf("ps%d" % i) for i in range(8)]
        psctr = [0]

        def next_ps():
            i = psctr[0] % 6
            psctr[0] += 1
            return psb[i], bps[i]

        def phase():
            P.barrier()
            AFa.reset()
            AHa.reset()

        P.dma(SP, ident[:], ident_d, writes=[bconst])
        P.dma(SP, maskT[:], maskT_d, writes=[bconst])
        taus = P.sb("taus_sb", [128, 2, 16], F32)
        P.dma(SP, taus[:], taus_d, writes=[bconst])
        P.dma(SP, vec[:], vecs, writes=[bconst])
        P.op(DVE, lambda e: e.tensor_copy(out=identb[:], in_=ident[:]), reads=[bconst], writes=[bconst])
        P.op(DVE, lambda e: e.memset(onesb[:], 1.0), writes=[bconst])
        P.op(DVE, lambda e: e.memset(s5S[:], 0.0), writes=bs5S)
        P.op(ACT, lambda e: e.activation(out=lbv[:], in_=vec[:, V_LB:V_LB + 4, :], func=AF.Exp), reads=[bconst], writes=[bconst])
        P.op(DVE, lambda e: e.tensor_tensor(out=oml[:, 0, :], in0=lbv[:, 0, :], in1=lbv[:, 1, :], op=ALU.add), reads=[bconst], writes=[bconst])
        P.op(DVE, lambda e: e.tensor_tensor(out=oml[:, 1, :], in0=lbv[:, 2, :], in1=lbv[:, 3, :], op=ALU.add), reads=[bconst], writes=[bconst])
        P.op(DVE, lambda e: e.tensor_tensor(out=oml[:, 0, :], in0=oml[:, 0, :], in1=oml[:, 1, :], op=ALU.add), reads=[bconst], writes=[bconst])
        P.op(DVE, lambda e: e.reciprocal(out=oml[:, 0, :], in_=oml[:, 0, :]), reads=[bconst], writes=[bconst])
        for l in range(4):
            P.op(DVE, lambda e, l=l: e.tensor_tensor(out=lbv[:, l, :], in0=lbv[:, l, :], in1=oml[:, 0, :], op=ALU.mult), reads=[bconst], writes=[bconst])
        P.op(DVE, lambda e: e.memset(lbv[:, 0, :], 0.0), writes=[bconst])
        P.op(DVE, lambda e: e.tensor_tensor(out=lbv[:, 2, :], in0=lbv[:, 2, :], in1=lbv[:, 1, :], op=ALU.add), reads=[bconst], writes=[bconst])
        P.op(DVE, lambda e: e.tensor_tensor(out=lbv[:, 3, :], in0=lbv[:, 3, :], in1=lbv[:, 2, :], op=ALU.add), reads=[bconst], writes=[bconst])
        P.op(DVE, lambda e: e.tensor_scalar(out=oml[:], in0=lbv[:], scalar1=-1.0, scalar2=1.0, op0=ALU.mult, op1=ALU.add), reads=[bconst], writes=[bconst])

        def wload(src2d, kcn, ncols):
            i = wctr[0] % NSLOT
            wctr[0] += 1
            assert kcn * ncols <= WSL
            view = wslots[i][:, 0:kcn * ncols].rearrange("p (k n) -> p k n", k=kcn)
            P.dma(POOL, view, src2d.rearrange("(k p) n -> p k n", p=128), writes=[wbufs[i]])
            return view, wbufs[i]

        def proj_fm(W2d, kcn, ncols_total, src, srcbuf, T, evac, colblk=256):
            cb = min(colblk, WSL // kcn // 128 * 128)
            j = 0
            for c0 in range(0, ncols_total, cb):
                cw = min(cb, ncols_total - c0)
                wt, wb = wload(W2d[:, c0:c0 + cw], kcn, cw)
                for jj in range(cw // 128):
                    ps, pb = next_ps()
                    for kc in range(kcn):
                        P.op(PE, lambda e, ps=ps, wt=wt, kc=kc, jj=jj: e.matmul(ps[:, 0:T], lhsT=wt[:, kc, jj * 128:(jj + 1) * 128], rhs=src[:, kc, 0:T], start=(kc == 0), stop=(kc == kcn - 1)),
                             reads=[wb, srcbuf], writes=[pb])
                    evac(j, ps, pb)
                    j += 1

        def rmsnorm(T, vslot, out_bf, out_buf, src=None, src_buf=None, out_f32=None, out_f32_buf=None):
            src = x if src is None else src
            src_buf = bx if src_buf is None else src_buf
            P.op(ACT, lambda e: e.activation(out=out_bf[:, :, 0:T], in_=src[:, :, 0:T], func=AF.Square), reads=[src_buf], writes=[out_buf])
            ps, pb = psb[6], bps[6]
            for kc in range(KC):
                P.op(PE, lambda e, kc=kc: e.matmul(ps[:, 0:T], lhsT=onesb[:], rhs=out_bf[:, kc, 0:T], start=(kc == 0), stop=(kc == KC - 1)), reads=[out_buf, bconst], writes=[pb])
            P.op(DVE, lambda e: e.tensor_scalar(out=rstd[:, 0:T], in0=ps[:, 0:T], scalar1=1.0 / D, scalar2=EPS, op0=ALU.mult, op1=ALU.add), reads=[pb], writes=[brstd])
            P.op(ACT, lambda e: e.activation(out=rstd[:, 0:T], in_=rstd[:, 0:T], func=AF.Sqrt), reads=[brstd], writes=[brstd])
            P.op(DVE, lambda e: e.reciprocal(out=rstd[:, 0:T], in_=rstd[:, 0:T]), reads=[brstd], writes=[brstd])
            for kc in range(KC):
                if out_f32 is not None:
                    P.op(DVE, lambda e, kc=kc: e.scalar_tensor_tensor(out=out_f32[:, kc, 0:T], in0=src[:, kc, 0:T], scalar=vec[:, vslot, kc:kc + 1], in1=rstd[:, 0:T], op0=ALU.mult, op1=ALU.mult),
                         reads=[src_buf, brstd, bconst], writes=[out_f32_buf])
                    P.op(POOL, lambda e, kc=kc: e.tensor_copy(out=out_bf[:, kc, 0:T], in_=out_f32[:, kc, 0:T]), reads=[out_f32_buf], writes=[out_buf])
                else:
                    P.op(DVE, lambda e, kc=kc: e.scalar_tensor_tensor(out=out_bf[:, kc, 0:T], in0=src[:, kc, 0:T], scalar=vec[:, vslot, kc:kc + 1], in1=rstd[:, 0:T], op0=ALU.mult, op1=ALU.mult),
                         reads=[src_buf, brstd, bconst], writes=[out_buf])

        def add_to_x(T):
            def ev(j, ps, pb):
                P.op(DVE, lambda e, j=j, ps=ps: e.tensor_tensor(out=x[:, j, 0:T], in0=x[:, j, 0:T], in1=ps[:, 0:T], op=ALU.add), reads=[pb, bx], writes=[bx])
            return ev

        def ffn(layer, T):
            phase()
            rmsnorm(T, V_FFN + layer, xn, bxn)
            hid = AHa.take(44 * TP).rearrange("p (k t) -> p k t", k=44)
            bhid = P.buf("hid")
            sg = AFa.take(2 * TP).rearrange("p (a t) -> p a t", a=2)
            bsg = [P.buf("sg0"), P.buf("sg1")]
            W = ffn_w_in[layer]
            for j in range(44):
                wt, wb = wload(W[:, j * 128:(j + 1) * 128], KC, 128)
                wt2, wb2 = wload(W[:, DFF + j * 128:DFF + (j + 1) * 128], KC, 128)
                psg, pbg = next_ps()
                psu, pbu = next_ps()
                for kc in range(KC):
                    P.op(PE, lambda e, kc=kc, wt=wt, psg=psg: e.matmul(psg[:, 0:T], lhsT=wt[:, kc, :], rhs=xn[:, kc, 0:T], start=(kc == 0), stop=(kc == KC - 1)), reads=[wb, bxn], writes=[pbg])
                for kc in range(KC):
                    P.op(PE, lambda e, kc=kc, wt2=wt2, psu=psu: e.matmul(psu[:, 0:T], lhsT=wt2[:, kc, :], rhs=xn[:, kc, 0:T], start=(kc == 0), stop=(kc == KC - 1)), reads=[wb2, bxn], writes=[pbu])
                s = j % 2
                P.op(ACT, lambda e, s=s, psg=psg: e.activation(out=sg[:, s, 0:T], in_=psg[:, 0:T], func=AF.Silu), reads=[pbg], writes=[bsg[s]])
                P.op(DVE, lambda e, s=s, psu=psu, j=j: e.tensor_tensor(out=hid[:, j, 0:T], in0=sg[:, s, 0:T], in1=psu[:, 0:T], op=ALU.mult), reads=[bsg[s], pbu], writes=[bhid])
            Wo = ffn_w_out[layer]
            for j in range(KC):
                wa, wab = wload(Wo[0:2816, j * 128:(j + 1) * 128], 22, 128)
                wb_, wbb = wload(Wo[2816:5632, j * 128:(j + 1) * 128], 22, 128)
                ps, pb = next_ps()
                for kc in range(44):
                    wt_, wtb_ = (wa, wab) if kc < 22 else (wb_, wbb)
                    P.op(PE, lambda e, kc=kc, wt_=wt_, ps=ps: e.matmul(ps[:, 0:T], lhsT=wt_[:, kc % 22, :], rhs=hid[:, kc, 0:T], start=(kc == 0), stop=(kc == 43)), reads=[wtb_, bhid], writes=[pb])
                add_to_x(T)(j, ps, pb)

        def transpose_blocks(dst_fn, src_fn, nblk, rbuf, wbuf, dt_bf=False, rows=128):
            idn = identb if dt_bf else ident
            for g0 in range(0, nblk, 4):
                n = min(4, nblk - g0)
                pst, pbt = (psb[7], bps[7])
                pv = pst[:].bitcast(BF16) if dt_bf else pst[:]
                for i in range(n):
                    P.op(PE, lambda e, i=i, g0=g0, pv=pv: e.transpose(pv[:, i * 128:i * 128 + rows], src_fn(g0 + i), idn[0:rows, 0:rows]), reads=[rbuf, bconst], writes=[pbt])
                for i in range(n):
                    P.op(ACT, lambda e, i=i, g0=g0, pv=pv: e.activation(out=dst_fn(g0 + i), in_=pv[:, i * 128:i * 128 + rows], func=AF.Copy), reads=[pbt], writes=[wbuf])

        def xattn(layer, T, sample, write_kv):
            phase()
            rmsnorm(T, V_XA + layer, xn, bxn)
            q = AHa.take(KC * T).rearrange("p (k t) -> p k t", k=KC)
            bq = P.buf("q")

            def evq(j, ps, pb):
                P.op(ACT, lambda e, j=j, ps=ps: e.activation(out=q[:, j, :], in_=ps[:, 0:T], func=AF.Copy), reads=[pb], writes=[bq])
            proj_fm(x_w_q[layer], KC, D, xn, bxn, T, evq)
            kf = AHa.take(KC * NMEM).rearrange("p (k m) -> p k m", k=KC)
            bkf = P.buf("kf")
            vt = AHa.take(2 * D).rearrange("p (a d) -> p a d", a=2)
            bvt = P.buf("vt")
            vt_def, bvt_def = vt, bvt
            pbf = AHa.take(4 * NMEM).rearrange("p (h m) -> p h m", h=4)
            bpbf = P.buf("pbf")
            LP = 8 if sample else 128
            pT = AHa.take(4 * 2 * LP).rearrange("p (h a l) -> p h a l", h=4, a=2)
            bpT = P.buf("pT")
            stat = AFa.take(16).rearrange("p (a b) -> p a b", a=4)
            bstat = P.buf("stat")
            scale = 1.0 / math.sqrt(512.0)

            def attend(l0, L, vt=None, bvt=None):
                vt = vt_def if vt is None else vt
                bvt = bvt_def if bvt is None else bvt
                ps, pb = next_ps()
                ps2, pb2 = next_ps()
                for h in range(4):
                    pss = (ps if h < 2 else ps2)[0:L, (h % 2) * 256:(h % 2) * 256 + 256]
                    pbb = pb if h < 2 else pb2
                    for kk in range(4):
                        P.op(PE, lambda e, pss=pss, h=h, kk=kk: e.matmul(pss, lhsT=q[:, h * 4 + kk, l0:l0 + L], rhs=kf[:, h * 4 + kk, :], start=(kk == 0), stop=(kk == 3)), reads=[bq, bkf], writes=[pbb])
                for h in range(4):
                    pss = (ps if h < 2 else ps2)[0:L, (h % 2) * 256:(h % 2) * 256 + 256]
                    pbb = pb if h < 2 else pb2
                    P.op(DVE, lambda e, pss=pss, h=h: e.reduce_max(out=stat[0:L, h, 0:1], in_=pss, axis=mybir.AxisListType.X), reads=[pbb], writes=[bstat])
                    P.op(DVE, lambda e, h=h: e.tensor_scalar(out=stat[0:L, h, 1:2], in0=stat[0:L, h, 0:1], scalar1=-scale, scalar2=None, op0=ALU.mult), reads=[bstat], writes=[bstat])
                    P.op(ACT, lambda e, pss=pss, h=h: e.activation(out=pbf[0:L, h, :], in_=pss, func=AF.Exp, bias=stat[0:L, h, 1:2], scale=scale, accum_out=stat[0:L, h, 2:3]), reads=[pbb, bstat], writes=[bpbf, bstat])
                    P.op(DVE, lambda e, h=h: e.reciprocal(out=stat[0:L, h, 3:4], in_=stat[0:L, h, 2:3]), reads=[bstat], writes=[bstat])
                    P.op(DVE, lambda e, h=h: e.tensor_scalar(out=pbf[0:L, h, :], in0=pbf[0:L, h, :], scalar1=stat[0:L, h, 3:4], scalar2=None, op0=ALU.mult), reads=[bstat, bpbf], writes=[bpbf])
                pst, pbt = psb[7], bps[7]
                pv = pst[:].bitcast(BF16)
                for h in range(4):
                    for a in range(2):
                        P.op(PE, lambda e, h=h, a=a: e.transpose(pv[:, (h * 2 + a) * 128:(h * 2 + a) * 128 + L], pbf[0:L, h, a * 128:(a + 1) * 128], identb[0:L, 0:L]), reads=[bpbf, bconst], writes=[pbt])
                P.op(ACT, lambda e: e.activation(out=pT[:, :, :, 0:L], in_=pv[:, 0:1024].rearrange("p (h a l) -> p h a l", h=4, a=2)[:, :, :, 0:L], func=AF.Copy), reads=[pbt], writes=[bpT])
                for cg in range(4):
                    po, pob = next_ps()
                    for ci in range(4):
                        c = cg * 4 + ci
                        for a in range(2):
                            P.op(PE, lambda e, c=c, cg=cg, ci=ci, a=a, po=po: e.matmul(po[:, ci * 128:ci * 128 + L], lhsT=vt[:, a, c * 128:(c + 1) * 128], rhs=pT[:, cg, a, 0:L], start=(a == 0), stop=(a == 1)), reads=[bvt, bpT], writes=[pob])
                    P.op(ACT, lambda e, cg=cg, po=po: e.activation(out=xn[:, cg * 4:cg * 4 + 4, l0:l0 + L], in_=po[:, 0:512].rearrange("p (c l) -> p c l", c=4)[:, :, 0:L], func=AF.Copy), reads=[pob], writes=[bxn])

            if not sample:
                mt = AFa.take(2 * D).rearrange("p (a d) -> p a d", a=2)
                bmt = P.buf("mt")
                P.dma(SP, mt, memp.rearrange("(a p) d -> p a d", p=128), writes=[bmt])
                mst = AFa.take(8).rearrange("p (a b) -> p a b", a=2)
                bmst = P.buf("mst")
                msq = AFa.take(D)
                bmsq = P.buf("msq")
                mnb = AFa.take(2 * D).rearrange("p (a d) -> p a d", a=2)
                bmnb = P.buf("mnb")
                gam = AFa.take(D)
                bgam = P.buf("gam")
                dg = AFa.take(128)
                bdg = P.buf("dg")
                for kc in range(KC):
                    P.op(DVE, lambda e, kc=kc: e.tensor_scalar(out=dg, in0=ident[:], scalar1=vec[:, V_MEM + layer, kc:kc + 1], scalar2=None, op0=ALU.mult), reads=[bconst], writes=[bdg])
                    pg, pbg = next_ps()
                    P.op(PE, lambda e, pg=pg: e.matmul(pg[:, 0:128], lhsT=onesf[:], rhs=dg, start=True, stop=True), reads=[bdg, bconst], writes=[pbg])
                    P.op(ACT, lambda e, kc=kc, pg=pg: e.activation(out=gam[:, kc * 128:(kc + 1) * 128], in_=pg[:, 0:128], func=AF.Copy), reads=[pbg], writes=[bgam])
                for a in range(2):
                    P.op(ACT, lambda e, a=a: e.activation(out=msq, in_=mt[:, a, :], func=AF.Square, accum_out=mst[:, a, 0:1]), reads=[bmt], writes=[bmsq, bmst])
                    P.op(DVE, lambda e, a=a: e.tensor_scalar(out=mst[:, a, 1:2], in0=mst[:, a, 0:1], scalar1=1.0 / D, scalar2=EPS, op0=ALU.mult, op1=ALU.add), reads=[bmst], writes=[bmst])
                    P.op(ACT, lambda e, a=a: e.activation(out=mst[:, a, 1:2], in_=mst[:, a, 1:2], func=AF.Sqrt), reads=[bmst], writes=[bmst])
                    P.op(DVE, lambda e, a=a: e.reciprocal(out=mst[:, a, 2:3], in_=mst[:, a, 1:2]), reads=[bmst], writes=[bmst])
                    P.op(DVE, lambda e, a=a: e.scalar_tensor_tensor(out=mnb[:, a, :], in0=mt[:, a, :], scalar=mst[:, a, 2:3], in1=gam, op0=ALU.mult, op1=ALU.mult), reads=[bmt, bmst, bgam], writes=[bmnb])
                mnf = AHa.take(KC * NMEM).rearrange("p (k m) -> p k m", k=KC)
                bmnf = P.buf("mnf")
                transpose_blocks(lambda i: mnf[:, i // 2, (i % 2) * 128:(i % 2) * 128 + 128], lambda i: mnb[:, i % 2, (i // 2) * 128:(i // 2) * 128 + 128], 32, bmnb, bmnf, dt_bf=False)
                def evk(j, ps, pb):
                    P.op(ACT, lambda e, j=j, ps=ps: e.activation(out=kf[:, j, :], in_=ps[:, 0:NMEM], func=AF.Copy), reads=[pb], writes=[bkf])
                proj_fm(x_w_k[layer], KC, D, mnf, bmnf, NMEM, evk)
                ost = AFa.take(D)
                bost = P.buf("ost")
                for which, W, dst in (("k", x_w_k[layer], o_mk), ("v", x_w_v[layer], o_mv)):
                    if which == "k" and not write_kv:
                        continue
                    for c0 in range(0, D, 256):
                        wt, wb = wload(W[:, c0:c0 + 256], KC, 256)
                        for a in range(2):
                            ps, pb = next_ps()
                            for kc in range(KC):
                                P.op(PE, lambda e, ps=ps, wt=wt, kc=kc, a=a: e.matmul(ps[:, 0:256], lhsT=mnf[:, kc, a * 128:(a + 1) * 128], rhs=wt[:, kc, :], start=(kc == 0), stop=(kc == KC - 1)), reads=[wb, bmnf], writes=[pb])
                            if which == "v":
                                P.op(ACT, lambda e, ps=ps, a=a, c0=c0: e.activation(out=vt[:, a, c0:c0 + 256], in_=ps[:, 0:256], func=AF.Copy), reads=[pb], writes=[bvt])
                            if write_kv:
                                P.op(DVE, lambda e, ps=ps, a=a, c0=c0: e.tensor_copy(out=ost[:, 0:256], in_=ps[:, 0:256]), reads=[pb], writes=[bost])
                                P.dma(SP, dst[layer, a * 128:(a + 1) * 128, c0:c0 + 256], ost[:, 0:256], reads=[bost])
                for l0 in range(0, T, 128):
                    attend(l0, 128)
            else:
                kst2 = [AHa.take(2 * D).rearrange("p (a d) -> p a d", a=2) for _ in range(2)]
                bkst2 = [P.buf("kst0"), P.buf("kst1")]
                vt2 = [vt, AHa.take(2 * D).rearrange("p (a d) -> p a d", a=2)]
                bvt2 = [bvt, P.buf("vt1")]
                for b in range(SPC):
                    kst, bkst = kst2[b % 2], bkst2[b % 2]
                    vtb, bvtb = vt2[b % 2], bvt2[b % 2]
                    P.dma(POOL, kst, ck[layer, b].rearrange("(a p) d -> p a d", p=128), writes=[bkst])
                    P.dma(POOL, vtb, cv[layer, b].rearrange("(a p) d -> p a d", p=128), writes=[bvtb])
                    transpose_blocks(lambda i: kf[:, i // 2, (i % 2) * 128:(i % 2) * 128 + 128], lambda i, kst=kst: kst[:, i % 2, (i // 2) * 128:(i // 2) * 128 + 128], 32, bkst, bkf, dt_bf=True)
                    attend(b * 8, 8, vtb, bvtb)
            proj_fm(x_w_o[layer], KC, D, xn, bxn, T, add_to_x(T))

        onesf = P.sb("onesf", [128, 128], F32)
        P.op(DVE, lambda e: e.memset(onesf[:], 1.0), writes=[bconst])

        bbar_d = nc.dram_tensor("bbar_scr", [2, 128, 2, 8192], BF16, kind="Internal").ap()
        cpad_d = nc.dram_tensor("cpad_scr", [2, 128, 2, 8192], BF16, kind="Internal").ap()
        hgscr = nc.dram_tensor("hg_scr", [2, 128, 16, 128], F32, kind="Internal").ap()
        bbbar = [P.buf("bbar0"), P.buf("bbar1")]
        bcpad = [P.buf("cpad0"), P.buf("cpad1")]
        bhgscr = [P.buf("hgscr0"), P.buf("hgscr1")]
        lamS = P.sb("lamS", [128, 2, 5, 64], F32)
        blamS = P.buf("lamS")
        tint = P.sb("tint", [128, 512], I32)
        btint = P.buf("tint")

        def sin_of(out, th, shift, n, tmp, rb, wb, tb):
            P.op(DVE, lambda e: e.tensor_scalar(out=tint[:, 0:n], in0=th, scalar1=1.0 / TWO_PI, scalar2=shift / TWO_PI, op0=ALU.mult, op1=ALU.add), reads=rb, writes=[btint])
            P.op(DVE, lambda e: e.tensor_copy(out=tmp, in_=tint[:, 0:n]), reads=[btint], writes=[tb])
            P.op(DVE, lambda e: e.scalar_tensor_tensor(out=tmp, in0=tmp, scalar=-TWO_PI, in1=th, op0=ALU.mult, op1=ALU.add), reads=rb + [tb], writes=[tb])
            P.op(DVE, lambda e: e.tensor_scalar(out=tmp, in0=tmp, scalar1=shift, scalar2=-3.14159, op0=ALU.add, op1=ALU.max), reads=[tb], writes=[tb])
            P.op(DVE, lambda e: e.tensor_scalar(out=tmp, in0=tmp, scalar1=3.14159, scalar2=None, op0=ALU.min), reads=[tb], writes=[tb])
            P.op(ACT, lambda e: e.activation(out=out, in_=tmp, func=AF.Sin), reads=[tb], writes=wb)

        def lam_bar(lr, li, ld, n, t, rb, tb):
            dt_, th, mag, tmp = t[2], t[3], t[4], t[5]
            P.op(ACT, lambda e: e.activation(out=dt_, in_=ld, func=AF.Exp), reads=rb, writes=[tb])
            P.op(DVE, lambda e: e.tensor_tensor(out=th, in0=li, in1=dt_, op=ALU.mult), reads=rb + [tb], writes=[tb])
            P.op(DVE, lambda e: e.tensor_tensor(out=mag, in0=lr, in1=dt_, op=ALU.mult), reads=rb + [tb], writes=[tb])
            P.op(ACT, lambda e: e.activation(out=mag, in_=mag, func=AF.Exp), reads=[tb], writes=[tb])
            sin_of(t[1], th, 0.0, n, tmp, [tb], [tb], tb)
            sin_of(t[0], th, math.pi / 2.0, n, tmp, [tb], [tb], tb)
            P.op(DVE, lambda e: e.tensor_tensor(out=t[0], in0=t[0], in1=mag, op=ALU.mult), reads=[tb], writes=[tb])
            P.op(DVE, lambda e: e.tensor_tensor(out=t[1], in0=t[1], in1=mag, op=ALU.mult), reads=[tb], writes=[tb])

        def s5_setup():
            for j in range(2):
                phase()
                ls = AFa.take(3 * 64).rearrange("p (a n) -> p a n", a=3)
                bls = P.buf("ls")
                P.dma(SP, ls, lam_sl[j], writes=[bls])
                t = [AFa.take(64) for _ in range(6)]
                bt = P.buf("t")
                lam_bar(ls[:, 0, :], ls[:, 1, :], ls[:, 2, :], 64, t, [bls], bt)
                P.op(DVE, lambda e, j=j, t=t: e.tensor_copy(out=lamS[:, j, 0, :], in_=t[0]), reads=[bt], writes=[blamS])
                P.op(DVE, lambda e, j=j, t=t: e.tensor_copy(out=lamS[:, j, 1, :], in_=t[1]), reads=[bt], writes=[blamS])
                P.op(DVE, lambda e, j=j, t=t: e.tensor_scalar(out=lamS[:, j, 2, :], in0=t[1], scalar1=-1.0, scalar2=None, op0=ALU.mult), reads=[bt], writes=[blamS])
                P.op(DVE, lambda e, j=j, t=t: e.tensor_copy(out=lamS[:, j, 3, :], in_=t[4]), reads=[bt], writes=[blamS])
                P.op(DVE, lambda e, j=j, t=t: e.tensor_copy(out=lamS[:, j, 4, :], in_=t[3]), reads=[bt], writes=[blamS])
                NB = 512
                lp = AFa.take(3 * NB).rearrange("p (a n) -> p a n", a=3)
                blp = P.buf("lp")
                bp = AFa.take(2 * NB).rearrange("p (a n) -> p a n", a=2)
                bbp = P.buf("bp")
                tt = [AFa.take(NB) for _ in range(10)]
                btt = P.buf("tt")
                ob = AHa.take(2 * NB).rearrange("p (a n) -> p a n", a=2)
                bob = P.buf("ob")
                for blk in range(16):
                    cs = slice(blk * NB, (blk + 1) * NB)
                    P.dma(SP, lp, lam_pl[j][:, :, cs], writes=[blp])
                    P.dma(SP, bp, b_pl[j][:, :, cs], writes=[bbp])
                    lr, li = lp[:, 0, :], lp[:, 1, :]
                    lam_bar(lr, li, lp[:, 2, :], NB, tt, [blp], btt)
                    mr, mi, a_, b_, den, kr, ki = tt[0], tt[1], tt[6], tt[7], tt[8], tt[2], tt[3]
                    P.op(DVE, lambda e: e.tensor_scalar(out=mr, in0=mr, scalar1=-1.0, scalar2=None, op0=ALU.add), reads=[btt], writes=[btt])
                    P.op(DVE, lambda e: e.tensor_tensor(out=a_, in0=mr, in1=lr, op=ALU.mult), reads=[btt, blp], writes=[btt])
                    P.op(DVE, lambda e: e.tensor_tensor(out=b_, in0=mi, in1=li, op=ALU.mult), reads=[btt, blp], writes=[btt])
                    P.op(DVE, lambda e: e.tensor_tensor(out=kr, in0=a_, in1=b_, op=ALU.add), reads=[btt], writes=[btt])
                    P.op(DVE, lambda e: e.tensor_tensor(out=a_, in0=mi, in1=lr, op=ALU.mult), reads=[btt, blp], writes=[btt])
                    P.op(DVE, lambda e: e.tensor_tensor(out=b_, in0=mr, in1=li, op=ALU.mult), reads=[btt, blp], writes=[btt])
                    P.op(DVE, lambda e: e.tensor_tensor(out=ki, in0=a_, in1=b_, op=ALU.subtract), reads=[btt], writes=[btt])
                    P.op(DVE, lambda e: e.tensor_tensor(out=a_, in0=lr, in1=lr, op=ALU.mult), reads=[blp], writes=[btt])
                    P.op(DVE, lambda e: e.tensor_tensor(out=b_, in0=li, in1=li, op=ALU.mult), reads=[blp], writes=[btt])
                    P.op(DVE, lambda e: e.tensor_tensor(out=den, in0=a_, in1=b_, op=ALU.add), reads=[btt], writes=[btt])
                    P.op(DVE, lambda e: e.reciprocal(out=den, in_=den), reads=[btt], writes=[btt])
                    P.op(DVE, lambda e: e.tensor_tensor(out=kr, in0=kr, in1=den, op=ALU.mult), reads=[btt], writes=[btt])
                    P.op(DVE, lambda e: e.tensor_tensor(out=ki, in0=ki, in1=den, op=ALU.mult), reads=[btt], writes=[btt])
                    P.op(DVE, lambda e: e.tensor_tensor(out=a_, in0=kr, in1=bp[:, 0, :], op=ALU.mult), reads=[btt, bbp], writes=[btt])
                    P.op(DVE, lambda e: e.tensor_tensor(out=b_, in0=ki, in1=bp[:, 1, :], op=ALU.mult), reads=[btt, bbp], writes=[btt])
                    P.op(DVE, lambda e: e.tensor_tensor(out=ob[:, 0, :], in0=a_, in1=b_, op=ALU.subtract), reads=[btt], writes=[bob])
                    P.op(DVE, lambda e: e.tensor_tensor(out=a_, in0=kr, in1=bp[:, 1, :], op=ALU.mult), reads=[btt, bbp], writes=[btt])
                    P.op(DVE, lambda e: e.tensor_tensor(out=b_, in0=ki, in1=bp[:, 0, :], op=ALU.mult), reads=[btt, bbp], writes=[btt])
                    P.op(DVE, lambda e: e.tensor_tensor(out=ob[:, 1, :], in0=a_, in1=b_, op=ALU.add), reads=[btt], writes=[bob])
                    P.dma(SP, bbar_d[j][:, :, cs], ob, reads=[bob], writes=[bbbar[j]])
                cpf = AFa.take(2 * 2048).rearrange("p (a n) -> p a n", a=2)
                bcpf = P.buf("cpf")
                cpb = AHa.take(2 * 2048).rearrange("p (a n) -> p a n", a=2)
                bcpb = P.buf("cpb")
                for blk in range(4):
                    cs = slice(blk * 2048, (blk + 1) * 2048)
                    P.dma(SP, cpf, c_pl[j][:, :, cs], writes=[bcpf])
                    P.op(ACT, lambda e: e.activation(out=cpb[:, 0, :], in_=cpf[:, 0, :], func=AF.Copy), reads=[bcpf], writes=[bcpb])
                    P.op(DVE, lambda e: e.tensor_scalar(out=cpb[:, 1, :], in0=cpf[:, 1, :], scalar1=-1.0, scalar2=None, op0=ALU.mult), reads=[bcpf], writes=[bcpb])
                    P.dma(SP, cpad_d[j][:, :, cs], cpb, reads=[bcpb], writes=[bcpad[j]])

        if CFG["setup"]:
            s5_setup()

        def s5_mixer(j, T, sample, last):
            layer = 2 * j
            phase()
            TB = 16
            nb = 2 if sample else 1
            L = TB // nb
            u = AFa.take(KC * T).rearrange("p (k t) -> p k t", k=KC)
            bu = P.buf("u")
            rmsnorm(T, V_MIX + layer, xn, bxn, out_f32=u, out_f32_buf=bu)
            Bp = AHa.take(2 * 8192).rearrange("p (a n) -> p a n", a=2)
            bBp = P.buf("Bp")
            P.dma(SP, Bp, bbar_d[j], reads=[bbbar[j]], writes=[bBp])
            Cq = []
            for gq in range(4):
                if gq < 3:
                    cw = wslots[gq][:, 0:4096].rearrange("p (a n) -> p a n", a=2)
                    cb = wbufs[gq]
                else:
                    cw = AHa.take(4096).rearrange("p (a n) -> p a n", a=2)
                    cb = P.buf("Cq3")
                P.dma(SP, cw, cpad_d[j][:, :, gq * 2048:(gq + 1) * 2048], reads=[bcpad[j]], writes=[cb])
                Cq.append((cw, cb))
            S = AFa.take(2 * 64 * TB).rearrange("p (r g t) -> p r g t", r=2, g=64)
            bS = P.buf("S")
            Sb = AHa.take(2 * 64 * TB).rearrange("p (r g t) -> p r g t", r=2, g=64)
            bSb = P.buf("Sb")
            cs_t = AFa.take(64 * TB).rearrange("p (g t) -> p g t", g=64)
            sn_t = AFa.take(64 * TB).rearrange("p (g t) -> p g t", g=64)
            rt = AFa.take(64 * TB).rearrange("p (g t) -> p g t", g=64)
            btab = P.buf("tab")
            ta = AFa.take(2 * 64 * TB).rearrange("p (r g t) -> p r g t", r=2, g=64)
            bta = P.buf("ta")
            tb_ = AFa.take(64 * TB).rearrange("p (g t) -> p g t", g=64)
            btb = P.buf("tb")
            if sample:
                sini = AFa.take(2 * 64 * SPC).rearrange("p (r g b) -> p r g b", r=2, g=64)
                bsini = P.buf("sini")
                P.dma(SP, sini, s5init[j], writes=[bsini])
                sfin = AFa.take(2 * 64 * SPC).rearrange("p (r g b) -> p r g b", r=2, g=64)
                bsfin = P.buf("sfin")
            taurow = taus[:, 1 if sample else 0, :]
            argv = ta[:, 0]
            P.op(DVE, lambda e: e.tensor_tensor(out=argv, in0=lamS[:, j, 4, :].unsqueeze(2).to_broadcast([128, 64, TB]), in1=taurow.unsqueeze(1).to_broadcast([128, 64, TB]), op=ALU.mult), reads=[blamS, bconst], writes=[bta])
            argf = argv.rearrange("p g t -> p (g t)")
            tmpf = ta[:, 1].rearrange("p g t -> p (g t)")
            snf = sn_t.rearrange("p g t -> p (g t)")
            csf = cs_t.rearrange("p g t -> p (g t)")
            for c0 in range(0, 64 * TB, 512):
                sin_of(snf[:, c0:c0 + 512], argf[:, c0:c0 + 512], 0.0, 512, tmpf[:, c0:c0 + 512], [bta], [btab], bta)
                sin_of(csf[:, c0:c0 + 512], argf[:, c0:c0 + 512], math.pi / 2.0, 512, tmpf[:, c0:c0 + 512], [bta], [btab], bta)
            P.op(DVE, lambda e: e.tensor_copy(out=rt, in_=lamS[:, j, 3, :].unsqueeze(2).to_broadcast([128, 64, TB])), reads=[blamS], writes=[btab])
            P.op(DVE, lambda e: e.memset(rt.rearrange("p g (b l) -> p g b l", b=nb)[:, :, :, 0:1], 0.0), writes=[btab])
            magb = lamS[:, j, 3, :].unsqueeze(1).unsqueeze(3).to_broadcast([128, 2, 64, nb])
            csb = cs_t.unsqueeze(1).to_broadcast([128, 2, 64, TB])
            for kc in range(KC):
                P.op(DVE, lambda e, kc=kc: e.tensor_scalar(out=u[:, kc, 0:T], in0=u[:, kc, 0:T], scalar1=vec[:, V_S5D + j, kc:kc + 1], scalar2=None, op0=ALU.mult), reads=[bu, bconst], writes=[bu])
            tcar = tb_.rearrange("p g t -> p (g t)")[:, 0:2 * 64 * nb].rearrange("p (r g b) -> p r g b", r=2, g=64)
            for blk in range(T // TB):
                t0 = blk * TB
                for gc in range(KC):
                    ps, pb = next_ps()
                    for ri in range(2):
                        for jj in range(4):
                            P.op(PE, lambda e, ps=ps, ri=ri, jj=jj, gc=gc, t0=t0: e.matmul(ps[:, (ri * 4 + jj) * TB:(ri * 4 + jj + 1) * TB], lhsT=Bp[:, ri, (gc * 4 + jj) * 128:(gc * 4 + jj + 1) * 128], rhs=xn[:, gc, t0:t0 + TB], start=True, stop=True), reads=[bBp, bxn], writes=[pb])
                    P.op(ACT, lambda e, ps=ps, gc=gc: e.activation(out=S[:, :, gc * 4:(gc + 1) * 4, :], in_=ps[:, 0:8 * TB].rearrange("p (r g t) -> p r g t", r=2, g=4), func=AF.Copy), reads=[pb], writes=[bS])
                P.op(DVE, lambda e: e.tensor_tensor(out=ta, in0=S, in1=csb, op=ALU.mult), reads=[bS, btab], writes=[bta])
                P.op(DVE, lambda e: e.tensor_tensor(out=tb_, in0=S[:, 1], in1=sn_t, op=ALU.mult), reads=[bS, btab], writes=[btb])
                P.op(DVE, lambda e: e.tensor_tensor(out=ta[:, 0], in0=ta[:, 0], in1=tb_, op=ALU.add), reads=[bta, btb], writes=[bta])
                P.op(DVE, lambda e: e.tensor_tensor(out=tb_, in0=S[:, 0], in1=sn_t, op=ALU.mult), reads=[bS, btab], writes=[btb])
                P.op(DVE, lambda e: e.tensor_tensor(out=ta[:, 1], in0=ta[:, 1], in1=tb_, op=ALU.subtract), reads=[bta, btb], writes=[bta])
                if sample:
                    car, bcar = sini[:, :, :, blk * nb:(blk + 1) * nb], bsini
                else:
                    car, bcar = s5S[:, j, :, :].unsqueeze(3), bs5S[j]
                ta0 = ta.rearrange("p r g (b l) -> p r g b l", b=nb)[:, :, :, :, 0]
                P.op(DVE, lambda e, car=car: e.tensor_tensor(out=tcar, in0=car, in1=magb, op=ALU.mult), reads=[bcar, blamS, btb], writes=[btb])
                P.op(DVE, lambda e, ta0=ta0: e.tensor_tensor(out=ta0, in0=ta0, in1=tcar, op=ALU.add), reads=[bta, btb], writes=[bta])
                for ri in range(2):
                    P.op(DVE, lambda e, ri=ri: e.tensor_tensor_scan(out=S[:, ri].rearrange("p g t -> p (g t)"), data0=rt.rearrange("p g t -> p (g t)"), data1=ta[:, ri].rearrange("p g t -> p (g t)"), initial=0.0, op0=ALU.mult, op1=ALU.add), reads=[bta, btab], writes=[bS])
                P.op(DVE, lambda e: e.tensor_tensor(out=ta, in0=S, in1=csb, op=ALU.mult), reads=[bS, btab], writes=[bta])
                P.op(DVE, lambda e: e.tensor_tensor(out=tb_, in0=S[:, 1], in1=sn_t, op=ALU.mult), reads=[bS, btab], writes=[btb])
                P.op(DVE, lambda e: e.tensor_tensor(out=ta[:, 0], in0=ta[:, 0], in1=tb_, op=ALU.subtract), reads=[bta, btb], writes=[bta])
                P.op(DVE, lambda e: e.tensor_tensor(out=tb_, in0=S[:, 0], in1=sn_t, op=ALU.mult), reads=[bS, btab], writes=[btb])
                P.op(DVE, lambda e: e.tensor_tensor(out=ta[:, 1], in0=ta[:, 1], in1=tb_, op=ALU.add), reads=[bta, btb], writes=[bta])
                taL = ta.rearrange("p r g (b l) -> p r g b l", b=nb)[:, :, :, :, L - 1]
                if sample:
                    P.op(DVE, lambda e, blk=blk, taL=taL: e.tensor_copy(out=sfin[:, :, :, blk * nb:(blk + 1) * nb], in_=taL), reads=[bta], writes=[bsfin])
                else:
                    P.op(DVE, lambda e, taL=taL: e.tensor_copy(out=s5S[:, j, :, :].unsqueeze(3), in_=taL), reads=[bta], writes=[bs5S[j]])
                P.op(POOL, lambda e: e.tensor_copy(out=Sb, in_=ta), reads=[bta], writes=[bSb])
                psy, pby = next_ps()
                for gq in range(4):
                    Cw, cb = Cq[gq]
                    for g4 in range(4):
                        gc = gq * 4 + g4
                        n = 0
                        for ri in range(2):
                            for jj in range(4):
                                P.op(PE, lambda e, Cw=Cw, ri=ri, jj=jj, g4=g4, gc=gc, n=n, psy=psy: e.matmul(psy[:, gc * TB:(gc + 1) * TB], lhsT=Cw[:, ri, (g4 * 4 + jj) * 128:(g4 * 4 + jj + 1) * 128], rhs=Sb[:, ri, gc * 4 + jj, :], start=(n == 0), stop=(n == 7)), reads=[cb, bSb], writes=[pby])
                                n += 1
                P.op(DVE, lambda e, t0=t0, psy=psy: e.tensor_tensor(out=u[:, :, t0:t0 + TB], in0=u[:, :, t0:t0 + TB], in1=psy[:, 0:KC * TB].rearrange("p (k t) -> p k t", k=KC), op=ALU.add), reads=[bu, pby], writes=[bu])
            if sample:
                P.dma(SP, o_s5s[j], sfin, reads=[bsfin])
            elif last:
                P.dma(SP, o_s5p[j], s5S[:, j, :, :], reads=[bs5S[j]])
            taf = ta.rearrange("p r g t -> p (r g t)")
            g1 = taf[:, 0:TP]
            g2 = taf[:, TP:2 * TP]
            bg1, bg2 = bta, bta
            for kc in range(KC):
                v = u[:, kc, 0:T]
                P.op(ACT, lambda e, v=v: e.activation(out=g1[:, 0:T], in_=v, func=AF.Square), reads=[bu], writes=[bg1])
                P.op(DVE, lambda e: e.tensor_scalar(out=g1[:, 0:T], in0=g1[:, 0:T], scalar1=0.044715, scalar2=1.0, op0=ALU.mult, op1=ALU.add), reads=[bg1], writes=[bg1])
                P.op(DVE, lambda e, v=v: e.tensor_tensor(out=g1[:, 0:T], in0=g1[:, 0:T], in1=v, op=ALU.mult), reads=[bg1, bu], writes=[bg1])
                P.op(ACT, lambda e: e.activation(out=g2[:, 0:T], in_=g1[:, 0:T], func=AF.Sigmoid, scale=1.5957691216057308), reads=[bg1], writes=[bg2])
                P.op(DVE, lambda e, v=v: e.tensor_tensor(out=v, in0=v, in1=g2[:, 0:T], op=ALU.mult), reads=[bg2, bu], writes=[bu])
                P.op(POOL, lambda e, v=v, kc=kc: e.tensor_copy(out=xn[:, kc, 0:T], in_=v), reads=[bu], writes=[bxn])

            def evg(jc, ps, pb):
                P.op(ACT, lambda e, ps=ps: e.activation(out=g2[:, 0:T], in_=ps[:, 0:T], func=AF.Sigmoid), reads=[pb], writes=[bg2])
                P.op(DVE, lambda e, jc=jc: e.tensor_tensor(out=g2[:, 0:T], in0=g2[:, 0:T], in1=u[:, jc, 0:T], op=ALU.mult), reads=[bg2, bu], writes=[bg2])
                P.op(DVE, lambda e, jc=jc: e.tensor_tensor(out=x[:, jc, 0:T], in0=x[:, jc, 0:T], in1=g2[:, 0:T], op=ALU.add), reads=[bg2, bx], writes=[bx])
            proj_fm(w_glu[j], KC, D, xn, bxn, T, evg)

        def hgrn_mixer(j, layer, T, sample, last, first):
            phase()
            rmsnorm(T, V_MIX + layer, xn, bxn)
            CS = 8 if sample else 64
            nch = T // CS
            W = hg_w_in[j]
            Sst = AFa.take(16 * 128).rearrange("p (h v) -> p h v", h=16)
            bSst = P.buf("Sst")
            if not sample:
                if first:
                    P.op(DVE, lambda e: e.memset(Sst, 0.0), writes=[bSst])
                else:
                    P.dma(SP, Sst, hgscr[j], reads=[bhgscr[j]], writes=[bSst])
            sgm = AFa.take(4 * TP).rearrange("p (h t) -> p h t", h=4)
            lgf = AFa.take(4 * TP).rearrange("p (h t) -> p h t", h=4)
            eg = AFa.take(4 * TP).rearrange("p (h t) -> p h t", h=4)
            ob = AFa.take(4 * TP).rearrange("p (h t) -> p h t", h=4)
            gs = AFa.take(4 * TP).rearrange("p (h t) -> p h t", h=4)
            qs = AFa.take(2 * TP).rearrange("p (h t) -> p h t", h=2)
            rr = AFa.take(TP)
            NSL = 8
            Sld = AFa.take(NSL * 128).rearrange("p (a v) -> p a v", a=NSL)
            stmp4 = AFa.take(4 * 128).rearrange("p (a v) -> p a v", a=4)
            qt = AHa.take(4 * TP).rearrange("p (h t) -> p h t", h=4)
            kt = AHa.take(4 * TP).rearrange("p (h t) -> p h t", h=4)
            og = AHa.take(4 * TP).rearrange("p (h t) -> p h t", h=4)
            osq = AHa.take(4 * TP).rearrange("p (h t) -> p h t", h=4)
            vtok = AHa.take(nch * 512).rearrange("p (c v) -> p c v", c=nch)
            kTs4 = AHa.take(4 * 128).rearrange("p (a v) -> p a v", a=4)
            PT4 = AHa.take(4 * 64).rearrange("p (a v) -> p a v", a=4)
            Sbf = AHa.take(4 * 128).rearrange("p (h v) -> p h v", h=4)
            for hg in range(4):
                bsgm, blgf, beg, bob, bgs, brr = (P.buf(n) for n in ("sgm", "lgf", "eg", "ob", "gs", "rr"))
                bqs = [P.buf("qs0"), P.buf("qs1")]
                bSld = [P.buf("Sld%d" % i) for i in range(NSL)]
                bqt, bkt, bog, bosq, bvtok = (P.buf(n) for n in ("qt", "kt", "og", "osq", "vtok"))
                bstmp4 = [P.buf("stmp%d" % i) for i in range(4)]
                bkTs4 = [P.buf("kTs%d" % i) for i in range(4)]
                bPT4 = [P.buf("PT%d" % i) for i in range(4)]
                bSbf4 = [P.buf("Sbf%d" % i) for i in range(4)]
                slctr = [0]
                h0 = hg * 4
                def evf(jh, ps, pb):
                    P.op(ACT, lambda e, jh=jh, ps=ps: e.activation(out=sgm[:, jh, 0:T], in_=ps[:, 0:T], func=AF.Sigmoid), reads=[pb], writes=[bsgm])
                    P.op(DVE, lambda e, jh=jh, h0=h0: e.tensor_scalar(out=sgm[:, jh, 0:T], in0=sgm[:, jh, 0:T], scalar1=oml[:, layer, h0 + jh:h0 + jh + 1], scalar2=lbv[:, layer, h0 + jh:h0 + jh + 1], op0=ALU.mult, op1=ALU.add), reads=[bsgm, bconst], writes=[bsgm])
                    P.op(ACT, lambda e, jh=jh: e.activation(out=lgf[:, jh, 0:T], in_=sgm[:, jh, 0:T], func=AF.Ln), reads=[bsgm], writes=[blgf])
                    P.op(DVE, lambda e, jh=jh: e.tensor_scalar(out=sgm[:, jh, 0:T], in0=sgm[:, jh, 0:T], scalar1=-1.0, scalar2=1.0, op0=ALU.mult, op1=ALU.add), reads=[bsgm], writes=[bsgm])
                proj_fm(W[:, D + hg * 512:D + (hg + 1) * 512], KC, 512, xn, bxn, T, evf)
                for jh in range(4):
                    for c in range(nch):
                        P.op(DVE, lambda e, jh=jh, c=c: e.tensor_tensor_scan(out=eg[:, jh, c * CS:(c + 1) * CS], data0=onesf[:, 0:CS], data1=lgf[:, jh, c * CS:(c + 1) * CS], initial=0.0, op0=ALU.mult, op1=ALU.add), reads=[blgf, bconst], writes=[beg])
                P.op(ACT, lambda e: e.activation(out=lgf[:, :, 0:T], in_=eg[:, :, 0:T], func=AF.Exp), reads=[beg], writes=[blgf])
                P.op(ACT, lambda e: e.activation(out=eg[:, :, 0:T], in_=eg[:, :, 0:T], func=AF.Exp, scale=-1.0), reads=[beg, blgf], writes=[beg])
                P.op(DVE, lambda e: e.tensor_tensor(out=kt[:, :, 0:T], in0=sgm[:, :, 0:T], in1=eg[:, :, 0:T], op=ALU.mult), reads=[bsgm, beg], writes=[bkt])
                def evq(jh, ps, pb):
                    s = jh % 2
                    P.op(ACT, lambda e, s=s, ps=ps: e.activation(out=qs[:, s, 0:T], in_=ps[:, 0:T], func=AF.Silu), reads=[pb], writes=[bqs[s]])
                    P.op(DVE, lambda e, s=s, jh=jh: e.tensor_tensor(out=qt[:, jh, 0:T], in0=qs[:, s, 0:T], in1=lgf[:, jh, 0:T], op=ALU.mult), reads=[bqs[s], blgf], writes=[bqt])
                proj_fm(W[:, hg * 512:(hg + 1) * 512], KC, 512, xn, bxn, T, evq)
                for half in range(2):
                    c0 = 2 * D + hg * 512 + half * 256
                    wt, wb = wload(W[:, c0:c0 + 256], KC, 256)
                    for c in range(nch):
                        ps, pb = next_ps()
                        for kc in range(KC):
                            P.op(PE, lambda e, ps=ps, wt=wt, kc=kc, c=c: e.matmul(ps[0:CS, 0:256], lhsT=xn[:, kc, c * CS:(c + 1) * CS], rhs=wt[:, kc, :], start=(kc == 0), stop=(kc == KC - 1)), reads=[wb, bxn], writes=[pb])
                        P.op(ACT, lambda e, ps=ps, c=c, half=half: e.activation(out=vtok[0:CS, c, half * 256:(half + 1) * 256], in_=ps[0:CS, 0:256], func=AF.Copy), reads=[pb], writes=[bvtok])
                def evg(jh, ps, pb):
                    P.op(ACT, lambda e, jh=jh, ps=ps: e.activation(out=gs[:, jh, 0:T], in_=ps[:, 0:T], func=AF.Silu), reads=[pb], writes=[bgs])
                proj_fm(W[:, 3 * D + hg * 512:3 * D + (hg + 1) * 512], KC, 512, xn, bxn, T, evg)
                if not sample:
                    for jh in range(4):
                        P.op(ACT, lambda e, jh=jh, h=h0 + jh: e.activation(out=Sbf[:, jh, :], in_=Sst[:, h, :], func=AF.Copy), reads=[bSst], writes=[bSbf4[jh]])
                for c in range(nch):
                    for jh in range(4):
                        h = h0 + jh
                        stmp, bstmp = stmp4[:, jh, :], bstmp4[jh]
                        kTs, bkTs = kTs4[:, jh, :], bkTs4[jh]
                        PT, bPT = PT4[:, jh, :], bPT4[jh]
                        bSbf = bSbf4[jh]
                        cs = slice(c * CS, (c + 1) * CS)
                        if sample:
                            sl = slctr[0] % NSL
                            slctr[0] += 1
                            P.dma(SP, Sld[:, sl, :], hgst[j, c, h], writes=[bSld[sl]])
                            P.op(ACT, lambda e, jh=jh, sl=sl: e.activation(out=Sbf[:, jh, :], in_=Sld[:, sl, :], func=AF.Copy), reads=[bSld[sl]], writes=[bSbf])
                            Scur, bScur = Sld[:, sl, :], bSld[sl]
                        else:
                            Scur, bScur = Sst[:, h, :], bSst
                        ps1, pb1 = next_ps()
                        P.op(PE, lambda e, ps1=ps1, jh=jh, cs=cs: e.matmul(ps1[0:CS, 0:CS], lhsT=kt[:, jh, cs], rhs=qt[:, jh, cs], start=True, stop=True), reads=[bkt, bqt], writes=[pb1])
                        P.op(DVE, lambda e, ps1=ps1, PT=PT: e.tensor_tensor(out=PT[0:CS, 0:CS], in0=ps1[0:CS, 0:CS], in1=maskT[0:CS, 0:CS], op=ALU.mult), reads=[pb1, bconst], writes=[bPT])
                        pv = psb[7][:].bitcast(BF16)
                        P.op(PE, lambda e, jh=jh, cs=cs, pv=pv: e.transpose(pv[0:CS, jh * 128:(jh + 1) * 128], kt[:, jh, cs], identb[:]), reads=[bkt, bconst], writes=[bps[7]])
                        P.op(ACT, lambda e, pv=pv, kTs=kTs, jh=jh: e.activation(out=kTs[0:CS, :], in_=pv[0:CS, jh * 128:(jh + 1) * 128], func=AF.Copy), reads=[bps[7]], writes=[bkTs])
                        ps2, pb2 = next_ps()
                        P.op(PE, lambda e, ps2=ps2, c=c, jh=jh, PT=PT: e.matmul(ps2[:, 0:CS], lhsT=vtok[0:CS, c, jh * 128:(jh + 1) * 128], rhs=PT[0:CS, 0:CS], start=True, stop=False), reads=[bvtok, bPT], writes=[pb2])
                        P.op(PE, lambda e, ps2=ps2, cs=cs, jh=jh: e.matmul(ps2[:, 0:CS], lhsT=Sbf[:, jh, :], rhs=qt[:, jh, cs], start=False, stop=True), reads=[bSbf, bqt], writes=[pb2])
                        P.op(ACT, lambda e, ps2=ps2, jh=jh, cs=cs: e.activation(out=ob[:, jh, cs], in_=ps2[:, 0:CS], func=AF.Copy), reads=[pb2], writes=[bob])
                        ps3, pb3 = next_ps()
                        P.op(PE, lambda e, ps3=ps3, c=c, jh=jh, kTs=kTs: e.matmul(ps3[:, 0:128], lhsT=kTs[0:CS, :], rhs=vtok[0:CS, c, jh * 128:(jh + 1) * 128], start=True, stop=True), reads=[bkTs, bvtok], writes=[pb3])
                        P.op(DVE, lambda e, ps3=ps3, Scur=Scur, stmp=stmp: e.tensor_tensor(out=stmp, in0=Scur, in1=ps3[:, 0:128], op=ALU.add), reads=[pb3, bScur], writes=[bstmp])
                        P.op(DVE, lambda e, Scur=Scur, jh=jh, c=c, stmp=stmp: e.tensor_scalar(out=Scur, in0=stmp, scalar1=lgf[:, jh, (c + 1) * CS - 1:(c + 1) * CS], scalar2=None, op0=ALU.mult), reads=[bstmp, blgf], writes=[bScur])
                        if sample:
                            P.dma(SP, o_hgs[j, c, h], Scur, reads=[bScur])
                        else:
                            P.op(ACT, lambda e, jh=jh, Scur=Scur: e.activation(out=Sbf[:, jh, :], in_=Scur, func=AF.Copy), reads=[bScur], writes=[bSbf])
                P.op(ACT, lambda e: e.activation(out=osq[:, :, 0:T], in_=ob[:, :, 0:T], func=AF.Square), reads=[bob], writes=[bosq])
                for jh in range(4):
                    h = h0 + jh
                    ps, pb = next_ps()
                    P.op(PE, lambda e, ps=ps, jh=jh: e.matmul(ps[:, 0:T], lhsT=onesb[:], rhs=osq[:, jh, 0:T], start=True, stop=True), reads=[bosq, bconst], writes=[pb])
                    P.op(DVE, lambda e, ps=ps: e.tensor_scalar(out=rr[:, 0:T], in0=ps[:, 0:T], scalar1=1.0 / 128.0, scalar2=EPS, op0=ALU.mult, op1=ALU.add), reads=[pb], writes=[brr])
                    P.op(ACT, lambda e: e.activation(out=rr[:, 0:T], in_=rr[:, 0:T], func=AF.Sqrt), reads=[brr], writes=[brr])
                    P.op(DVE, lambda e: e.reciprocal(out=rr[:, 0:T], in_=rr[:, 0:T]), reads=[brr], writes=[brr])
                    P.op(DVE, lambda e, jh=jh, h=h: e.scalar_tensor_tensor(out=ob[:, jh, 0:T], in0=ob[:, jh, 0:T], scalar=vec[:, V_GN + j, h:h + 1], in1=rr[:, 0:T], op0=ALU.mult, op1=ALU.mult), reads=[bob, brr, bconst], writes=[bob])
                    P.op(DVE, lambda e, jh=jh: e.tensor_tensor(out=og[:, jh, 0:T], in0=ob[:, jh, 0:T], in1=gs[:, jh, 0:T], op=ALU.mult), reads=[bob, bgs], writes=[bog])
                proj_fm(hg_w_out[j][hg * 512:(hg + 1) * 512, :], 4, D, og, bog, T, add_to_x(T))
            if not sample:
                P.dma(SP, hgscr[j], Sst, reads=[bSst], writes=[bhgscr[j]])
                if last:
                    P.dma(SP, o_hgp[j].rearrange("h k v -> k h v"), Sst, reads=[bSst])

        def run_pass(T, src, dst, sample, first, last):
            phase()
            xt = AFa.take(D)
            bxt = P.buf("xt")
            for tt in range(T // 128):
                P.dma(SP, xt, src[tt * 128:(tt + 1) * 128, :], writes=[bxt])
                transpose_blocks(lambda i, tt=tt: x[:, i, tt * 128:(tt + 1) * 128], lambda i: xt[:, i * 128:(i + 1) * 128], KC, bxt, bx)
            for layer in range(4):
                j = layer // 2
                if layer % 2 == 0:
                    if CFG["s5"]:
                        s5_mixer(j, T, sample, last)
                else:
                    if CFG["hg"]:
                        hgrn_mixer(j, layer, T, sample, last, first)
                if CFG["xa"]:
                    xattn(layer, T, sample, first and not sample)
                if CFG["ffn"]:
                    ffn(layer, T)
            phase()
            yf = AFa.take(KC * TP).rearrange("p (k t) -> p k t", k=KC)
            byf = P.buf("yf")
            rmsnorm(T, V_FIN, xn, bxn, out_f32=yf, out_f32_buf=byf)
            yt = AFa.take(D)
            byt = P.buf("yt")
            for tt in range(T // 128):
                transpose_blocks(lambda i: yt[:, i * 128:(i + 1) * 128], lambda i, tt=tt: yf[:, i, tt * 128:(tt + 1) * 128], KC, byf, byt)
                P.dma(SP, dst[tt * 128:(tt + 1) * 128, :], yt, reads=[byt])

        for p in range(CFG["npass"]):
            run_pass(TP, xp[p * TP:(p + 1) * TP, :], o_yp[p * TP:(p + 1) * TP, :], False, p == 0, p == CFG["npass"] - 1)
        if CFG["sample"]:
            run_pass(128, xsm, o_ys, True, False, False)
        print("sbuf remaining", nc.sbuf_bytes_remaining)
        stats = P.emit()
        print("ops", stats)
    return nc


def _vec_layout(v):
    return np.ascontiguousarray(v.reshape(KC, 128).T)


_PROG = None


def kernel(_return_maps=False, **inp):
    f = np.float32
    g = {k: np.asarray(v) for k, v in inp.items()}
    vec = np.zeros((128, NV, 16), f)
    for l in range(4):
        vec[:, V_MIX + l] = _vec_layout(g["norm_mix"][l])
        vec[:, V_XA + l] = _vec_layout(g["norm_xattn"][l])
        vec[:, V_MEM + l] = _vec_layout(g["norm_mem_in"][l])
        vec[:, V_FFN + l] = _vec_layout(g["norm_ffn"][l])
        vec[:, V_LB + l] = _vec_layout(g["hg_lower_bounds"][l])
    vec[:, V_FIN] = _vec_layout(g["norm_final"])
    for j in range(2):
        vec[:, V_S5D + j] = _vec_layout(g["s5_d"][j])
        vec[:, V_GN + j] = _vec_layout(g["hg_g_norm"][j])
    ident = np.eye(128, dtype=f)
    maskT = np.triu(np.ones((64, 64), f))
    taus = np.zeros((128, 2, 16), f)
    taus[:, 0, :] = np.arange(1, 17, dtype=f)[None]
    taus[:, 1, :] = np.tile(np.arange(1, 9, dtype=f), 2)[None]
    lam_sl = np.zeros((2, 128, 3, 64), f)
    lam_pl = np.zeros((2, 128, 3, 8192), f)
    b_pl = np.zeros((2, 128, 2, 8192), f)
    c_pl = np.zeros((2, 128, 2, 8192), f)
    gidx = np.zeros((16, 4, 2), np.int64)
    for gc in range(16):
        for jj in range(4):
            for gh in range(2):
                gidx[gc, jj, gh] = 8 * gc + 4 * gh + jj
    for j in range(2):
        ld = np.broadcast_to(g["s5_log_dt"][j][:, None], (128, 64))
        for i, src in enumerate((g["s5_lam_re"][j], g["s5_lam_im"][j], ld)):
            a = src[gidx]
            lam_sl[j, :, i, :] = a.transpose(3, 2, 0, 1).reshape(128, 64)
            pl = a.transpose(0, 1, 3, 2).reshape(8192)
            lam_pl[j, :, i, :] = pl[None, :]
        for i, src in enumerate((g["s5_b_re"][j], g["s5_b_im"][j])):
            a = src[gidx]
            t = np.zeros((8, 16, 16, 4, 64, 2), f)
            for jj in range(4):
                for gh in range(2):
                    t[4 * gh + jj, :, :, jj, :, gh] = a[:, jj, gh].transpose(2, 0, 1)
            b_pl[j, :, i, :] = t.reshape(128, 8192)
        for i, src in enumerate((g["s5_c_re"][j], g["s5_c_im"][j])):
            a = src[gidx]
            t = np.zeros((64, 2, 16, 4, 8, 16), f)
            for jj in range(4):
                for gh in range(2):
                    t[:, gh, :, jj, 4 * gh + jj, :] = a[:, jj, gh].transpose(2, 0, 1)
            c_pl[j, :, i, :] = t.reshape(128, 8192)

    def s5_state_layout(re, im):
        out = np.zeros((128, 2, 64, re.shape[0]), f)
        for i, s in enumerate((re, im)):
            a = s[:, gidx]
            out[:, i] = a.transpose(4, 3, 1, 2, 0).reshape(128, 64, re.shape[0])
        return out

    def s5_state_unlayout(arr):
        B = arr.shape[-1]
        res = []
        for i in range(2):
            a = arr[:, i].reshape(64, 2, 16, 4, B)
            o = np.zeros((B, 128, 64), f)
            o[:, gidx] = a.transpose(4, 2, 3, 1, 0)
            res.append(o)
        return res

    in_maps = []
    for c in range(NCORE):
        s = c % 4
        sl = slice(c * SPC, (c + 1) * SPC)
        s5i = np.stack([s5_state_layout(g["state_s5_re"][j, sl], g["state_s5_im"][j, sl]) for j in range(2)])
        in_maps.append(dict(
            xp=np.ascontiguousarray(g["x_prompt"][s]), xsm=np.ascontiguousarray(g["x_sample"][sl].reshape(128, D)),
            memp=np.ascontiguousarray(g["mem_prompt"][s]),
            ck=np.ascontiguousarray(g["cache_mem_k"][:, sl].reshape(4, SPC, NMEM, D)),
            cv=np.ascontiguousarray(g["cache_mem_v"][:, sl].reshape(4, SPC, NMEM, D)),
            hgst=np.ascontiguousarray(g["state_hgrn"][:, sl]),
            vecs=vec, ident=ident, maskT=maskT, taus=taus, lam_sl=lam_sl, lam_pl=lam_pl, b_pl=b_pl, c_pl=c_pl, s5init=s5i,
            s5_w_glu=g["s5_w_glu"], hg_w_in=g["hg_w_in"], hg_w_out=g["hg_w_out"],
            x_w_q=g["x_w_q"], x_w_k=g["x_w_k"], x_w_v=g["x_w_v"], x_w_o=g["x_w_o"],
            ffn_w_in=g["ffn_w_in"], ffn_w_out=g["ffn_w_out"]))
    if _return_maps:
        return in_maps
    global _PROG
    if _PROG is None:
        _PROG = build_program()
    nc = _PROG
    res = run_bass_kernel_spmd(nc, in_maps, core_ids=list(range(NCORE))).results
    y_prompt = np.stack([res[s]["o_yp"] for s in range(4)]).astype(f)
    y_sample = np.concatenate([res[c]["o_ys"].reshape(SPC, 8, D) for c in range(NCORE)]).astype(f)
    s5p = [[s5_state_unlayout(res[s]["o_s5p"][j][..., None]) for s in range(4)] for j in range(2)]
    s5_re_p = np.stack([np.concatenate([s5p[j][s][0] for s in range(4)]) for j in range(2)])
    s5_im_p = np.stack([np.concatenate([s5p[j][s][1] for s in range(4)]) for j in range(2)])
    hg_p = np.stack([np.stack([res[s]["o_hgp"][j] for s in range(4)]) for j in range(2)])
    mk = np.stack([np.stack([res[s]["o_mk"][l].reshape(NMEM, 4, 512) for s in range(4)]) for l in range(4)])
    mv = np.stack([np.stack([res[s]["o_mv"][l].reshape(NMEM, 4, 512) for s in range(4)]) for l in range(4)])
    s5s = [[s5_state_unlayout(res[c]["o_s5s"][j]) for c in range(NCORE)] for j in range(2)]
    s5_re_s = np.stack([np.concatenate([s5s[j][c][0] for c in range(NCORE)]) for j in range(2)])
    s5_im_s = np.stack([np.concatenate([s5s[j][c][1] for c in range(NCORE)]) for j in range(2)])
    hg_s = np.stack([np.concatenate([res[c]["o_hgs"][j] for c in range(NCORE)]) for j in range(2)])
    return (y_prompt, y_sample, s5_re_p.astype(f), s5_im_p.astype(f), hg_p.astype(f), mk.astype(f), mv.astype(f),
            s5_re_s.astype(f), s5_im_s.astype(f), hg_s.astype(f))
```

```python
import math
import numpy as np
from contextlib import ExitStack
import concourse.bass as bass
import concourse.mybir as mybir
from concourse.bass_utils import run_bass_kernel_spmd

F32 = mybir.dt.float32
BF16 = mybir.dt.bfloat16
I32 = mybir.dt.int32
AF = mybir.ActivationFunctionType
ALU = mybir.AluOpType
PE, ACT, DVE, POOL, SP = "tensor", "scalar", "vector", "gpsimd", "sync"

D = 2048
KC = 16
DFF = 5632
NMEM = 256
EPS = 1e-6
NCORE = 8
SPC = 16
TP = 512
NPASS = 4
TWO_PI = 2.0 * math.pi
CFG = dict(s5lvl=9, npass=4, s5=True, hg=True, xa=True, ffn=True, sample=True, setup=True)


class Buf:
    __slots__ = ("name", "last_w", "readers", "sem", "cnt")

    def __init__(self, name):
        self.name = name
        self.last_w = None
        self.readers = []
        self.sem = None
        self.cnt = 0


class Op:
    __slots__ = ("eng", "fn", "deps", "dma", "ev", "signal", "chain")

    def __init__(self, eng, fn, dma):
        self.eng = eng
        self.fn = fn
        self.deps = []
        self.dma = dma
        self.ev = None
        self.signal = False
        self.chain = False


class Prog:
    def __init__(self, nc, stack, n_dma_sems=80):
        self.nc = nc
        self.stack = stack
        self.ops = []
        self.esem = {e: [stack.enter_context(nc.semaphore("es%d_" % k + e)) for k in range(4)] for e in (PE, ACT, DVE, POOL)}
        self.free_sems = []
        for i in range(n_dma_sems):
            try:
                self.free_sems.append([stack.enter_context(nc.semaphore("ds%d" % i)), 0])
            except KeyError:
                break
        print("dma sems", len(self.free_sems))
        self.all_chans = list(self.free_sems)
        self.live = []
        self.pending = {e: [] for e in (PE, ACT, DVE, POOL, SP)}
        self.dma_bufs = []

    def buf(self, name="b"):
        return Buf(name)

    def sb(self, name, shape, dtype):
        return self.stack.enter_context(self.nc.sbuf_tensor(name, list(shape), dtype))

    def ps(self, name, shape, dtype=F32):
        return self.stack.enter_context(self.nc.psum_tensor(name, list(shape), dtype))

    def op(self, eng, fn, reads=(), writes=(), dma=False, chain=False):
        o = Op(eng, fn, dma)
        o.chain = chain
        deps = list(self.pending[eng])
        self.pending[eng] = []
        for r in reads:
            if r.last_w is not None:
                deps.append(r.last_w)
        for w in writes:
            if w.last_w is not None:
                deps.append(w.last_w)
            lastr = {}
            for rd in w.readers:
                if rd.dma:
                    deps.append(rd)
                else:
                    lastr[rd.eng] = rd
            deps.extend(lastr.values())
        for r in reads:
            r.readers.append(o)
        for w in writes:
            w.last_w = o
            w.readers = []
        if dma:
            tgt = (list(writes) + list(reads))[0]
            if tgt.sem is not None and tgt.sem[1] >= 1800:
                tgt.sem = None
                if tgt in self.live:
                    self.live.remove(tgt)
            if tgt.sem is None:
                tgt.sem = self.free_sems.pop()
                tgt.cnt = tgt.sem[1]
                self.live.append(tgt)
            tgt.cnt += 1
            tgt.sem[1] = tgt.cnt
            o.ev = (tgt.sem[0], 16 * tgt.cnt)
        seen = set()
        for d in deps:
            if d is o or id(d) in seen:
                continue
            seen.add(id(d))
            if (not d.dma) and d.eng == PE and eng == PE and not dma:
                continue
            if chain and (not d.dma) and d.eng == eng and d.chain:
                continue
            o.deps.append(d)
            d.signal = True
        self.ops.append(o)
        return o

    def dma(self, eng, out, in_, reads=(), writes=(), **kw):
        return self.op(eng, lambda e: e.dma_start(out=out, in_=in_, **kw), reads=reads, writes=writes, dma=True)

    def barrier(self):
        last = {}
        lastd = {}
        for o in self.ops:
            if o.dma:
                lastd[id(o.ev[0])] = o
            else:
                last[o.eng] = o
        deps = list(last.values()) + list(lastd.values())
        for e in self.pending:
            self.pending[e] = list(deps)
        keep = []
        for b in self.live:
            if getattr(b, "name", "").startswith("keep"):
                keep.append(b)
            else:
                if b.sem[1] < 1700:
                    self.free_sems.append(b.sem)
                b.sem = None
        self.live = keep

    def emit(self):
        nc = self.nc
        cnt = {e: 0 for e in self.esem}
        for o in self.ops:
            if (not o.dma) and o.signal:
                cnt[o.eng] += 1
                ep, v = divmod(cnt[o.eng] - 1, 30000)
                o.ev = (self.esem[o.eng][ep], v + 1)
        print("signal counts", cnt)
        per = {e: [] for e in (PE, ACT, DVE, POOL, SP)}
        for o in self.ops:
            per[o.eng].append(o)
        finals = [(c[0], 16 * c[1]) for c in self.all_chans if c[1] > 0]

        def run(engname, eng):
            waited = {}
            for o in per[engname]:
                for d in o.deps:
                    sem, val = d.ev
                    k = id(sem)
                    if waited.get(k, 0) >= val:
                        continue
                    eng.wait_ge(sem, val)
                    waited[k] = val
                ins = o.fn(eng)
                if o.dma:
                    ins.then_inc(o.ev[0], 16)
                elif o.signal:
                    ins.then_inc(o.ev[0], 1)
            if engname == SP:
                for sem, val in finals:
                    eng.wait_ge(sem, val)

        with nc.Block() as block:
            @block.tensor
            def _(e):
                run(PE, e)

            @block.scalar
            def _(e):
                run(ACT, e)

            @block.vector
            def _(e):
                run(DVE, e)

            @block.gpsimd
            def _(e):
                run(POOL, e)

            @block.sync
            def _(e):
                run(SP, e)
        return {e: len(per[e]) for e in per}


class Arena:
    def __init__(self, P, name, n, dtype):
        self.t = P.sb(name, [128, n], dtype)
        self.n = n
        self.off = 0

    def reset(self):
        self.off = 0

    def take(self, n):
        assert self.off + n <= self.n, (self.off, n, self.n)
        v = self.t[:, self.off:self.off + n]
        self.off += n
        return v


V_MIX, V_XA, V_MEM, V_FFN, V_FIN, V_S5D, V_GN, V_LB = 0, 4, 8, 12, 16, 17, 19, 21
NV = 25


def build_program():
    nc = bass.Bass("TRN2", target_bir_lowering=False)

    def din(name, shape):
        return nc.dram_tensor(name, list(shape), F32, kind="ExternalInput").ap()

    def dout(name, shape):
        return nc.dram_tensor(name, list(shape), F32, kind="ExternalOutput").ap()

    xp = din("xp", [NPASS * TP, D])
    xsm = din("xsm", [128, D])
    memp = din("memp", [NMEM, D])
    ck = din("ck", [4, SPC, NMEM, D])
    cv = din("cv", [4, SPC, NMEM, D])
    hgst = din("hgst", [2, SPC, 16, 128, 128])
    vecs = din("vecs", [128, NV, 16])
    ident_d = din("ident", [128, 128])
    maskT_d = din("maskT", [64, 64])
    taus_d = din("taus", [128, 2, 16])
    lam_sl = din("lam_sl", [2, 128, 3, 64])
    lam_pl = din("lam_pl", [2, 128, 3, 8192])
    b_pl = din("b_pl", [2, 128, 2, 8192])
    c_pl = din("c_pl", [2, 128, 2, 8192])
    s5init = din("s5init", [2, 128, 2, 64, SPC])
    w_glu = din("s5_w_glu", [2, D, D])
    hg_w_in = din("hg_w_in", [2, D, 4 * D])
    hg_w_out = din("hg_w_out", [2, D, D])
    x_w_q = din("x_w_q", [4, D, D])
    x_w_k = din("x_w_k", [4, D, D])
    x_w_v = din("x_w_v", [4, D, D])
    x_w_o = din("x_w_o", [4, D, D])
    ffn_w_in = din("ffn_w_in", [4, D, 2 * DFF])
    ffn_w_out = din("ffn_w_out", [4, DFF, D])

    o_yp = dout("o_yp", [NPASS * TP, D])
    o_ys = dout("o_ys", [128, D])
    o_s5p = dout("o_s5p", [2, 128, 2, 64])
    o_hgp = dout("o_hgp", [2, 16, 128, 128])
    o_mk = dout("o_mk", [4, NMEM, D])
    o_mv = dout("o_mv", [4, NMEM, D])
    o_s5s = dout("o_s5s", [2, 128, 2, 64, SPC])
    o_hgs = dout("o_hgs", [2, SPC, 16, 128, 128])

    with ExitStack() as st:
        P = Prog(nc, st)
        x = P.sb("x", [128, KC, TP], F32)
        bx = P.buf("x")
        xn = P.sb("xn", [128, KC, TP], BF16)
        bxn = P.buf("xn")
        AFa = Arena(P, "AF", 16384, F32)
        AHa = Arena(P, "AH", 24576, BF16)
        NSLOT = 4
        WSL = 4096
        wslots = [P.sb("w%d" % i, [128, WSL], BF16) for i in range(NSLOT)]
        wbufs = [P.buf("keepw%d" % i) for i in range(NSLOT)]
        wctr = [0]
        ident = P.sb("ident_sb", [128, 128], F32)
        identb = P.sb("identb", [128, 128], BF16)
        onesb = P.sb("onesb", [128, 128], BF16)
        maskT = P.sb("maskT_sb", [64, 64], F32)
        vec = P.sb("vec", [128, NV, 16], F32)
        lbv = P.sb("lbv", [128, 4, 16], F32)
        oml = P.sb("oml", [128, 4, 16], F32)
        bconst = P.buf("keepconst")
        s5S = P.sb("s5S", [128, 2, 2, 64], F32)
        bs5S = [P.buf("s5S0"), P.buf("s5S1")]
        rstd = P.sb("rstd", [128, TP], F32)
        brstd = P.buf("rstd")
        psb = [P.ps("ps%d" % i, [128, 512], F32) for i in range(8)]
        bps = [P.buf("ps%d" % i) for i in range(8)]
        psctr = [0]

        def next_ps():
            i = psctr[0] % 6
            psctr[0] += 1
            return psb[i], bps[i]

        def phase():
            P.barrier()
            AFa.reset()
            AHa.reset()

        P.dma(SP, ident[:], ident_d, writes=[bconst])
        P.dma(SP, maskT[:], maskT_d, writes=[bconst])
        taus = P.sb("taus_sb", [128, 2, 16], F32)
        P.dma(SP, taus[:], taus_d, writes=[bconst])
        P.dma(SP, vec[:], vecs, writes=[bconst])
        P.op(DVE, lambda e: e.tensor_copy(out=identb[:], in_=ident[:]), reads=[bconst], writes=[bconst])
        P.op(DVE, lambda e: e.memset(onesb[:], 1.0), writes=[bconst])
        P.op(DVE, lambda e: e.memset(s5S[:], 0.0), writes=bs5S)
        P.op(ACT, lambda e: e.activation(out=lbv[:], in_=vec[:, V_LB:V_LB + 4, :], func=AF.Exp), reads=[bconst], writes=[bconst])
        P.op(DVE, lambda e: e.tensor_tensor(out=oml[:, 0, :], in0=lbv[:, 0, :], in1=lbv[:, 1, :], op=ALU.add), reads=[bconst], writes=[bconst])
        P.op(DVE, lambda e: e.tensor_tensor(out=oml[:, 1, :], in0=lbv[:, 2, :], in1=lbv[:, 3, :], op=ALU.add), reads=[bconst], writes=[bconst])
        P.op(DVE, lambda e: e.tensor_tensor(out=oml[:, 0, :], in0=oml[:, 0, :], in1=oml[:, 1, :], op=ALU.add), reads=[bconst], writes=[bconst])
        P.op(DVE, lambda e: e.reciprocal(out=oml[:, 0, :], in_=oml[:, 0, :]), reads=[bconst], writes=[bconst])
        for l in range(4):
            P.op(DVE, lambda e, l=l: e.tensor_tensor(out=lbv[:, l, :], in0=lbv[:, l, :], in1=oml[:, 0, :], op=ALU.mult), reads=[bconst], writes=[bconst])
        P.op(DVE, lambda e: e.memset(lbv[:, 0, :], 0.0), writes=[bconst])
        P.op(DVE, lambda e: e.tensor_tensor(out=lbv[:, 2, :], in0=lbv[:, 2, :], in1=lbv[:, 1, :], op=ALU.add), reads=[bconst], writes=[bconst])
        P.op(DVE, lambda e: e.tensor_tensor(out=lbv[:, 3, :], in0=lbv[:, 3, :], in1=lbv[:, 2, :], op=ALU.add), reads=[bconst], writes=[bconst])
        P.op(DVE, lambda e: e.tensor_scalar(out=oml[:], in0=lbv[:], scalar1=-1.0, scalar2=1.0, op0=ALU.mult, op1=ALU.add), reads=[bconst], writes=[bconst])

        def wload(src2d, kcn, ncols):
            i = wctr[0] % NSLOT
            wctr[0] += 1
            assert kcn * ncols <= WSL
            view = wslots[i][:, 0:kcn * ncols].rearrange("p (k n) -> p k n", k=kcn)
            P.dma(POOL, view, src2d.rearrange("(k p) n -> p k n", p=128), writes=[wbufs[i]])
            return view, wbufs[i]

        def proj_fm(W2d, kcn, ncols_total, src, srcbuf, T, evac, colblk=256):
            cb = min(colblk, WSL // kcn // 128 * 128)
            j = 0
            for c0 in range(0, ncols_total, cb):
                cw = min(cb, ncols_total - c0)
                wt, wb = wload(W2d[:, c0:c0 + cw], kcn, cw)
                for jj in range(cw // 128):
                    ps, pb = next_ps()
                    for kc in range(kcn):
                        P.op(PE, lambda e, ps=ps, wt=wt, kc=kc, jj=jj: e.matmul(ps[:, 0:T], lhsT=wt[:, kc, jj * 128:(jj + 1) * 128], rhs=src[:, kc, 0:T], start=(kc == 0), stop=(kc == kcn - 1)),
                             reads=[wb, srcbuf], writes=[pb])
                    evac(j, ps, pb)
                    j += 1

        def rmsnorm(T, vslot, out_bf, out_buf, src=None, src_buf=None, out_f32=None, out_f32_buf=None):
            src = x if src is None else src
            src_buf = bx if src_buf is None else src_buf
            P.op(ACT, lambda e: e.activation(out=out_bf[:, :, 0:T], in_=src[:, :, 0:T], func=AF.Square), reads=[src_buf], writes=[out_buf])
            ps, pb = psb[6], bps[6]
            for kc in range(KC):
                P.op(PE, lambda e, kc=kc: e.matmul(ps[:, 0:T], lhsT=onesb[:], rhs=out_bf[:, kc, 0:T], start=(kc == 0), stop=(kc == KC - 1)), reads=[out_buf, bconst], writes=[pb])
            P.op(DVE, lambda e: e.tensor_scalar(out=rstd[:, 0:T], in0=ps[:, 0:T], scalar1=1.0 / D, scalar2=EPS, op0=ALU.mult, op1=ALU.add), reads=[pb], writes=[brstd])
            P.op(ACT, lambda e: e.activation(out=rstd[:, 0:T], in_=rstd[:, 0:T], func=AF.Sqrt), reads=[brstd], writes=[brstd])
            P.op(DVE, lambda e: e.reciprocal(out=rstd[:, 0:T], in_=rstd[:, 0:T]), reads=[brstd], writes=[brstd])
            for kc in range(KC):
                if out_f32 is not None:
                    P.op(DVE, lambda e, kc=kc: e.scalar_tensor_tensor(out=out_f32[:, kc, 0:T], in0=src[:, kc, 0:T], scalar=vec[:, vslot, kc:kc + 1], in1=rstd[:, 0:T], op0=ALU.mult, op1=ALU.mult),
                         reads=[src_buf, brstd, bconst], writes=[out_f32_buf])
                    P.op(POOL, lambda e, kc=kc: e.tensor_copy(out=out_bf[:, kc, 0:T], in_=out_f32[:, kc, 0:T]), reads=[out_f32_buf], writes=[out_buf])
                else:
                    P.op(DVE, lambda e, kc=kc: e.scalar_tensor_tensor(out=out_bf[:, kc, 0:T], in0=src[:, kc, 0:T], scalar=vec[:, vslot, kc:kc + 1], in1=rstd[:, 0:T], op0=ALU.mult, op1=ALU.mult),
                         reads=[src_buf, brstd, bconst], writes=[out_buf])

        def add_to_x(T):
            def ev(j, ps, pb):
                P.op(DVE, lambda e, j=j, ps=ps: e.tensor_tensor(out=x[:, j, 0:T], in0=x[:, j, 0:T], in1=ps[:, 0:T], op=ALU.add), reads=[pb, bx], writes=[bx])
            return ev

        def ffn(layer, T):
            phase()
            rmsnorm(T, V_FFN + layer, xn, bxn)
            hid = AHa.take(44 * TP).rearrange("p (k t) -> p k t", k=44)
            bhid = P.buf("hid")
            sg = AFa.take(2 * TP).rearrange("p (a t) -> p a t", a=2)
            bsg = [P.buf("sg0"), P.buf("sg1")]
            W = ffn_w_in[layer]
            for jp in range(22):
                wt, wb = wload(W[:, jp * 256:(jp + 1) * 256], KC, 256)
                wt2, wb2 = wload(W[:, DFF + jp * 256:DFF + (jp + 1) * 256], KC, 256)
                for sub in range(2):
                    j = jp * 2 + sub
                    psg, pbg = next_ps()
                    psu, pbu = next_ps()
                    for kc in range(KC):
                        P.op(PE, lambda e, kc=kc, wt=wt, psg=psg, sub=sub: e.matmul(psg[:, 0:T], lhsT=wt[:, kc, sub * 128:(sub + 1) * 128], rhs=xn[:, kc, 0:T], start=(kc == 0), stop=(kc == KC - 1)), reads=[wb, bxn], writes=[pbg])
                    for kc in range(KC):
                        P.op(PE, lambda e, kc=kc, wt2=wt2, psu=psu, sub=sub: e.matmul(psu[:, 0:T], lhsT=wt2[:, kc, sub * 128:(sub + 1) * 128], rhs=xn[:, kc, 0:T], start=(kc == 0), stop=(kc == KC - 1)), reads=[wb2, bxn], writes=[pbu])
                    s_ = j % 2
                    P.op(ACT, lambda e, s_=s_, psg=psg: e.activation(out=sg[:, s_, 0:T], in_=psg[:, 0:T], func=AF.Silu), reads=[pbg], writes=[bsg[s_]])
                    P.op(DVE, lambda e, s_=s_, psu=psu, j=j: e.tensor_tensor(out=hid[:, j, 0:T], in0=sg[:, s_, 0:T], in1=psu[:, 0:T], op=ALU.mult), reads=[bsg[s_], pbu], writes=[bhid])
            Wo = ffn_w_out[layer]
            for j in range(KC):
                wa, wab = wload(Wo[0:2816, j * 128:(j + 1) * 128], 22, 128)
                wb_, wbb = wload(Wo[2816:5632, j * 128:(j + 1) * 128], 22, 128)
                ps, pb = next_ps()
                for kc in range(44):
                    wt_, wtb_ = (wa, wab) if kc < 22 else (wb_, wbb)
                    P.op(PE, lambda e, kc=kc, wt_=wt_, ps=ps: e.matmul(ps[:, 0:T], lhsT=wt_[:, kc % 22, :], rhs=hid[:, kc, 0:T], start=(kc == 0), stop=(kc == 43)), reads=[wtb_, bhid], writes=[pb])
                add_to_x(T)(j, ps, pb)

        def transpose_blocks(dst_fn, src_fn, nblk, rbuf, wbuf, dt_bf=False, rows=128):
            idn = identb if dt_bf else ident
            for g0 in range(0, nblk, 4):
                n = min(4, nblk - g0)
                pst, pbt = (psb[7], bps[7])
                pv = pst[:].bitcast(BF16) if dt_bf else pst[:]
                for i in range(n):
                    P.op(PE, lambda e, i=i, g0=g0, pv=pv: e.transpose(pv[:, i * 128:i * 128 + rows], src_fn(g0 + i), idn[0:rows, 0:rows]), reads=[rbuf, bconst], writes=[pbt])
                for i in range(n):
                    P.op(ACT, lambda e, i=i, g0=g0, pv=pv: e.activation(out=dst_fn(g0 + i), in_=pv[:, i * 128:i * 128 + rows], func=AF.Copy), reads=[pbt], writes=[wbuf])

        def xattn(layer, T, sample, write_kv):
            phase()
            rmsnorm(T, V_XA + layer, xn, bxn)
            q = AHa.take(KC * T).rearrange("p (k t) -> p k t", k=KC)
            bq = P.buf("q")

            def evq(j, ps, pb):
                P.op(ACT, lambda e, j=j, ps=ps: e.activation(out=q[:, j, :], in_=ps[:, 0:T], func=AF.Copy), reads=[pb], writes=[bq])
            proj_fm(x_w_q[layer], KC, D, xn, bxn, T, evq)
            kf = AHa.take(KC * NMEM).rearrange("p (k m) -> p k m", k=KC)
            bkf = P.buf("kf")
            vt = AHa.take(2 * D).rearrange("p (a d) -> p a d", a=2)
            bvt = P.buf("vt")
            vt_def, bvt_def = vt, bvt
            pbf = AHa.take(4 * NMEM).rearrange("p (h m) -> p h m", h=4)
            bpbf = P.buf("pbf")
            LP = 8 if sample else 128
            pT = AHa.take(4 * 2 * LP).rearrange("p (h a l) -> p h a l", h=4, a=2)
            bpT = P.buf("pT")
            stat = AFa.take(16).rearrange("p (a b) -> p a b", a=4)
            bstat = P.buf("stat")
            scale = 1.0 / math.sqrt(512.0)

            def attend(l0, L, vt=None, bvt=None):
                vt = vt_def if vt is None else vt
                bvt = bvt_def if bvt is None else bvt
                ps, pb = next_ps()
                ps2, pb2 = next_ps()
                for h in range(4):
                    pss = (ps if h < 2 else ps2)[0:L, (h % 2) * 256:(h % 2) * 256 + 256]
                    pbb = pb if h < 2 else pb2
                    for kk in range(4):
                        P.op(PE, lambda e, pss=pss, h=h, kk=kk: e.matmul(pss, lhsT=q[:, h * 4 + kk, l0:l0 + L], rhs=kf[:, h * 4 + kk, :], start=(kk == 0), stop=(kk == 3)), reads=[bq, bkf], writes=[pbb])
                for h in range(4):
                    pss = (ps if h < 2 else ps2)[0:L, (h % 2) * 256:(h % 2) * 256 + 256]
                    pbb = pb if h < 2 else pb2
                    P.op(DVE, lambda e, pss=pss, h=h: e.reduce_max(out=stat[0:L, h, 0:1], in_=pss, axis=mybir.AxisListType.X), reads=[pbb], writes=[bstat])
                    P.op(DVE, lambda e, h=h: e.tensor_scalar(out=stat[0:L, h, 1:2], in0=stat[0:L, h, 0:1], scalar1=-scale, scalar2=None, op0=ALU.mult), reads=[bstat], writes=[bstat])
                    P.op(ACT, lambda e, pss=pss, h=h: e.activation(out=pbf[0:L, h, :], in_=pss, func=AF.Exp, bias=stat[0:L, h, 1:2], scale=scale, accum_out=stat[0:L, h, 2:3]), reads=[pbb, bstat], writes=[bpbf, bstat])
                    P.op(DVE, lambda e, h=h: e.reciprocal(out=stat[0:L, h, 3:4], in_=stat[0:L, h, 2:3]), reads=[bstat], writes=[bstat])
                    P.op(DVE, lambda e, h=h: e.tensor_scalar(out=pbf[0:L, h, :], in0=pbf[0:L, h, :], scalar1=stat[0:L, h, 3:4], scalar2=None, op0=ALU.mult), reads=[bstat, bpbf], writes=[bpbf])
                pst, pbt = psb[7], bps[7]
                pv = pst[:].bitcast(BF16)
                for h in range(4):
                    for a in range(2):
                        P.op(PE, lambda e, h=h, a=a: e.transpose(pv[:, (h * 2 + a) * 128:(h * 2 + a) * 128 + L], pbf[0:L, h, a * 128:(a + 1) * 128], identb[0:L, 0:L]), reads=[bpbf, bconst], writes=[pbt])
                P.op(ACT, lambda e: e.activation(out=pT[:, :, :, 0:L], in_=pv[:, 0:1024].rearrange("p (h a l) -> p h a l", h=4, a=2)[:, :, :, 0:L], func=AF.Copy), reads=[pbt], writes=[bpT])
                for cg in range(4):
                    po, pob = next_ps()
                    for ci in range(4):
                        c = cg * 4 + ci
                        for a in range(2):
                            P.op(PE, lambda e, c=c, cg=cg, ci=ci, a=a, po=po: e.matmul(po[:, ci * 128:ci * 128 + L], lhsT=vt[:, a, c * 128:(c + 1) * 128], rhs=pT[:, cg, a, 0:L], start=(a == 0), stop=(a == 1)), reads=[bvt, bpT], writes=[pob])
                    P.op(ACT, lambda e, cg=cg, po=po: e.activation(out=xn[:, cg * 4:cg * 4 + 4, l0:l0 + L], in_=po[:, 0:512].rearrange("p (c l) -> p c l", c=4)[:, :, 0:L], func=AF.Copy), reads=[pob], writes=[bxn])

            if not sample:
                mt = AFa.take(2 * D).rearrange("p (a d) -> p a d", a=2)
                bmt = P.buf("mt")
                P.dma(SP, mt, memp.rearrange("(a p) d -> p a d", p=128), writes=[bmt])
                mst = AFa.take(8).rearrange("p (a b) -> p a b", a=2)
                bmst = P.buf("mst")
                msq = AFa.take(D)
                bmsq = P.buf("msq")
                mnb = AFa.take(2 * D).rearrange("p (a d) -> p a d", a=2)
                bmnb = P.buf("mnb")
                gam = AFa.take(D)
                bgam = P.buf("gam")
                dg = AFa.take(128)
                bdg = P.buf("dg")
                for kc in range(KC):
                    P.op(DVE, lambda e, kc=kc: e.tensor_scalar(out=dg, in0=ident[:], scalar1=vec[:, V_MEM + layer, kc:kc + 1], scalar2=None, op0=ALU.mult), reads=[bconst], writes=[bdg])
                    pg, pbg = next_ps()
                    P.op(PE, lambda e, pg=pg: e.matmul(pg[:, 0:128], lhsT=onesf[:], rhs=dg, start=True, stop=True), reads=[bdg, bconst], writes=[pbg])
                    P.op(ACT, lambda e, kc=kc, pg=pg: e.activation(out=gam[:, kc * 128:(kc + 1) * 128], in_=pg[:, 0:128], func=AF.Copy), reads=[pbg], writes=[bgam])
                for a in range(2):
                    P.op(ACT, lambda e, a=a: e.activation(out=msq, in_=mt[:, a, :], func=AF.Square, accum_out=mst[:, a, 0:1]), reads=[bmt], writes=[bmsq, bmst])
                    P.op(DVE, lambda e, a=a: e.tensor_scalar(out=mst[:, a, 1:2], in0=mst[:, a, 0:1], scalar1=1.0 / D, scalar2=EPS, op0=ALU.mult, op1=ALU.add), reads=[bmst], writes=[bmst])
                    P.op(ACT, lambda e, a=a: e.activation(out=mst[:, a, 1:2], in_=mst[:, a, 1:2], func=AF.Sqrt), reads=[bmst], writes=[bmst])
                    P.op(DVE, lambda e, a=a: e.reciprocal(out=mst[:, a, 2:3], in_=mst[:, a, 1:2]), reads=[bmst], writes=[bmst])
                    P.op(DVE, lambda e, a=a: e.scalar_tensor_tensor(out=mnb[:, a, :], in0=mt[:, a, :], scalar=mst[:, a, 2:3], in1=gam, op0=ALU.mult, op1=ALU.mult), reads=[bmt, bmst, bgam], writes=[bmnb])
                mnf = AHa.take(KC * NMEM).rearrange("p (k m) -> p k m", k=KC)
                bmnf = P.buf("mnf")
                transpose_blocks(lambda i: mnf[:, i // 2, (i % 2) * 128:(i % 2) * 128 + 128], lambda i: mnb[:, i % 2, (i // 2) * 128:(i // 2) * 128 + 128], 32, bmnb, bmnf, dt_bf=False)
                def evk(j, ps, pb):
                    P.op(ACT, lambda e, j=j, ps=ps: e.activation(out=kf[:, j, :], in_=ps[:, 0:NMEM], func=AF.Copy), reads=[pb], writes=[bkf])
                proj_fm(x_w_k[layer], KC, D, mnf, bmnf, NMEM, evk)
                ost = AFa.take(D)
                bost = P.buf("ost")
                for which, W, dst in (("k", x_w_k[layer], o_mk), ("v", x_w_v[layer], o_mv)):
                    if which == "k" and not write_kv:
                        continue
                    for c0 in range(0, D, 256):
                        wt, wb = wload(W[:, c0:c0 + 256], KC, 256)
                        for a in range(2):
                            ps, pb = next_ps()
                            for kc in range(KC):
                                P.op(PE, lambda e, ps=ps, wt=wt, kc=kc, a=a: e.matmul(ps[:, 0:256], lhsT=mnf[:, kc, a * 128:(a + 1) * 128], rhs=wt[:, kc, :], start=(kc == 0), stop=(kc == KC - 1)), reads=[wb, bmnf], writes=[pb])
                            if which == "v":
                                P.op(ACT, lambda e, ps=ps, a=a, c0=c0: e.activation(out=vt[:, a, c0:c0 + 256], in_=ps[:, 0:256], func=AF.Copy), reads=[pb], writes=[bvt])
                            if write_kv:
                                P.op(DVE, lambda e, ps=ps, a=a, c0=c0: e.tensor_copy(out=ost[:, 0:256], in_=ps[:, 0:256]), reads=[pb], writes=[bost])
                                P.dma(SP, dst[layer, a * 128:(a + 1) * 128, c0:c0 + 256], ost[:, 0:256], reads=[bost])
                for l0 in range(0, T, 128):
                    attend(l0, 128)
            else:
                kst2 = [AHa.take(2 * D).rearrange("p (a d) -> p a d", a=2) for _ in range(2)]
                bkst2 = [P.buf("kst0"), P.buf("kst1")]
                vt2 = [vt, AHa.take(2 * D).rearrange("p (a d) -> p a d", a=2)]
                bvt2 = [bvt, P.buf("vt1")]
                for b in range(SPC):
                    kst, bkst = kst2[b % 2], bkst2[b % 2]
                    vtb, bvtb = vt2[b % 2], bvt2[b % 2]
                    P.dma(POOL, kst, ck[layer, b].rearrange("(a p) d -> p a d", p=128), writes=[bkst])
                    P.dma(POOL, vtb, cv[layer, b].rearrange("(a p) d -> p a d", p=128), writes=[bvtb])
                    transpose_blocks(lambda i: kf[:, i // 2, (i % 2) * 128:(i % 2) * 128 + 128], lambda i, kst=kst: kst[:, i % 2, (i // 2) * 128:(i // 2) * 128 + 128], 32, bkst, bkf, dt_bf=True)
                    attend(b * 8, 8, vtb, bvtb)
            proj_fm(x_w_o[layer], KC, D, xn, bxn, T, add_to_x(T))

        onesf = P.sb("onesf", [128, 128], F32)
        P.op(DVE, lambda e: e.memset(onesf[:], 1.0), writes=[bconst])

        bbar_d = nc.dram_tensor("bbar_scr", [2, 128, 2, 8192], BF16, kind="Internal").ap()
        cpad_d = nc.dram_tensor("cpad_scr", [2, 128, 2, 8192], BF16, kind="Internal").ap()
        hgscr = nc.dram_tensor("hg_scr", [2, 128, 16, 128], F32, kind="Internal").ap()
        bbbar = [P.buf("bbar0"), P.buf("bbar1")]
        bcpad = [P.buf("cpad0"), P.buf("cpad1")]
        bhgscr = [P.buf("hgscr0"), P.buf("hgscr1")]
        lamS = P.sb("lamS", [128, 2, 5, 64], F32)
        blamS = P.buf("lamS")
        tint = P.sb("tint", [128, 512], I32)
        btint = P.buf("tint")

        def sin_of(out, th, shift, n, tmp, rb, wb, tb):
            P.op(DVE, lambda e: e.tensor_scalar(out=tint[:, 0:n], in0=th, scalar1=1.0 / TWO_PI, scalar2=shift / TWO_PI, op0=ALU.mult, op1=ALU.add), reads=rb, writes=[btint])
            P.op(DVE, lambda e: e.tensor_copy(out=tmp, in_=tint[:, 0:n]), reads=[btint], writes=[tb])
            P.op(DVE, lambda e: e.scalar_tensor_tensor(out=tmp, in0=tmp, scalar=-TWO_PI, in1=th, op0=ALU.mult, op1=ALU.add), reads=rb + [tb], writes=[tb])
            P.op(DVE, lambda e: e.tensor_scalar(out=tmp, in0=tmp, scalar1=shift, scalar2=-3.14159, op0=ALU.add, op1=ALU.max), reads=[tb], writes=[tb])
            P.op(DVE, lambda e: e.tensor_scalar(out=tmp, in0=tmp, scalar1=3.14159, scalar2=None, op0=ALU.min), reads=[tb], writes=[tb])
            P.op(ACT, lambda e: e.activation(out=out, in_=tmp, func=AF.Sin), reads=[tb], writes=wb)

        def lam_bar(lr, li, ld, n, t, rb, tb):
            dt_, th, mag, tmp = t[2], t[3], t[4], t[5]
            P.op(ACT, lambda e: e.activation(out=dt_, in_=ld, func=AF.Exp), reads=rb, writes=[tb])
            P.op(DVE, lambda e: e.tensor_tensor(out=th, in0=li, in1=dt_, op=ALU.mult), reads=rb + [tb], writes=[tb])
            P.op(DVE, lambda e: e.tensor_tensor(out=mag, in0=lr, in1=dt_, op=ALU.mult), reads=rb + [tb], writes=[tb])
            P.op(ACT, lambda e: e.activation(out=mag, in_=mag, func=AF.Exp), reads=[tb], writes=[tb])
            sin_of(t[1], th, 0.0, n, tmp, [tb], [tb], tb)
            sin_of(t[0], th, math.pi / 2.0, n, tmp, [tb], [tb], tb)
            P.op(DVE, lambda e: e.tensor_tensor(out=t[0], in0=t[0], in1=mag, op=ALU.mult), reads=[tb], writes=[tb])
            P.op(DVE, lambda e: e.tensor_tensor(out=t[1], in0=t[1], in1=mag, op=ALU.mult), reads=[tb], writes=[tb])

        def s5_setup():
            for j in range(2):
                phase()
                ls = AFa.take(3 * 64).rearrange("p (a n) -> p a n", a=3)
                bls = P.buf("ls")
                P.dma(SP, ls, lam_sl[j], writes=[bls])
                t = [AFa.take(64) for _ in range(6)]
                bt = P.buf("t")
                lam_bar(ls[:, 0, :], ls[:, 1, :], ls[:, 2, :], 64, t, [bls], bt)
                P.op(DVE, lambda e, j=j, t=t: e.tensor_copy(out=lamS[:, j, 0, :], in_=t[0]), reads=[bt], writes=[blamS])
                P.op(DVE, lambda e, j=j, t=t: e.tensor_copy(out=lamS[:, j, 1, :], in_=t[1]), reads=[bt], writes=[blamS])
                P.op(DVE, lambda e, j=j, t=t: e.tensor_scalar(out=lamS[:, j, 2, :], in0=t[1], scalar1=-1.0, scalar2=None, op0=ALU.mult), reads=[bt], writes=[blamS])
                P.op(DVE, lambda e, j=j, t=t: e.tensor_copy(out=lamS[:, j, 3, :], in_=t[4]), reads=[bt], writes=[blamS])
                P.op(DVE, lambda e, j=j, t=t: e.tensor_copy(out=lamS[:, j, 4, :], in_=t[3]), reads=[bt], writes=[blamS])
                NB = 512
                lp = AFa.take(3 * NB).rearrange("p (a n) -> p a n", a=3)
                blp = P.buf("lp")
                bp = AFa.take(2 * NB).rearrange("p (a n) -> p a n", a=2)
                bbp = P.buf("bp")
                tt = [AFa.take(NB) for _ in range(10)]
                btt = P.buf("tt")
                ob = AHa.take(2 * NB).rearrange("p (a n) -> p a n", a=2)
                bob = P.buf("ob")
                for blk in range(16):
                    cs = slice(blk * NB, (blk + 1) * NB)
                    P.dma(SP, lp, lam_pl[j][:, :, cs], writes=[blp])
                    P.dma(SP, bp, b_pl[j][:, :, cs], writes=[bbp])
                    lr, li = lp[:, 0, :], lp[:, 1, :]
                    lam_bar(lr, li, lp[:, 2, :], NB, tt, [blp], btt)
                    mr, mi, a_, b_, den, kr, ki = tt[0], tt[1], tt[6], tt[7], tt[8], tt[2], tt[3]
                    P.op(DVE, lambda e: e.tensor_scalar(out=mr, in0=mr, scalar1=-1.0, scalar2=None, op0=ALU.add), reads=[btt], writes=[btt])
                    P.op(DVE, lambda e: e.tensor_tensor(out=a_, in0=mr, in1=lr, op=ALU.mult), reads=[btt, blp], writes=[btt])
                    P.op(DVE, lambda e: e.tensor_tensor(out=b_, in0=mi, in1=li, op=ALU.mult), reads=[btt, blp], writes=[btt])
                    P.op(DVE, lambda e: e.tensor_tensor(out=kr, in0=a_, in1=b_, op=ALU.add), reads=[btt], writes=[btt])
                    P.op(DVE, lambda e: e.tensor_tensor(out=a_, in0=mi, in1=lr, op=ALU.mult), reads=[btt, blp], writes=[btt])
                    P.op(DVE, lambda e: e.tensor_tensor(out=b_, in0=mr, in1=li, op=ALU.mult), reads=[btt, blp], writes=[btt])
                    P.op(DVE, lambda e: e.tensor_tensor(out=ki, in0=a_, in1=b_, op=ALU.subtract), reads=[btt], writes=[btt])
                    P.op(DVE, lambda e: e.tensor_tensor(out=a_, in0=lr, in1=lr, op=ALU.mult), reads=[blp], writes=[btt])
                    P.op(DVE, lambda e: e.tensor_tensor(out=b_, in0=li, in1=li, op=ALU.mult), reads=[blp], writes=[btt])
                    P.op(DVE, lambda e: e.tensor_tensor(out=den, in0=a_, in1=b_, op=ALU.add), reads=[btt], writes=[btt])
                    P.op(DVE, lambda e: e.reciprocal(out=den, in_=den), reads=[btt], writes=[btt])
                    P.op(DVE, lambda e: e.tensor_tensor(out=kr, in0=kr, in1=den, op=ALU.mult), reads=[btt], writes=[btt])
                    P.op(DVE, lambda e: e.tensor_tensor(out=ki, in0=ki, in1=den, op=ALU.mult), reads=[btt], writes=[btt])
                    P.op(DVE, lambda e: e.tensor_tensor(out=a_, in0=kr, in1=bp[:, 0, :], op=ALU.mult), reads=[btt, bbp], writes=[btt])
                    P.op(DVE, lambda e: e.tensor_tensor(out=b_, in0=ki, in1=bp[:, 1, :], op=ALU.mult), reads=[btt, bbp], writes=[btt])
                    P.op(DVE, lambda e: e.tensor_tensor(out=ob[:, 0, :], in0=a_, in1=b_, op=ALU.subtract), reads=[btt], writes=[bob])
                    P.op(DVE, lambda e: e.tensor_tensor(out=a_, in0=kr, in1=bp[:, 1, :], op=ALU.mult), reads=[btt, bbp], writes=[btt])
                    P.op(DVE, lambda e: e.tensor_tensor(out=b_, in0=ki, in1=bp[:, 0, :], op=ALU.mult), reads=[btt, bbp], writes=[btt])
                    P.op(DVE, lambda e: e.tensor_tensor(out=ob[:, 1, :], in0=a_, in1=b_, op=ALU.add), reads=[btt], writes=[bob])
                    P.dma(SP, bbar_d[j][:, :, cs], ob, reads=[bob], writes=[bbbar[j]])
                cpf = AFa.take(2 * 2048).rearrange("p (a n) -> p a n", a=2)
                bcpf = P.buf("cpf")
                cpb = AHa.take(2 * 2048).rearrange("p (a n) -> p a n", a=2)
                bcpb = P.buf("cpb")
                for blk in range(4):
                    cs = slice(blk * 2048, (blk + 1) * 2048)
                    P.dma(SP, cpf, c_pl[j][:, :, cs], writes=[bcpf])
                    P.op(ACT, lambda e: e.activation(out=cpb[:, 0, :], in_=cpf[:, 0, :], func=AF.Copy), reads=[bcpf], writes=[bcpb])
                    P.op(DVE, lambda e: e.tensor_scalar(out=cpb[:, 1, :], in0=cpf[:, 1, :], scalar1=-1.0, scalar2=None, op0=ALU.mult), reads=[bcpf], writes=[bcpb])
                    P.dma(SP, cpad_d[j][:, :, cs], cpb, reads=[bcpb], writes=[bcpad[j]])

        if CFG["setup"]:
            s5_setup()

        def s5_mixer(j, T, sample, last):
            layer = 2 * j
            phase()
            TB = 16
            nb = 2 if sample else 1
            L = TB // nb
            u = AFa.take(KC * T).rearrange("p (k t) -> p k t", k=KC)
            bu = P.buf("u")
            rmsnorm(T, V_MIX + layer, xn, bxn, out_f32=u, out_f32_buf=bu)
            Bp = AHa.take(2 * 8192).rearrange("p (a n) -> p a n", a=2)
            bBp = P.buf("Bp")
            P.dma(SP, Bp, bbar_d[j], reads=[bbbar[j]], writes=[bBp])
            Cq = []
            for gq in range(4):
                if gq < 3:
                    cw = wslots[gq][:, 0:4096].rearrange("p (a n) -> p a n", a=2)
                    cb = wbufs[gq]
                else:
                    cw = AHa.take(4096).rearrange("p (a n) -> p a n", a=2)
                    cb = P.buf("Cq3")
                P.dma(SP, cw, cpad_d[j][:, :, gq * 2048:(gq + 1) * 2048], reads=[bcpad[j]], writes=[cb])
                Cq.append((cw, cb))
            S = AFa.take(2 * 64 * TB).rearrange("p (r g t) -> p r g t", r=2, g=64)
            bS = P.buf("S")
            Sb = AHa.take(2 * 64 * TB).rearrange("p (r g t) -> p r g t", r=2, g=64)
            bSb = P.buf("Sb")
            cs_t = AFa.take(64 * TB).rearrange("p (g t) -> p g t", g=64)
            sn_t = AFa.take(64 * TB).rearrange("p (g t) -> p g t", g=64)
            rt = AFa.take(64 * TB).rearrange("p (g t) -> p g t", g=64)
            btab = P.buf("tab")
            ta = AFa.take(2 * 64 * TB).rearrange("p (r g t) -> p r g t", r=2, g=64)
            bta = P.buf("ta")
            tb_ = AFa.take(64 * TB).rearrange("p (g t) -> p g t", g=64)
            btb = P.buf("tb")
            if sample:
                sini = AFa.take(2 * 64 * SPC).rearrange("p (r g b) -> p r g b", r=2, g=64)
                bsini = P.buf("sini")
                P.dma(SP, sini, s5init[j], writes=[bsini])
                sfin = AFa.take(2 * 64 * SPC).rearrange("p (r g b) -> p r g b", r=2, g=64)
                bsfin = P.buf("sfin")
            taurow = taus[:, 1 if sample else 0, :]
            argv = ta[:, 0]
            P.op(DVE, lambda e: e.tensor_tensor(out=argv, in0=lamS[:, j, 4, :].unsqueeze(2).to_broadcast([128, 64, TB]), in1=taurow.unsqueeze(1).to_broadcast([128, 64, TB]), op=ALU.mult), reads=[blamS, bconst], writes=[bta])
            argf = argv.rearrange("p g t -> p (g t)")
            tmpf = ta[:, 1].rearrange("p g t -> p (g t)")
            snf = sn_t.rearrange("p g t -> p (g t)")
            csf = cs_t.rearrange("p g t -> p (g t)")
            for c0 in range(0, 64 * TB, 512):
                sin_of(snf[:, c0:c0 + 512], argf[:, c0:c0 + 512], 0.0, 512, tmpf[:, c0:c0 + 512], [bta], [btab], bta)
                sin_of(csf[:, c0:c0 + 512], argf[:, c0:c0 + 512], math.pi / 2.0, 512, tmpf[:, c0:c0 + 512], [bta], [btab], bta)
            P.op(DVE, lambda e: e.tensor_copy(out=rt, in_=lamS[:, j, 3, :].unsqueeze(2).to_broadcast([128, 64, TB])), reads=[blamS], writes=[btab])
            P.op(DVE, lambda e: e.memset(rt.rearrange("p g (b l) -> p g b l", b=nb)[:, :, :, 0:1], 0.0), writes=[btab])
            magb = lamS[:, j, 3, :].unsqueeze(1).unsqueeze(3).to_broadcast([128, 2, 64, nb])
            csb = cs_t.unsqueeze(1).to_broadcast([128, 2, 64, TB])
            for kc in range(KC):
                P.op(DVE, lambda e, kc=kc: e.tensor_scalar(out=u[:, kc, 0:T], in0=u[:, kc, 0:T], scalar1=vec[:, V_S5D + j, kc:kc + 1], scalar2=None, op0=ALU.mult), reads=[bu, bconst], writes=[bu])
            tcar = tb_.rearrange("p g t -> p (g t)")[:, 0:2 * 64 * nb].rearrange("p (r g b) -> p r g b", r=2, g=64)
            for blk in range(T // TB):
                t0 = blk * TB
                for gc in range(KC):
                    ps, pb = next_ps()
                    for ri in range(2):
                        for jj in range(4):
                            P.op(PE, lambda e, ps=ps, ri=ri, jj=jj, gc=gc, t0=t0: e.matmul(ps[:, (ri * 4 + jj) * TB:(ri * 4 + jj + 1) * TB], lhsT=Bp[:, ri, (gc * 4 + jj) * 128:(gc * 4 + jj + 1) * 128], rhs=xn[:, gc, t0:t0 + TB], start=True, stop=True), reads=[bBp, bxn], writes=[pb])
                    P.op(ACT, lambda e, ps=ps, gc=gc: e.activation(out=S[:, :, gc * 4:(gc + 1) * 4, :], in_=ps[:, 0:8 * TB].rearrange("p (r g t) -> p r g t", r=2, g=4), func=AF.Copy), reads=[pb], writes=[bS])
                P.op(DVE, lambda e: e.tensor_tensor(out=ta, in0=S, in1=csb, op=ALU.mult), reads=[bS, btab], writes=[bta])
                P.op(DVE, lambda e: e.tensor_tensor(out=tb_, in0=S[:, 1], in1=sn_t, op=ALU.mult), reads=[bS, btab], writes=[btb])
                P.op(DVE, lambda e: e.tensor_tensor(out=ta[:, 0], in0=ta[:, 0], in1=tb_, op=ALU.add), reads=[bta, btb], writes=[bta])
                P.op(DVE, lambda e: e.tensor_tensor(out=tb_, in0=S[:, 0], in1=sn_t, op=ALU.mult), reads=[bS, btab], writes=[btb])
                P.op(DVE, lambda e: e.tensor_tensor(out=ta[:, 1], in0=ta[:, 1], in1=tb_, op=ALU.subtract), reads=[bta, btb], writes=[bta])
                if sample:
                    car, bcar = sini[:, :, :, blk * nb:(blk + 1) * nb], bsini
                else:
                    car, bcar = s5S[:, j, :, :].unsqueeze(3), bs5S[j]
                ta0 = ta.rearrange("p r g (b l) -> p r g b l", b=nb)[:, :, :, :, 0]
                P.op(DVE, lambda e, car=car: e.tensor_tensor(out=tcar, in0=car, in1=magb, op=ALU.mult), reads=[bcar, blamS, btb], writes=[btb])
                P.op(DVE, lambda e, ta0=ta0: e.tensor_tensor(out=ta0, in0=ta0, in1=tcar, op=ALU.add), reads=[bta, btb], writes=[bta])
                for ri in range(2):
                    P.op(DVE, lambda e, ri=ri: e.tensor_tensor_scan(out=S[:, ri].rearrange("p g t -> p (g t)"), data0=rt.rearrange("p g t -> p (g t)"), data1=ta[:, ri].rearrange("p g t -> p (g t)"), initial=0.0, op0=ALU.mult, op1=ALU.add), reads=[bta, btab], writes=[bS])
                P.op(DVE, lambda e: e.tensor_tensor(out=ta, in0=S, in1=csb, op=ALU.mult), reads=[bS, btab], writes=[bta])
                P.op(DVE, lambda e: e.tensor_tensor(out=tb_, in0=S[:, 1], in1=sn_t, op=ALU.mult), reads=[bS, btab], writes=[btb])
                P.op(DVE, lambda e: e.tensor_tensor(out=ta[:, 0], in0=ta[:, 0], in1=tb_, op=ALU.subtract), reads=[bta, btb], writes=[bta])
                P.op(DVE, lambda e: e.tensor_tensor(out=tb_, in0=S[:, 0], in1=sn_t, op=ALU.mult), reads=[bS, btab], writes=[btb])
                P.op(DVE, lambda e: e.tensor_tensor(out=ta[:, 1], in0=ta[:, 1], in1=tb_, op=ALU.add), reads=[bta, btb], writes=[bta])
                taL = ta.rearrange("p r g (b l) -> p r g b l", b=nb)[:, :, :, :, L - 1]
                if sample:
                    P.op(DVE, lambda e, blk=blk, taL=taL: e.tensor_copy(out=sfin[:, :, :, blk * nb:(blk + 1) * nb], in_=taL), reads=[bta], writes=[bsfin])
                else:
                    P.op(DVE, lambda e, taL=taL: e.tensor_copy(out=s5S[:, j, :, :].unsqueeze(3), in_=taL), reads=[bta], writes=[bs5S[j]])
                P.op(POOL, lambda e: e.tensor_copy(out=Sb, in_=ta), reads=[bta], writes=[bSb])
                psy, pby = next_ps()
                for gq in range(4):
                    Cw, cb = Cq[gq]
                    for g4 in range(4):
                        gc = gq * 4 + g4
                        n = 0
                        for ri in range(2):
                            for jj in range(4):
                                P.op(PE, lambda e, Cw=Cw, ri=ri, jj=jj, g4=g4, gc=gc, n=n, psy=psy: e.matmul(psy[:, gc * TB:(gc + 1) * TB], lhsT=Cw[:, ri, (g4 * 4 + jj) * 128:(g4 * 4 + jj + 1) * 128], rhs=Sb[:, ri, gc * 4 + jj, :], start=(n == 0), stop=(n == 7)), reads=[cb, bSb], writes=[pby])
                                n += 1
                P.op(DVE, lambda e, t0=t0, psy=psy: e.tensor_tensor(out=u[:, :, t0:t0 + TB], in0=u[:, :, t0:t0 + TB], in1=psy[:, 0:KC * TB].rearrange("p (k t) -> p k t", k=KC), op=ALU.add), reads=[bu, pby], writes=[bu])
            if sample:
                P.dma(SP, o_s5s[j], sfin, reads=[bsfin])
            elif last:
                P.dma(SP, o_s5p[j], s5S[:, j, :, :], reads=[bs5S[j]])
            taf = ta.rearrange("p r g t -> p (r g t)")
            g1 = taf[:, 0:TP]
            g2 = taf[:, TP:2 * TP]
            bg1, bg2 = bta, bta
            for kc in range(KC):
                v = u[:, kc, 0:T]
                P.op(ACT, lambda e, v=v: e.activation(out=g1[:, 0:T], in_=v, func=AF.Square), reads=[bu], writes=[bg1])
                P.op(DVE, lambda e: e.tensor_scalar(out=g1[:, 0:T], in0=g1[:, 0:T], scalar1=0.044715, scalar2=1.0, op0=ALU.mult, op1=ALU.add), reads=[bg1], writes=[bg1])
                P.op(DVE, lambda e, v=v: e.tensor_tensor(out=g1[:, 0:T], in0=g1[:, 0:T], in1=v, op=ALU.mult), reads=[bg1, bu], writes=[bg1])
                P.op(ACT, lambda e: e.activation(out=g2[:, 0:T], in_=g1[:, 0:T], func=AF.Sigmoid, scale=1.5957691216057308), reads=[bg1], writes=[bg2])
                P.op(DVE, lambda e, v=v: e.tensor_tensor(out=v, in0=v, in1=g2[:, 0:T], op=ALU.mult), reads=[bg2, bu], writes=[bu])
                P.op(POOL, lambda e, v=v, kc=kc: e.tensor_copy(out=xn[:, kc, 0:T], in_=v), reads=[bu], writes=[bxn])

            def evg(jc, ps, pb):
                P.op(ACT, lambda e, ps=ps: e.activation(out=g2[:, 0:T], in_=ps[:, 0:T], func=AF.Sigmoid), reads=[pb], writes=[bg2])
                P.op(DVE, lambda e, jc=jc: e.tensor_tensor(out=g2[:, 0:T], in0=g2[:, 0:T], in1=u[:, jc, 0:T], op=ALU.mult), reads=[bg2, bu], writes=[bg2])
                P.op(DVE, lambda e, jc=jc: e.tensor_tensor(out=x[:, jc, 0:T], in0=x[:, jc, 0:T], in1=g2[:, 0:T], op=ALU.add), reads=[bg2, bx], writes=[bx])
            proj_fm(w_glu[j], KC, D, xn, bxn, T, evg)

        def hgrn_mixer(j, layer, T, sample, last, first):
            phase()
            rmsnorm(T, V_MIX + layer, xn, bxn)
            CS = 8 if sample else 64
            nch = T // CS
            W = hg_w_in[j]
            Sst = AFa.take(16 * 128).rearrange("p (h v) -> p h v", h=16)
            bSst = P.buf("Sst")
            if not sample:
                if first:
                    P.op(DVE, lambda e: e.memset(Sst, 0.0), writes=[bSst])
                else:
                    P.dma(SP, Sst, hgscr[j], reads=[bhgscr[j]], writes=[bSst])
            sgm = AFa.take(4 * TP).rearrange("p (h t) -> p h t", h=4)
            lgf = AFa.take(4 * TP).rearrange("p (h t) -> p h t", h=4)
            eg = AFa.take(4 * TP).rearrange("p (h t) -> p h t", h=4)
            ob = AFa.take(4 * TP).rearrange("p (h t) -> p h t", h=4)
            gs = AFa.take(4 * TP).rearrange("p (h t) -> p h t", h=4)
            qs = AFa.take(2 * TP).rearrange("p (h t) -> p h t", h=2)
            rr = AFa.take(TP)
            NSL = 8
            Sld = AFa.take(NSL * 128).rearrange("p (a v) -> p a v", a=NSL)
            stmp4 = AFa.take(4 * 128).rearrange("p (a v) -> p a v", a=4)
            qt = AHa.take(4 * TP).rearrange("p (h t) -> p h t", h=4)
            kt = AHa.take(4 * TP).rearrange("p (h t) -> p h t", h=4)
            og = AHa.take(4 * TP).rearrange("p (h t) -> p h t", h=4)
            osq = AHa.take(4 * TP).rearrange("p (h t) -> p h t", h=4)
            vtok = AHa.take(nch * 512).rearrange("p (c v) -> p c v", c=nch)
            kTs4 = AHa.take(4 * 128).rearrange("p (a v) -> p a v", a=4)
            PT4 = AHa.take(4 * 64).rearrange("p (a v) -> p a v", a=4)
            Sbf = AHa.take(4 * 128).rearrange("p (h v) -> p h v", h=4)
            for hg in range(4):
                bsgm, blgf, beg, bob, bgs, brr = (P.buf(n) for n in ("sgm", "lgf", "eg", "ob", "gs", "rr"))
                bqs = [P.buf("qs0"), P.buf("qs1")]
                bSld = [P.buf("Sld%d" % i) for i in range(NSL)]
                bqt, bkt, bog, bosq, bvtok = (P.buf(n) for n in ("qt", "kt", "og", "osq", "vtok"))
                bstmp4 = [P.buf("stmp%d" % i) for i in range(4)]
                bkTs4 = [P.buf("kTs%d" % i) for i in range(4)]
                bPT4 = [P.buf("PT%d" % i) for i in range(4)]
                bSbf4 = [P.buf("Sbf%d" % i) for i in range(4)]
                slctr = [0]
                h0 = hg * 4
                def evf(jh, ps, pb):
                    P.op(ACT, lambda e, jh=jh, ps=ps: e.activation(out=sgm[:, jh, 0:T], in_=ps[:, 0:T], func=AF.Sigmoid), reads=[pb], writes=[bsgm])
                    P.op(DVE, lambda e, jh=jh, h0=h0: e.tensor_scalar(out=sgm[:, jh, 0:T], in0=sgm[:, jh, 0:T], scalar1=oml[:, layer, h0 + jh:h0 + jh + 1], scalar2=lbv[:, layer, h0 + jh:h0 + jh + 1], op0=ALU.mult, op1=ALU.add), reads=[bsgm, bconst], writes=[bsgm])
                    P.op(ACT, lambda e, jh=jh: e.activation(out=lgf[:, jh, 0:T], in_=sgm[:, jh, 0:T], func=AF.Ln), reads=[bsgm], writes=[blgf])
                    P.op(DVE, lambda e, jh=jh: e.tensor_scalar(out=sgm[:, jh, 0:T], in0=sgm[:, jh, 0:T], scalar1=-1.0, scalar2=1.0, op0=ALU.mult, op1=ALU.add), reads=[bsgm], writes=[bsgm])
                proj_fm(W[:, D + hg * 512:D + (hg + 1) * 512], KC, 512, xn, bxn, T, evf)
                for jh in range(4):
                    for c in range(nch):
                        P.op(DVE, lambda e, jh=jh, c=c: e.tensor_tensor_scan(out=eg[:, jh, c * CS:(c + 1) * CS], data0=onesf[:, 0:CS], data1=lgf[:, jh, c * CS:(c + 1) * CS], initial=0.0, op0=ALU.mult, op1=ALU.add), reads=[blgf, bconst], writes=[beg])
                P.op(ACT, lambda e: e.activation(out=lgf[:, :, 0:T], in_=eg[:, :, 0:T], func=AF.Exp), reads=[beg], writes=[blgf])
                P.op(ACT, lambda e: e.activation(out=eg[:, :, 0:T], in_=eg[:, :, 0:T], func=AF.Exp, scale=-1.0), reads=[beg, blgf], writes=[beg])
                P.op(DVE, lambda e: e.tensor_tensor(out=kt[:, :, 0:T], in0=sgm[:, :, 0:T], in1=eg[:, :, 0:T], op=ALU.mult), reads=[bsgm, beg], writes=[bkt])
                def evq(jh, ps, pb):
                    s = jh % 2
                    P.op(ACT, lambda e, s=s, ps=ps: e.activation(out=qs[:, s, 0:T], in_=ps[:, 0:T], func=AF.Silu), reads=[pb], writes=[bqs[s]])
                    P.op(DVE, lambda e, s=s, jh=jh: e.tensor_tensor(out=qt[:, jh, 0:T], in0=qs[:, s, 0:T], in1=lgf[:, jh, 0:T], op=ALU.mult), reads=[bqs[s], blgf], writes=[bqt])
                proj_fm(W[:, hg * 512:(hg + 1) * 512], KC, 512, xn, bxn, T, evq)
                for half in range(2):
                    c0 = 2 * D + hg * 512 + half * 256
                    wt, wb = wload(W[:, c0:c0 + 256], KC, 256)
                    for c in range(nch):
                        ps, pb = next_ps()
                        for kc in range(KC):
                            P.op(PE, lambda e, ps=ps, wt=wt, kc=kc, c=c: e.matmul(ps[0:CS, 0:256], lhsT=xn[:, kc, c * CS:(c + 1) * CS], rhs=wt[:, kc, :], start=(kc == 0), stop=(kc == KC - 1)), reads=[wb, bxn], writes=[pb])
                        P.op(ACT, lambda e, ps=ps, c=c, half=half: e.activation(out=vtok[0:CS, c, half * 256:(half + 1) * 256], in_=ps[0:CS, 0:256], func=AF.Copy), reads=[pb], writes=[bvtok])
                def evg(jh, ps, pb):
                    P.op(ACT, lambda e, jh=jh, ps=ps: e.activation(out=gs[:, jh, 0:T], in_=ps[:, 0:T], func=AF.Silu), reads=[pb], writes=[bgs])
                proj_fm(W[:, 3 * D + hg * 512:3 * D + (hg + 1) * 512], KC, 512, xn, bxn, T, evg)
                if not sample:
                    for jh in range(4):
                        P.op(ACT, lambda e, jh=jh, h=h0 + jh: e.activation(out=Sbf[:, jh, :], in_=Sst[:, h, :], func=AF.Copy), reads=[bSst], writes=[bSbf4[jh]])
                for c in range(nch):
                    for jh in range(4):
                        h = h0 + jh
                        stmp, bstmp = stmp4[:, jh, :], bstmp4[jh]
                        kTs, bkTs = kTs4[:, jh, :], bkTs4[jh]
                        PT, bPT = PT4[:, jh, :], bPT4[jh]
                        bSbf = bSbf4[jh]
                        cs = slice(c * CS, (c + 1) * CS)
                        if sample:
                            sl = slctr[0] % NSL
                            slctr[0] += 1
                            P.dma(SP, Sld[:, sl, :], hgst[j, c, h], writes=[bSld[sl]])
                            P.op(ACT, lambda e, jh=jh, sl=sl: e.activation(out=Sbf[:, jh, :], in_=Sld[:, sl, :], func=AF.Copy), reads=[bSld[sl]], writes=[bSbf])
                            Scur, bScur = Sld[:, sl, :], bSld[sl]
                        else:
                            Scur, bScur = Sst[:, h, :], bSst
                        ps1, pb1 = next_ps()
                        P.op(PE, lambda e, ps1=ps1, jh=jh, cs=cs: e.matmul(ps1[0:CS, 0:CS], lhsT=kt[:, jh, cs], rhs=qt[:, jh, cs], start=True, stop=True), reads=[bkt, bqt], writes=[pb1])
                        P.op(DVE, lambda e, ps1=ps1, PT=PT: e.tensor_tensor(out=PT[0:CS, 0:CS], in0=ps1[0:CS, 0:CS], in1=maskT[0:CS, 0:CS], op=ALU.mult), reads=[pb1, bconst], writes=[bPT])
                        pv = psb[7][:].bitcast(BF16)
                        P.op(PE, lambda e, jh=jh, cs=cs, pv=pv: e.transpose(pv[0:CS, jh * 128:(jh + 1) * 128], kt[:, jh, cs], identb[:]), reads=[bkt, bconst], writes=[bps[7]])
                        P.op(ACT, lambda e, pv=pv, kTs=kTs, jh=jh: e.activation(out=kTs[0:CS, :], in_=pv[0:CS, jh * 128:(jh + 1) * 128], func=AF.Copy), reads=[bps[7]], writes=[bkTs])
                        ps2, pb2 = next_ps()
                        P.op(PE, lambda e, ps2=ps2, c=c, jh=jh, PT=PT: e.matmul(ps2[:, 0:CS], lhsT=vtok[0:CS, c, jh * 128:(jh + 1) * 128], rhs=PT[0:CS, 0:CS], start=True, stop=False), reads=[bvtok, bPT], writes=[pb2])
                        P.op(PE, lambda e, ps2=ps2, cs=cs, jh=jh: e.matmul(ps2[:, 0:CS], lhsT=Sbf[:, jh, :], rhs=qt[:, jh, cs], start=False, stop=True), reads=[bSbf, bqt], writes=[pb2])
                        P.op(ACT, lambda e, ps2=ps2, jh=jh, cs=cs: e.activation(out=ob[:, jh, cs], in_=ps2[:, 0:CS], func=AF.Copy), reads=[pb2], writes=[bob])
                        ps3, pb3 = next_ps()
                        P.op(PE, lambda e, ps3=ps3, c=c, jh=jh, kTs=kTs: e.matmul(ps3[:, 0:128], lhsT=kTs[0:CS, :], rhs=vtok[0:CS, c, jh * 128:(jh + 1) * 128], start=True, stop=True), reads=[bkTs, bvtok], writes=[pb3])
                        P.op(DVE, lambda e, ps3=ps3, Scur=Scur, stmp=stmp: e.tensor_tensor(out=stmp, in0=Scur, in1=ps3[:, 0:128], op=ALU.add), reads=[pb3, bScur], writes=[bstmp])
                        P.op(DVE, lambda e, Scur=Scur, jh=jh, c=c, stmp=stmp: e.tensor_scalar(out=Scur, in0=stmp, scalar1=lgf[:, jh, (c + 1) * CS - 1:(c + 1) * CS], scalar2=None, op0=ALU.mult), reads=[bstmp, blgf], writes=[bScur])
                        if sample:
                            P.dma(SP, o_hgs[j, c, h], Scur, reads=[bScur])
                        else:
                            P.op(ACT, lambda e, jh=jh, Scur=Scur: e.activation(out=Sbf[:, jh, :], in_=Scur, func=AF.Copy), reads=[bScur], writes=[bSbf])
                P.op(ACT, lambda e: e.activation(out=osq[:, :, 0:T], in_=ob[:, :, 0:T], func=AF.Square), reads=[bob], writes=[bosq])
                for jh in range(4):
                    h = h0 + jh
                    ps, pb = next_ps()
                    P.op(PE, lambda e, ps=ps, jh=jh: e.matmul(ps[:, 0:T], lhsT=onesb[:], rhs=osq[:, jh, 0:T], start=True, stop=True), reads=[bosq, bconst], writes=[pb])
                    P.op(DVE, lambda e, ps=ps: e.tensor_scalar(out=rr[:, 0:T], in0=ps[:, 0:T], scalar1=1.0 / 128.0, scalar2=EPS, op0=ALU.mult, op1=ALU.add), reads=[pb], writes=[brr])
                    P.op(ACT, lambda e: e.activation(out=rr[:, 0:T], in_=rr[:, 0:T], func=AF.Sqrt), reads=[brr], writes=[brr])
                    P.op(DVE, lambda e: e.reciprocal(out=rr[:, 0:T], in_=rr[:, 0:T]), reads=[brr], writes=[brr])
                    P.op(DVE, lambda e, jh=jh, h=h: e.scalar_tensor_tensor(out=ob[:, jh, 0:T], in0=ob[:, jh, 0:T], scalar=vec[:, V_GN + j, h:h + 1], in1=rr[:, 0:T], op0=ALU.mult, op1=ALU.mult), reads=[bob, brr, bconst], writes=[bob])
                    P.op(DVE, lambda e, jh=jh: e.tensor_tensor(out=og[:, jh, 0:T], in0=ob[:, jh, 0:T], in1=gs[:, jh, 0:T], op=ALU.mult), reads=[bob, bgs], writes=[bog])
                proj_fm(hg_w_out[j][hg * 512:(hg + 1) * 512, :], 4, D, og, bog, T, add_to_x(T))
            if not sample:
                P.dma(SP, hgscr[j], Sst, reads=[bSst], writes=[bhgscr[j]])
                if last:
                    P.dma(SP, o_hgp[j].rearrange("h k v -> k h v"), Sst, reads=[bSst])

        def run_pass(T, src, dst, sample, first, last):
            phase()
            xt = AFa.take(D)
            bxt = P.buf("xt")
            for tt in range(T // 128):
                P.dma(SP, xt, src[tt * 128:(tt + 1) * 128, :], writes=[bxt])
                transpose_blocks(lambda i, tt=tt: x[:, i, tt * 128:(tt + 1) * 128], lambda i: xt[:, i * 128:(i + 1) * 128], KC, bxt, bx)
            for layer in range(4):
                j = layer // 2
                if layer % 2 == 0:
                    if CFG["s5"]:
                        s5_mixer(j, T, sample, last)
                else:
                    if CFG["hg"]:
                        hgrn_mixer(j, layer, T, sample, last, first)
                if CFG["xa"]:
                    xattn(layer, T, sample, first and not sample)
                if CFG["ffn"]:
                    ffn(layer, T)
            phase()
            yf = AFa.take(KC * TP).rearrange("p (k t) -> p k t", k=KC)
            byf = P.buf("yf")
            rmsnorm(T, V_FIN, xn, bxn, out_f32=yf, out_f32_buf=byf)
            yt = AFa.take(D)
            byt = P.buf("yt")
            for tt in range(T // 128):
                transpose_blocks(lambda i: yt[:, i * 128:(i + 1) * 128], lambda i, tt=tt: yf[:, i, tt * 128:(tt + 1) * 128], KC, byf, byt)
                P.dma(SP, dst[tt * 128:(tt + 1) * 128, :], yt, reads=[byt])

        for p in range(CFG["npass"]):
            run_pass(TP, xp[p * TP:(p + 1) * TP, :], o_yp[p * TP:(p + 1) * TP, :], False, p == 0, p == CFG["npass"] - 1)
        if CFG["sample"]:
            run_pass(128, xsm, o_ys, True, False, False)
        print("sbuf remaining", nc.sbuf_bytes_remaining)
        stats = P.emit()
        print("ops", stats)
    return nc


def _vec_layout(v):
    return np.ascontiguousarray(v.reshape(KC, 128).T)


_PROG = None


def kernel(_return_maps=False, **inp):
    f = np.float32
    g = {k: np.asarray(v) for k, v in inp.items()}
    vec = np.zeros((128, NV, 16), f)
    for l in range(4):
        vec[:, V_MIX + l] = _vec_layout(g["norm_mix"][l])
        vec[:, V_XA + l] = _vec_layout(g["norm_xattn"][l])
        vec[:, V_MEM + l] = _vec_layout(g["norm_mem_in"][l])
        vec[:, V_FFN + l] = _vec_layout(g["norm_ffn"][l])
        vec[:, V_LB + l] = _vec_layout(g["hg_lower_bounds"][l])
    vec[:, V_FIN] = _vec_layout(g["norm_final"])
    for j in range(2):
        vec[:, V_S5D + j] = _vec_layout(g["s5_d"][j])
        vec[:, V_GN + j] = _vec_layout(g["hg_g_norm"][j])
    ident = np.eye(128, dtype=f)
    maskT = np.triu(np.ones((64, 64), f))
    taus = np.zeros((128, 2, 16), f)
    taus[:, 0, :] = np.arange(1, 17, dtype=f)[None]
    taus[:, 1, :] = np.tile(np.arange(1, 9, dtype=f), 2)[None]
    lam_sl = np.zeros((2, 128, 3, 64), f)
    lam_pl = np.zeros((2, 128, 3, 8192), f)
    b_pl = np.zeros((2, 128, 2, 8192), f)
    c_pl = np.zeros((2, 128, 2, 8192), f)
    gidx = np.zeros((16, 4, 2), np.int64)
    for gc in range(16):
        for jj in range(4):
            for gh in range(2):
                gidx[gc, jj, gh] = 8 * gc + 4 * gh + jj
    for j in range(2):
        ld = np.broadcast_to(g["s5_log_dt"][j][:, None], (128, 64))
        for i, src in enumerate((g["s5_lam_re"][j], g["s5_lam_im"][j], ld)):
            a = src[gidx]
            lam_sl[j, :, i, :] = a.transpose(3, 2, 0, 1).reshape(128, 64)
            pl = a.transpose(0, 1, 3, 2).reshape(8192)
            lam_pl[j, :, i, :] = pl[None, :]
        for i, src in enumerate((g["s5_b_re"][j], g["s5_b_im"][j])):
            a = src[gidx]
            t = np.zeros((8, 16, 16, 4, 64, 2), f)
            for jj in range(4):
                for gh in range(2):
                    t[4 * gh + jj, :, :, jj, :, gh] = a[:, jj, gh].transpose(2, 0, 1)
            b_pl[j, :, i, :] = t.reshape(128, 8192)
        for i, src in enumerate((g["s5_c_re"][j], g["s5_c_im"][j])):
            a = src[gidx]
            t = np.zeros((64, 2, 16, 4, 8, 16), f)
            for jj in range(4):
                for gh in range(2):
                    t[:, gh, :, jj, 4 * gh + jj, :] = a[:, jj, gh].transpose(2, 0, 1)
            c_pl[j, :, i, :] = t.reshape(128, 8192)

    def s5_state_layout(re, im):
        out = np.zeros((128, 2, 64, re.shape[0]), f)
        for i, s in enumerate((re, im)):
            a = s[:, gidx]
            out[:, i] = a.transpose(4, 3, 1, 2, 0).reshape(128, 64, re.shape[0])
        return out

    def s5_state_unlayout(arr):
        B = arr.shape[-1]
        res = []
        for i in range(2):
            a = arr[:, i].reshape(64, 2, 16, 4, B)
            o = np.zeros((B, 128, 64), f)
            o[:, gidx] = a.transpose(4, 2, 3, 1, 0)
            res.append(o)
        return res

    in_maps = []
    for c in range(NCORE):
        s = c % 4
        sl = slice(c * SPC, (c + 1) * SPC)
        s5i = np.stack([s5_state_layout(g["state_s5_re"][j, sl], g["state_s5_im"][j, sl]) for j in range(2)])
        in_maps.append(dict(
            xp=np.ascontiguousarray(g["x_prompt"][s]), xsm=np.ascontiguousarray(g["x_sample"][sl].reshape(128, D)),
            memp=np.ascontiguousarray(g["mem_prompt"][s]),
            ck=np.ascontiguousarray(g["cache_mem_k"][:, sl].reshape(4, SPC, NMEM, D)),
            cv=np.ascontiguousarray(g["cache_mem_v"][:, sl].reshape(4, SPC, NMEM, D)),
            hgst=np.ascontiguousarray(g["state_hgrn"][:, sl]),
            vecs=vec, ident=ident, maskT=maskT, taus=taus, lam_sl=lam_sl, lam_pl=lam_pl, b_pl=b_pl, c_pl=c_pl, s5init=s5i,
            s5_w_glu=g["s5_w_glu"], hg_w_in=g["hg_w_in"], hg_w_out=g["hg_w_out"],
            x_w_q=g["x_w_q"], x_w_k=g["x_w_k"], x_w_v=g["x_w_v"], x_w_o=g["x_w_o"],
            ffn_w_in=g["ffn_w_in"], ffn_w_out=g["ffn_w_out"]))
    if _return_maps:
        return in_maps
    global _PROG
    if _PROG is None:
        _PROG = build_program()
    nc = _PROG
    res = run_bass_kernel_spmd(nc, in_maps, core_ids=list(range(NCORE))).results
    y_prompt = np.stack([res[s]["o_yp"] for s in range(4)]).astype(f)
    y_sample = np.concatenate([res[c]["o_ys"].reshape(SPC, 8, D) for c in range(NCORE)]).astype(f)
    s5p = [[s5_state_unlayout(res[s]["o_s5p"][j][..., None]) for s in range(4)] for j in range(2)]
    s5_re_p = np.stack([np.concatenate([s5p[j][s][0] for s in range(4)]) for j in range(2)])
    s5_im_p = np.stack([np.concatenate([s5p[j][s][1] for s in range(4)]) for j in range(2)])
    hg_p = np.stack([np.stack([res[s]["o_hgp"][j] for s in range(4)]) for j in range(2)])
    mk = np.stack([np.stack([res[s]["o_mk"][l].reshape(NMEM, 4, 512) for s in range(4)]) for l in range(4)])
    mv = np.stack([np.stack([res[s]["o_mv"][l].reshape(NMEM, 4, 512) for s in range(4)]) for l in range(4)])
    s5s = [[s5_state_unlayout(res[c]["o_s5s"][j]) for c in range(NCORE)] for j in range(2)]
    s5_re_s = np.stack([np.concatenate([s5s[j][c][0] for c in range(NCORE)]) for j in range(2)])
    s5_im_s = np.stack([np.concatenate([s5s[j][c][1] for c in range(NCORE)]) for j in range(2)])
    hg_s = np.stack([np.concatenate([res[c]["o_hgs"][j] for c in range(NCORE)]) for j in range(2)])
    return (y_prompt, y_sample, s5_re_p.astype(f), s5_im_p.astype(f), hg_p.astype(f), mk.astype(f), mv.astype(f),
            s5_re_s.astype(f), s5_im_s.astype(f), hg_s.astype(f))
```

```python
import math
import numpy as np
from contextlib import ExitStack
import concourse.bass as bass
import concourse.mybir as mybir
from concourse.bass_utils import run_bass_kernel_spmd

F32 = mybir.dt.float32
BF16 = mybir.dt.bfloat16
I32 = mybir.dt.int32
AF = mybir.ActivationFunctionType
ALU = mybir.AluOpType
PE, ACT, DVE, POOL, SP = "tensor", "scalar", "vector", "gpsimd", "sync"

D = 2048
KC = 16
DFF = 5632
NMEM = 256
EPS = 1e-6
NCORE = 8
SPC = 16
TP = 512
NPASS = 4
TWO_PI = 2.0 * math.pi
CFG = dict(s5lvl=9, npass=4, s5=True, hg=True, xa=True, ffn=True, sample=True, setup=True)


class Buf:
    __slots__ = ("name", "last_w", "readers", "sem", "cnt")

    def __init__(self, name):
        self.name = name
        self.last_w = None
        self.readers = []
        self.sem = None
        self.cnt = 0


class Op:
    __slots__ = ("eng", "fn", "deps", "dma", "ev", "signal", "chain")

    def __init__(self, eng, fn, dma):
        self.eng = eng
        self.fn = fn
        self.deps = []
        self.dma = dma
        self.ev = None
        self.signal = False
        self.chain = False


class Prog:
    def __init__(self, nc, stack, n_dma_sems=80):
        self.nc = nc
        self.stack = stack
        self.ops = []
        self.esem = {e: [stack.enter_context(nc.semaphore("es%d_" % k + e)) for k in range(4)] for e in (PE, ACT, DVE, POOL)}
        self.free_sems = []
        for i in range(n_dma_sems):
            try:
                self.free_sems.append([stack.enter_context(nc.semaphore("ds%d" % i)), 0])
            except KeyError:
                break
        print("dma sems", len(self.free_sems))
        self.all_chans = list(self.free_sems)
        self.live = []
        self.pending = {e: [] for e in (PE, ACT, DVE, POOL, SP)}
        self.dma_bufs = []

    def buf(self, name="b"):
        return Buf(name)

    def sb(self, name, shape, dtype):
        return self.stack.enter_context(self.nc.sbuf_tensor(name, list(shape), dtype))

    def ps(self, name, shape, dtype=F32):
        return self.stack.enter_context(self.nc.psum_tensor(name, list(shape), dtype))

    def op(self, eng, fn, reads=(), writes=(), dma=False, chain=False):
        o = Op(eng, fn, dma)
        o.chain = chain
        deps = list(self.pending[eng])
        self.pending[eng] = []
        for r in reads:
            if r.last_w is not None:
                deps.append(r.last_w)
        for w in writes:
            if w.last_w is not None:
                deps.append(w.last_w)
            lastr = {}
            for rd in w.readers:
                if rd.dma:
                    deps.append(rd)
                else:
                    lastr[rd.eng] = rd
            deps.extend(lastr.values())
        for r in reads:
            r.readers.append(o)
        for w in writes:
            w.last_w = o
            w.readers = []
        if dma:
            tgt = (list(writes) + list(reads))[0]
            if tgt.sem is not None and tgt.sem[1] >= 1800:
                tgt.sem = None
                if tgt in self.live:
                    self.live.remove(tgt)
            if tgt.sem is None:
                tgt.sem = self.free_sems.pop()
                tgt.cnt = tgt.sem[1]
                self.live.append(tgt)
            tgt.cnt += 1
            tgt.sem[1] = tgt.cnt
            o.ev = (tgt.sem[0], 16 * tgt.cnt)
        seen = set()
        for d in deps:
            if d is o or id(d) in seen:
                continue
            seen.add(id(d))
            if (not d.dma) and d.eng == PE and eng == PE and not dma:
                continue
            if chain and (not d.dma) and d.eng == eng and d.chain:
                continue
            o.deps.append(d)
            d.signal = True
        self.ops.append(o)
        return o

    def dma(self, eng, out, in_, reads=(), writes=(), **kw):
        return self.op(eng, lambda e: e.dma_start(out=out, in_=in_, **kw), reads=reads, writes=writes, dma=True)

    def barrier(self):
        last = {}
        lastd = {}
        for o in self.ops:
            if o.dma:
                lastd[id(o.ev[0])] = o
            else:
                last[o.eng] = o
        deps = list(last.values()) + list(lastd.values())
        for e in self.pending:
            self.pending[e] = list(deps)
        keep = []
        for b in self.live:
            if getattr(b, "name", "").startswith("keep"):
                keep.append(b)
            else:
                if b.sem[1] < 1700:
                    self.free_sems.append(b.sem)
                b.sem = None
        self.live = keep

    def emit(self):
        nc = self.nc
        cnt = {e: 0 for e in self.esem}
        for o in self.ops:
            if (not o.dma) and o.signal:
                cnt[o.eng] += 1
                ep, v = divmod(cnt[o.eng] - 1, 30000)
                o.ev = (self.esem[o.eng][ep], v + 1)
        print("signal counts", cnt)
        per = {e: [] for e in (PE, ACT, DVE, POOL, SP)}
        for o in self.ops:
            per[o.eng].append(o)
        finals = [(c[0], 16 * c[1]) for c in self.all_chans if c[1] > 0]

        def run(engname, eng):
            waited = {}
            for o in per[engname]:
                for d in o.deps:
                    sem, val = d.ev
                    k = id(sem)
                    if waited.get(k, 0) >= val:
                        continue
                    eng.wait_ge(sem, val)
                    waited[k] = val
                ins = o.fn(eng)
                if o.dma:
                    ins.then_inc(o.ev[0], 16)
                elif o.signal:
                    ins.then_inc(o.ev[0], 1)
            if engname == SP:
                for sem, val in finals:
                    eng.wait_ge(sem, val)

        with nc.Block() as block:
            @block.tensor
            def _(e):
                run(PE, e)

            @block.scalar
            def _(e):
                run(ACT, e)

            @block.vector
            def _(e):
                run(DVE, e)

            @block.gpsimd
            def _(e):
                run(POOL, e)

            @block.sync
            def _(e):
                run(SP, e)
        return {e: len(per[e]) for e in per}


class Arena:
    def __init__(self, P, name, n, dtype):
        self.t = P.sb(name, [128, n], dtype)
        self.n = n
        self.off = 0

    def reset(self):
        self.off = 0

    def take(self, n):
        assert self.off + n <= self.n, (self.off, n, self.n)
        v = self.t[:, self.off:self.off + n]
        self.off += n
        return v


V_MIX, V_XA, V_MEM, V_FFN, V_FIN, V_S5D, V_GN, V_LB = 0, 4, 8, 12, 16, 17, 19, 21
NV = 25


def build_program():
    nc = bass.Bass("TRN2", target_bir_lowering=False)

    def din(name, shape):
        return nc.dram_tensor(name, list(shape), F32, kind="ExternalInput").ap()

    def dout(name, shape):
        return nc.dram_tensor(name, list(shape), F32, kind="ExternalOutput").ap()

    xp = din("xp", [NPASS * TP, D])
    xsm = din("xsm", [128, D])
    memp = din("memp", [NMEM, D])
    ck = din("ck", [4, SPC, NMEM, D])
    cv = din("cv", [4, SPC, NMEM, D])
    hgst = din("hgst", [2, SPC, 16, 128, 128])
    vecs = din("vecs", [128, NV, 16])
    ident_d = din("ident", [128, 128])
    maskT_d = din("maskT", [64, 64])
    taus_d = din("taus", [128, 2, 16])
    lam_sl = din("lam_sl", [2, 128, 3, 64])
    lam_pl = din("lam_pl", [2, 128, 3, 8192])
    b_pl = din("b_pl", [2, 128, 2, 8192])
    c_pl = din("c_pl", [2, 128, 2, 8192])
    s5init = din("s5init", [2, 128, 2, 64, SPC])
    w_glu = din("s5_w_glu", [2, D, D])
    hg_w_in = din("hg_w_in", [2, D, 4 * D])
    hg_w_out = din("hg_w_out", [2, D, D])
    x_w_q = din("x_w_q", [4, D, D])
    x_w_k = din("x_w_k", [4, D, D])
    x_w_v = din("x_w_v", [4, D, D])
    x_w_o = din("x_w_o", [4, D, D])
    ffn_w_in = din("ffn_w_in", [4, D, 2 * DFF])
    ffn_w_out = din("ffn_w_out", [4, DFF, D])

    o_yp = dout("o_yp", [NPASS * TP, D])
    o_ys = dout("o_ys", [128, D])
    o_s5p = dout("o_s5p", [2, 128, 2, 64])
    o_hgp = dout("o_hgp", [2, 16, 128, 128])
    o_mk = dout("o_mk", [4, NMEM, D])
    o_mv = dout("o_mv", [4, NMEM, D])
    o_s5s = dout("o_s5s", [2, 128, 2, 64, SPC])
    o_hgs = dout("o_hgs", [2, SPC, 16, 128, 128])

    with ExitStack() as st:
        P = Prog(nc, st)
        x = P.sb("x", [128, KC, TP], F32)
        bx = P.buf("x")
        xn = P.sb("xn", [128, KC, TP], BF16)
        bxn = P.buf("xn")
        AFa = Arena(P, "AF", 16384, F32)
        AHa = Arena(P, "AH", 24576, BF16)
        NSLOT = 4
        WSL = 4096
        wslots = [P.sb("w%d" % i, [128, WSL], BF16) for i in range(NSLOT)]
        wbufs = [P.buf("keepw%d" % i) for i in range(NSLOT)]
        wctr = [0]
        ident = P.sb("ident_sb", [128, 128], F32)
        identb = P.sb("identb", [128, 128], BF16)
        onesb = P.sb("onesb", [128, 128], BF16)
        maskT = P.sb("maskT_sb", [64, 64], F32)
        vec = P.sb("vec", [128, NV, 16], F32)
        lbv = P.sb("lbv", [128, 4, 16], F32)
        oml = P.sb("oml", [128, 4, 16], F32)
        bconst = P.buf("keepconst")
        s5S = P.sb("s5S", [128, 2, 2, 64], F32)
        bs5S = [P.buf("s5S0"), P.buf("s5S1")]
        rstd = P.sb("rstd", [128, TP], F32)
        brstd = P.buf("rstd")
        psb = [P.ps("ps%d" % i, [128, 512], F32) for i in range(8)]
        bps = [P.buf("ps%d" % i) for i in range(8)]
        psctr = [0]

        def next_ps():
            i = psctr[0] % 6
            psctr[0] += 1
            return psb[i], bps[i]

        def phase():
            P.barrier()
            AFa.reset()
            AHa.reset()

        P.dma(SP, ident[:], ident_d, writes=[bconst])
        P.dma(SP, maskT[:], maskT_d, writes=[bconst])
        taus = P.sb("taus_sb", [128, 2, 16], F32)
        P.dma(SP, taus[:], taus_d, writes=[bconst])
        P.dma(SP, vec[:], vecs, writes=[bconst])
        P.op(DVE, lambda e: e.tensor_copy(out=identb[:], in_=ident[:]), reads=[bconst], writes=[bconst])
        P.op(DVE, lambda e: e.memset(onesb[:], 1.0), writes=[bconst])
        P.op(DVE, lambda e: e.memset(s5S[:], 0.0), writes=bs5S)
        P.op(ACT, lambda e: e.activation(out=lbv[:], in_=vec[:, V_LB:V_LB + 4, :], func=AF.Exp), reads=[bconst], writes=[bconst])
        P.op(DVE, lambda e: e.tensor_tensor(out=oml[:, 0, :], in0=lbv[:, 0, :], in1=lbv[:, 1, :], op=ALU.add), reads=[bconst], writes=[bconst])
        P.op(DVE, lambda e: e.tensor_tensor(out=oml[:, 1, :], in0=lbv[:, 2, :], in1=lbv[:, 3, :], op=ALU.add), reads=[bconst], writes=[bconst])
        P.op(DVE, lambda e: e.tensor_tensor(out=oml[:, 0, :], in0=oml[:, 0, :], in1=oml[:, 1, :], op=ALU.add), reads=[bconst], writes=[bconst])
        P.op(DVE, lambda e: e.reciprocal(out=oml[:, 0, :], in_=oml[:, 0, :]), reads=[bconst], writes=[bconst])
        for l in range(4):
            P.op(DVE, lambda e, l=l: e.tensor_tensor(out=lbv[:, l, :], in0=lbv[:, l, :], in1=oml[:, 0, :], op=ALU.mult), reads=[bconst], writes=[bconst])
        P.op(DVE, lambda e: e.memset(lbv[:, 0, :], 0.0), writes=[bconst])
        P.op(DVE, lambda e: e.tensor_tensor(out=lbv[:, 2, :], in0=lbv[:, 2, :], in1=lbv[:, 1, :], op=ALU.add), reads=[bconst], writes=[bconst])
        P.op(DVE, lambda e: e.tensor_tensor(out=lbv[:, 3, :], in0=lbv[:, 3, :], in1=lbv[:, 2, :], op=ALU.add), reads=[bconst], writes=[bconst])
        P.op(DVE, lambda e: e.tensor_scalar(out=oml[:], in0=lbv[:], scalar1=-1.0, scalar2=1.0, op0=ALU.mult, op1=ALU.add), reads=[bconst], writes=[bconst])

        def wload(src2d, kcn, ncols):
            i = wctr[0] % NSLOT
            wctr[0] += 1
            assert kcn * ncols <= WSL
            view = wslots[i][:, 0:kcn * ncols].rearrange("p (k n) -> p k n", k=kcn)
            P.dma(POOL, view, src2d.rearrange("(k p) n -> p k n", p=128), writes=[wbufs[i]])
            return view, wbufs[i]

        def proj_fm(W2d, kcn, ncols_total, src, srcbuf, T, evac, colblk=256):
            cb = min(colblk, WSL // kcn // 128 * 128)
            j = 0
            for c0 in range(0, ncols_total, cb):
                cw = min(cb, ncols_total - c0)
                wt, wb = wload(W2d[:, c0:c0 + cw], kcn, cw)
                for jj in range(cw // 128):
                    ps, pb = next_ps()
                    for kc in range(kcn):
                        P.op(PE, lambda e, ps=ps, wt=wt, kc=kc, jj=jj: e.matmul(ps[:, 0:T], lhsT=wt[:, kc, jj * 128:(jj + 1) * 128], rhs=src[:, kc, 0:T], start=(kc == 0), stop=(kc == kcn - 1)),
                             reads=[wb, srcbuf], writes=[pb])
                    evac(j, ps, pb)
                    j += 1

        def rmsnorm(T, vslot, out_bf, out_buf, src=None, src_buf=None, out_f32=None, out_f32_buf=None):
            src = x if src is None else src
            src_buf = bx if src_buf is None else src_buf
            P.op(ACT, lambda e: e.activation(out=out_bf[:, :, 0:T], in_=src[:, :, 0:T], func=AF.Square), reads=[src_buf], writes=[out_buf])
            ps, pb = psb[6], bps[6]
            for kc in range(KC):
                P.op(PE, lambda e, kc=kc: e.matmul(ps[:, 0:T], lhsT=onesb[:], rhs=out_bf[:, kc, 0:T], start=(kc == 0), stop=(kc == KC - 1)), reads=[out_buf, bconst], writes=[pb])
            P.op(DVE, lambda e: e.tensor_scalar(out=rstd[:, 0:T], in0=ps[:, 0:T], scalar1=1.0 / D, scalar2=EPS, op0=ALU.mult, op1=ALU.add), reads=[pb], writes=[brstd])
            P.op(ACT, lambda e: e.activation(out=rstd[:, 0:T], in_=rstd[:, 0:T], func=AF.Sqrt), reads=[brstd], writes=[brstd])
            P.op(DVE, lambda e: e.reciprocal(out=rstd[:, 0:T], in_=rstd[:, 0:T]), reads=[brstd], writes=[brstd])
            for kc in range(KC):
                if out_f32 is not None:
                    P.op(DVE, lambda e, kc=kc: e.scalar_tensor_tensor(out=out_f32[:, kc, 0:T], in0=src[:, kc, 0:T], scalar=vec[:, vslot, kc:kc + 1], in1=rstd[:, 0:T], op0=ALU.mult, op1=ALU.mult),
                         reads=[src_buf, brstd, bconst], writes=[out_f32_buf])
                    P.op(POOL, lambda e, kc=kc: e.tensor_copy(out=out_bf[:, kc, 0:T], in_=out_f32[:, kc, 0:T]), reads=[out_f32_buf], writes=[out_buf])
                else:
                    P.op(DVE, lambda e, kc=kc: e.scalar_tensor_tensor(out=out_bf[:, kc, 0:T], in0=src[:, kc, 0:T], scalar=vec[:, vslot, kc:kc + 1], in1=rstd[:, 0:T], op0=ALU.mult, op1=ALU.mult),
                         reads=[src_buf, brstd, bconst], writes=[out_buf])

        def add_to_x(T):
            def ev(j, ps, pb):
                P.op(DVE, lambda e, j=j, ps=ps: e.tensor_tensor(out=x[:, j, 0:T], in0=x[:, j, 0:T], in1=ps[:, 0:T], op=ALU.add), reads=[pb, bx], writes=[bx])
            return ev

        def ffn(layer, T):
            phase()
            rmsnorm(T, V_FFN + layer, xn, bxn)
            hid = AHa.take(44 * TP).rearrange("p (k t) -> p k t", k=44)
            bhid = P.buf("hid")
            sg = AFa.take(2 * TP).rearrange("p (a t) -> p a t", a=2)
            bsg = [P.buf("sg0"), P.buf("sg1")]
            W = ffn_w_in[layer]
            for jp in range(22):
                wt, wb = wload(W[:, jp * 256:(jp + 1) * 256], KC, 256)
                wt2, wb2 = wload(W[:, DFF + jp * 256:DFF + (jp + 1) * 256], KC, 256)
                for sub in range(2):
                    j = jp * 2 + sub
                    psg, pbg = next_ps()
                    psu, pbu = next_ps()
                    for kc in range(KC):
                        P.op(PE, lambda e, kc=kc, wt=wt, psg=psg, sub=sub: e.matmul(psg[:, 0:T], lhsT=wt[:, kc, sub * 128:(sub + 1) * 128], rhs=xn[:, kc, 0:T], start=(kc == 0), stop=(kc == KC - 1)), reads=[wb, bxn], writes=[pbg])
                    for kc in range(KC):
                        P.op(PE, lambda e, kc=kc, wt2=wt2, psu=psu, sub=sub: e.matmul(psu[:, 0:T], lhsT=wt2[:, kc, sub * 128:(sub + 1) * 128], rhs=xn[:, kc, 0:T], start=(kc == 0), stop=(kc == KC - 1)), reads=[wb2, bxn], writes=[pbu])
                    s_ = j % 2
                    P.op(ACT, lambda e, s_=s_, psg=psg: e.activation(out=sg[:, s_, 0:T], in_=psg[:, 0:T], func=AF.Silu), reads=[pbg], writes=[bsg[s_]])
                    P.op(DVE, lambda e, s_=s_, psu=psu, j=j: e.tensor_tensor(out=hid[:, j, 0:T], in0=sg[:, s_, 0:T], in1=psu[:, 0:T], op=ALU.mult), reads=[bsg[s_], pbu], writes=[bhid])
            Wo = ffn_w_out[layer]
            for j in range(KC):
                wa, wab = wload(Wo[0:2816, j * 128:(j + 1) * 128], 22, 128)
                wb_, wbb = wload(Wo[2816:5632, j * 128:(j + 1) * 128], 22, 128)
                ps, pb = next_ps()
                for kc in range(44):
                    wt_, wtb_ = (wa, wab) if kc < 22 else (wb_, wbb)
                    P.op(PE, lambda e, kc=kc, wt_=wt_, ps=ps: e.matmul(ps[:, 0:T], lhsT=wt_[:, kc % 22, :], rhs=hid[:, kc, 0:T], start=(kc == 0), stop=(kc == 43)), reads=[wtb_, bhid], writes=[pb])
                add_to_x(T)(j, ps, pb)

        def transpose_blocks(dst_fn, src_fn, nblk, rbuf, wbuf, dt_bf=False, rows=128):
            idn = identb if dt_bf else ident
            for g0 in range(0, nblk, 4):
                n = min(4, nblk - g0)
                pst, pbt = (psb[7], bps[7])
                pv = pst[:].bitcast(BF16) if dt_bf else pst[:]
                for i in range(n):
                    P.op(PE, lambda e, i=i, g0=g0, pv=pv: e.transpose(pv[:, i * 128:i * 128 + rows], src_fn(g0 + i), idn[0:rows, 0:rows]), reads=[rbuf, bconst], writes=[pbt])
                for i in range(n):
                    P.op(ACT, lambda e, i=i, g0=g0, pv=pv: e.activation(out=dst_fn(g0 + i), in_=pv[:, i * 128:i * 128 + rows], func=AF.Copy), reads=[pbt], writes=[wbuf])

        def xattn(layer, T, sample, write_kv):
            phase()
            rmsnorm(T, V_XA + layer, xn, bxn)
            q = AHa.take(KC * T).rearrange("p (k t) -> p k t", k=KC)
            bq = P.buf("q")

            def evq(j, ps, pb):
                P.op(ACT, lambda e, j=j, ps=ps: e.activation(out=q[:, j, :], in_=ps[:, 0:T], func=AF.Copy), reads=[pb], writes=[bq])
            proj_fm(x_w_q[layer], KC, D, xn, bxn, T, evq)
            kf = AHa.take(KC * NMEM).rearrange("p (k m) -> p k m", k=KC)
            bkf = P.buf("kf")
            vt = AHa.take(2 * D).rearrange("p (a d) -> p a d", a=2)
            bvt = P.buf("vt")
            vt_def, bvt_def = vt, bvt
            pbf = AHa.take(4 * NMEM).rearrange("p (h m) -> p h m", h=4)
            bpbf = P.buf("pbf")
            LP = 8 if sample else 128
            pT = AHa.take(4 * 2 * LP).rearrange("p (h a l) -> p h a l", h=4, a=2)
            bpT = P.buf("pT")
            stat = AFa.take(16).rearrange("p (a b) -> p a b", a=4)
            bstat = P.buf("stat")
            scale = 1.0 / math.sqrt(512.0)

            def attend(l0, L, vt=None, bvt=None):
                vt = vt_def if vt is None else vt
                bvt = bvt_def if bvt is None else bvt
                ps, pb = next_ps()
                ps2, pb2 = next_ps()
                for h in range(4):
                    pss = (ps if h < 2 else ps2)[0:L, (h % 2) * 256:(h % 2) * 256 + 256]
                    pbb = pb if h < 2 else pb2
                    for kk in range(4):
                        P.op(PE, lambda e, pss=pss, h=h, kk=kk: e.matmul(pss, lhsT=q[:, h * 4 + kk, l0:l0 + L], rhs=kf[:, h * 4 + kk, :], start=(kk == 0), stop=(kk == 3)), reads=[bq, bkf], writes=[pbb])
                for h in range(4):
                    pss = (ps if h < 2 else ps2)[0:L, (h % 2) * 256:(h % 2) * 256 + 256]
                    pbb = pb if h < 2 else pb2
                    P.op(DVE, lambda e, pss=pss, h=h: e.reduce_max(out=stat[0:L, h, 0:1], in_=pss, axis=mybir.AxisListType.X), reads=[pbb], writes=[bstat])
                    P.op(DVE, lambda e, h=h: e.tensor_scalar(out=stat[0:L, h, 1:2], in0=stat[0:L, h, 0:1], scalar1=-scale, scalar2=None, op0=ALU.mult), reads=[bstat], writes=[bstat])
                    P.op(ACT, lambda e, pss=pss, h=h: e.activation(out=pbf[0:L, h, :], in_=pss, func=AF.Exp, bias=stat[0:L, h, 1:2], scale=scale, accum_out=stat[0:L, h, 2:3]), reads=[pbb, bstat], writes=[bpbf, bstat])
                    P.op(DVE, lambda e, h=h: e.reciprocal(out=stat[0:L, h, 3:4], in_=stat[0:L, h, 2:3]), reads=[bstat], writes=[bstat])
                    P.op(DVE, lambda e, h=h: e.tensor_scalar(out=pbf[0:L, h, :], in0=pbf[0:L, h, :], scalar1=stat[0:L, h, 3:4], scalar2=None, op0=ALU.mult), reads=[bstat, bpbf], writes=[bpbf])
                pst, pbt = psb[7], bps[7]
                pv = pst[:].bitcast(BF16)
                for h in range(4):
                    for a in range(2):
                        P.op(PE, lambda e, h=h, a=a: e.transpose(pv[:, (h * 2 + a) * 128:(h * 2 + a) * 128 + L], pbf[0:L, h, a * 128:(a + 1) * 128], identb[0:L, 0:L]), reads=[bpbf, bconst], writes=[pbt])
                P.op(ACT, lambda e: e.activation(out=pT[:, :, :, 0:L], in_=pv[:, 0:1024].rearrange("p (h a l) -> p h a l", h=4, a=2)[:, :, :, 0:L], func=AF.Copy), reads=[pbt], writes=[bpT])
                for cg in range(4):
                    po, pob = next_ps()
                    for ci in range(4):
                        c = cg * 4 + ci
                        for a in range(2):
                            P.op(PE, lambda e, c=c, cg=cg, ci=ci, a=a, po=po: e.matmul(po[:, ci * 128:ci * 128 + L], lhsT=vt[:, a, c * 128:(c + 1) * 128], rhs=pT[:, cg, a, 0:L], start=(a == 0), stop=(a == 1)), reads=[bvt, bpT], writes=[pob])
                    P.op(ACT, lambda e, cg=cg, po=po: e.activation(out=xn[:, cg * 4:cg * 4 + 4, l0:l0 + L], in_=po[:, 0:512].rearrange("p (c l) -> p c l", c=4)[:, :, 0:L], func=AF.Copy), reads=[pob], writes=[bxn])

            if not sample:
                mt = AFa.take(2 * D).rearrange("p (a d) -> p a d", a=2)
                bmt = P.buf("mt")
                P.dma(SP, mt, memp.rearrange("(a p) d -> p a d", p=128), writes=[bmt])
                mst = AFa.take(8).rearrange("p (a b) -> p a b", a=2)
                bmst = P.buf("mst")
                msq = AFa.take(D)
                bmsq = P.buf("msq")
                mnb = AFa.take(2 * D).rearrange("p (a d) -> p a d", a=2)
                bmnb = P.buf("mnb")
                gam = AFa.take(D)
                bgam = P.buf("gam")
                dg = AFa.take(128)
                bdg = P.buf("dg")
                for kc in range(KC):
                    P.op(DVE, lambda e, kc=kc: e.tensor_scalar(out=dg, in0=ident[:], scalar1=vec[:, V_MEM + layer, kc:kc + 1], scalar2=None, op0=ALU.mult), reads=[bconst], writes=[bdg])
                    pg, pbg = next_ps()
                    P.op(PE, lambda e, pg=pg: e.matmul(pg[:, 0:128], lhsT=onesf[:], rhs=dg, start=True, stop=True), reads=[bdg, bconst], writes=[pbg])
                    P.op(ACT, lambda e, kc=kc, pg=pg: e.activation(out=gam[:, kc * 128:(kc + 1) * 128], in_=pg[:, 0:128], func=AF.Copy), reads=[pbg], writes=[bgam])
                for a in range(2):
                    P.op(ACT, lambda e, a=a: e.activation(out=msq, in_=mt[:, a, :], func=AF.Square, accum_out=mst[:, a, 0:1]), reads=[bmt], writes=[bmsq, bmst])
                    P.op(DVE, lambda e, a=a: e.tensor_scalar(out=mst[:, a, 1:2], in0=mst[:, a, 0:1], scalar1=1.0 / D, scalar2=EPS, op0=ALU.mult, op1=ALU.add), reads=[bmst], writes=[bmst])
                    P.op(ACT, lambda e, a=a: e.activation(out=mst[:, a, 1:2], in_=mst[:, a, 1:2], func=AF.Sqrt), reads=[bmst], writes=[bmst])
                    P.op(DVE, lambda e, a=a: e.reciprocal(out=mst[:, a, 2:3], in_=mst[:, a, 1:2]), reads=[bmst], writes=[bmst])
                    P.op(DVE, lambda e, a=a: e.scalar_tensor_tensor(out=mnb[:, a, :], in0=mt[:, a, :], scalar=mst[:, a, 2:3], in1=gam, op0=ALU.mult, op1=ALU.mult), reads=[bmt, bmst, bgam], writes=[bmnb])
                mnf = AHa.take(KC * NMEM).rearrange("p (k m) -> p k m", k=KC)
                bmnf = P.buf("mnf")
                transpose_blocks(lambda i: mnf[:, i // 2, (i % 2) * 128:(i % 2) * 128 + 128], lambda i: mnb[:, i % 2, (i // 2) * 128:(i // 2) * 128 + 128], 32, bmnb, bmnf, dt_bf=False)
                def evk(j, ps, pb):
                    P.op(ACT, lambda e, j=j, ps=ps: e.activation(out=kf[:, j, :], in_=ps[:, 0:NMEM], func=AF.Copy), reads=[pb], writes=[bkf])
                proj_fm(x_w_k[layer], KC, D, mnf, bmnf, NMEM, evk)
                ost = AFa.take(D)
                bost = P.buf("ost")
                for which, W, dst in (("k", x_w_k[layer], o_mk), ("v", x_w_v[layer], o_mv)):
                    if which == "k" and not write_kv:
                        continue
                    for c0 in range(0, D, 256):
                        wt, wb = wload(W[:, c0:c0 + 256], KC, 256)
                        for a in range(2):
                            ps, pb = next_ps()
                            for kc in range(KC):
                                P.op(PE, lambda e, ps=ps, wt=wt, kc=kc, a=a: e.matmul(ps[:, 0:256], lhsT=mnf[:, kc, a * 128:(a + 1) * 128], rhs=wt[:, kc, :], start=(kc == 0), stop=(kc == KC - 1)), reads=[wb, bmnf], writes=[pb])
                            if which == "v":
                                P.op(ACT, lambda e, ps=ps, a=a, c0=c0: e.activation(out=vt[:, a, c0:c0 + 256], in_=ps[:, 0:256], func=AF.Copy), reads=[pb], writes=[bvt])
                            if write_kv:
                                P.op(DVE, lambda e, ps=ps, a=a, c0=c0: e.tensor_copy(out=ost[:, 0:256], in_=ps[:, 0:256]), reads=[pb], writes=[bost])
                                P.dma(SP, dst[layer, a * 128:(a + 1) * 128, c0:c0 + 256], ost[:, 0:256], reads=[bost])
                for l0 in range(0, T, 128):
                    attend(l0, 128)
            else:
                kst2 = [AHa.take(2 * D).rearrange("p (a d) -> p a d", a=2) for _ in range(2)]
                bkst2 = [P.buf("kst0"), P.buf("kst1")]
                vt2 = [vt, AHa.take(2 * D).rearrange("p (a d) -> p a d", a=2)]
                bvt2 = [bvt, P.buf("vt1")]
                for b in range(SPC):
                    kst, bkst = kst2[b % 2], bkst2[b % 2]
                    vtb, bvtb = vt2[b % 2], bvt2[b % 2]
                    P.dma(POOL, kst, ck[layer, b].rearrange("(a p) d -> p a d", p=128), writes=[bkst])
                    P.dma(POOL, vtb, cv[layer, b].rearrange("(a p) d -> p a d", p=128), writes=[bvtb])
                    transpose_blocks(lambda i: kf[:, i // 2, (i % 2) * 128:(i % 2) * 128 + 128], lambda i, kst=kst: kst[:, i % 2, (i // 2) * 128:(i // 2) * 128 + 128], 32, bkst, bkf, dt_bf=True)
                    attend(b * 8, 8, vtb, bvtb)
            proj_fm(x_w_o[layer], KC, D, xn, bxn, T, add_to_x(T))

        onesf = P.sb("onesf", [128, 128], F32)
        P.op(DVE, lambda e: e.memset(onesf[:], 1.0), writes=[bconst])

        bbar_d = nc.dram_tensor("bbar_scr", [2, 128, 2, 8192], BF16, kind="Internal").ap()
        cpad_d = nc.dram_tensor("cpad_scr", [2, 128, 2, 8192], BF16, kind="Internal").ap()
        hgscr = nc.dram_tensor("hg_scr", [2, 128, 16, 128], F32, kind="Internal").ap()
        bbbar = [P.buf("bbar0"), P.buf("bbar1")]
        bcpad = [P.buf("cpad0"), P.buf("cpad1")]
        bhgscr = [P.buf("hgscr0"), P.buf("hgscr1")]
        lamS = P.sb("lamS", [128, 2, 5, 64], F32)
        blamS = P.buf("lamS")
        tint = P.sb("tint", [128, 512], I32)
        btint = P.buf("tint")

        def sin_of(out, th, shift, n, tmp, rb, wb, tb):
            P.op(DVE, lambda e: e.tensor_scalar(out=tint[:, 0:n], in0=th, scalar1=1.0 / TWO_PI, scalar2=shift / TWO_PI, op0=ALU.mult, op1=ALU.add), reads=rb, writes=[btint])
            P.op(DVE, lambda e: e.tensor_copy(out=tmp, in_=tint[:, 0:n]), reads=[btint], writes=[tb])
            P.op(DVE, lambda e: e.scalar_tensor_tensor(out=tmp, in0=tmp, scalar=-TWO_PI, in1=th, op0=ALU.mult, op1=ALU.add), reads=rb + [tb], writes=[tb])
            P.op(DVE, lambda e: e.tensor_scalar(out=tmp, in0=tmp, scalar1=shift, scalar2=-3.14159, op0=ALU.add, op1=ALU.max), reads=[tb], writes=[tb])
            P.op(DVE, lambda e: e.tensor_scalar(out=tmp, in0=tmp, scalar1=3.14159, scalar2=None, op0=ALU.min), reads=[tb], writes=[tb])
            P.op(ACT, lambda e: e.activation(out=out, in_=tmp, func=AF.Sin), reads=[tb], writes=wb)

        def lam_bar(lr, li, ld, n, t, rb, tb):
            dt_, th, mag, tmp = t[2], t[3], t[4], t[5]
            P.op(ACT, lambda e: e.activation(out=dt_, in_=ld, func=AF.Exp), reads=rb, writes=[tb])
            P.op(DVE, lambda e: e.tensor_tensor(out=th, in0=li, in1=dt_, op=ALU.mult), reads=rb + [tb], writes=[tb])
            P.op(DVE, lambda e: e.tensor_tensor(out=mag, in0=lr, in1=dt_, op=ALU.mult), reads=rb + [tb], writes=[tb])
            P.op(ACT, lambda e: e.activation(out=mag, in_=mag, func=AF.Exp), reads=[tb], writes=[tb])
            sin_of(t[1], th, 0.0, n, tmp, [tb], [tb], tb)
            sin_of(t[0], th, math.pi / 2.0, n, tmp, [tb], [tb], tb)
            P.op(DVE, lambda e: e.tensor_tensor(out=t[0], in0=t[0], in1=mag, op=ALU.mult), reads=[tb], writes=[tb])
            P.op(DVE, lambda e: e.tensor_tensor(out=t[1], in0=t[1], in1=mag, op=ALU.mult), reads=[tb], writes=[tb])

        def s5_setup():
            for j in range(2):
                phase()
                ls = AFa.take(3 * 64).rearrange("p (a n) -> p a n", a=3)
                bls = P.buf("ls")
                P.dma(SP, ls, lam_sl[j], writes=[bls])
                t = [AFa.take(64) for _ in range(6)]
                bt = P.buf("t")
                lam_bar(ls[:, 0, :], ls[:, 1, :], ls[:, 2, :], 64, t, [bls], bt)
                P.op(DVE, lambda e, j=j, t=t: e.tensor_copy(out=lamS[:, j, 0, :], in_=t[0]), reads=[bt], writes=[blamS])
                P.op(DVE, lambda e, j=j, t=t: e.tensor_copy(out=lamS[:, j, 1, :], in_=t[1]), reads=[bt], writes=[blamS])
                P.op(DVE, lambda e, j=j, t=t: e.tensor_scalar(out=lamS[:, j, 2, :], in0=t[1], scalar1=-1.0, scalar2=None, op0=ALU.mult), reads=[bt], writes=[blamS])
                P.op(DVE, lambda e, j=j, t=t: e.tensor_copy(out=lamS[:, j, 3, :], in_=t[4]), reads=[bt], writes=[blamS])
                P.op(DVE, lambda e, j=j, t=t: e.tensor_copy(out=lamS[:, j, 4, :], in_=t[3]), reads=[bt], writes=[blamS])
                NB = 512
                lp = AFa.take(3 * NB).rearrange("p (a n) -> p a n", a=3)
                blp = P.buf("lp")
                bp = AFa.take(2 * NB).rearrange("p (a n) -> p a n", a=2)
                bbp = P.buf("bp")
                tt = [AFa.take(NB) for _ in range(10)]
                btt = P.buf("tt")
                ob = AHa.take(2 * NB).rearrange("p (a n) -> p a n", a=2)
                bob = P.buf("ob")
                for blk in range(16):
                    cs = slice(blk * NB, (blk + 1) * NB)
                    P.dma(SP, lp, lam_pl[j][:, :, cs], writes=[blp])
                    P.dma(SP, bp, b_pl[j][:, :, cs], writes=[bbp])
                    lr, li = lp[:, 0, :], lp[:, 1, :]
                    lam_bar(lr, li, lp[:, 2, :], NB, tt, [blp], btt)
                    mr, mi, a_, b_, den, kr, ki = tt[0], tt[1], tt[6], tt[7], tt[8], tt[2], tt[3]
                    P.op(DVE, lambda e: e.tensor_scalar(out=mr, in0=mr, scalar1=-1.0, scalar2=None, op0=ALU.add), reads=[btt], writes=[btt])
                    P.op(DVE, lambda e: e.tensor_tensor(out=a_, in0=mr, in1=lr, op=ALU.mult), reads=[btt, blp], writes=[btt])
                    P.op(DVE, lambda e: e.tensor_tensor(out=b_, in0=mi, in1=li, op=ALU.mult), reads=[btt, blp], writes=[btt])
                    P.op(DVE, lambda e: e.tensor_tensor(out=kr, in0=a_, in1=b_, op=ALU.add), reads=[btt], writes=[btt])
                    P.op(DVE, lambda e: e.tensor_tensor(out=a_, in0=mi, in1=lr, op=ALU.mult), reads=[btt, blp], writes=[btt])
                    P.op(DVE, lambda e: e.tensor_tensor(out=b_, in0=mr, in1=li, op=ALU.mult), reads=[btt, blp], writes=[btt])
                    P.op(DVE, lambda e: e.tensor_tensor(out=ki, in0=a_, in1=b_, op=ALU.subtract), reads=[btt], writes=[btt])
                    P.op(DVE, lambda e: e.tensor_tensor(out=a_, in0=lr, in1=lr, op=ALU.mult), reads=[blp], writes=[btt])
                    P.op(DVE, lambda e: e.tensor_tensor(out=b_, in0=li, in1=li, op=ALU.mult), reads=[blp], writes=[btt])
                    P.op(DVE, lambda e: e.tensor_tensor(out=den, in0=a_, in1=b_, op=ALU.add), reads=[btt], writes=[btt])
                    P.op(DVE, lambda e: e.reciprocal(out=den, in_=den), reads=[btt], writes=[btt])
                    P.op(DVE, lambda e: e.tensor_tensor(out=kr, in0=kr, in1=den, op=ALU.mult), reads=[btt], writes=[btt])
                    P.op(DVE, lambda e: e.tensor_tensor(out=ki, in0=ki, in1=den, op=ALU.mult), reads=[btt], writes=[btt])
                    P.op(DVE, lambda e: e.tensor_tensor(out=a_, in0=kr, in1=bp[:, 0, :], op=ALU.mult), reads=[btt, bbp], writes=[btt])
                    P.op(DVE, lambda e: e.tensor_tensor(out=b_, in0=ki, in1=bp[:, 1, :], op=ALU.mult), reads=[btt, bbp], writes=[btt])
                    P.op(DVE, lambda e: e.tensor_tensor(out=ob[:, 0, :], in0=a_, in1=b_, op=ALU.subtract), reads=[btt], writes=[bob])
                    P.op(DVE, lambda e: e.tensor_tensor(out=a_, in0=kr, in1=bp[:, 1, :], op=ALU.mult), reads=[btt, bbp], writes=[btt])
                    P.op(DVE, lambda e: e.tensor_tensor(out=b_, in0=ki, in1=bp[:, 0, :], op=ALU.mult), reads=[btt, bbp], writes=[btt])
                    P.op(DVE, lambda e: e.tensor_tensor(out=ob[:, 1, :], in0=a_, in1=b_, op=ALU.add), reads=[btt], writes=[bob])
                    P.dma(SP, bbar_d[j][:, :, cs], ob, reads=[bob], writes=[bbbar[j]])
                cpf = AFa.take(2 * 2048).rearrange("p (a n) -> p a n", a=2)
                bcpf = P.buf("cpf")
                cpb = AHa.take(2 * 2048).rearrange("p (a n) -> p a n", a=2)
                bcpb = P.buf("cpb")
                for blk in range(4):
                    cs = slice(blk * 2048, (blk + 1) * 2048)
                    P.dma(SP, cpf, c_pl[j][:, :, cs], writes=[bcpf])
                    P.op(ACT, lambda e: e.activation(out=cpb[:, 0, :], in_=cpf[:, 0, :], func=AF.Copy), reads=[bcpf], writes=[bcpb])
                    P.op(DVE, lambda e: e.tensor_scalar(out=cpb[:, 1, :], in0=cpf[:, 1, :], scalar1=-1.0, scalar2=None, op0=ALU.mult), reads=[bcpf], writes=[bcpb])
                    P.dma(SP, cpad_d[j][:, :, cs], cpb, reads=[bcpb], writes=[bcpad[j]])

        if CFG["setup"]:
            s5_setup()

        def s5_mixer(j, T, sample, last):
            layer = 2 * j
            phase()
            TB = 16
            nb = 2 if sample else 1
            L = TB // nb
            u = AFa.take(KC * T).rearrange("p (k t) -> p k t", k=KC)
            bu = P.buf("u")
            rmsnorm(T, V_MIX + layer, xn, bxn, out_f32=u, out_f32_buf=bu)
            Bp = AHa.take(2 * 8192).rearrange("p (a n) -> p a n", a=2)
            bBp = P.buf("Bp")
            P.dma(SP, Bp, bbar_d[j], reads=[bbbar[j]], writes=[bBp])
            Cq = []
            for gq in range(4):
                if gq < 3:
                    cw = wslots[gq][:, 0:4096].rearrange("p (a n) -> p a n", a=2)
                    cb = wbufs[gq]
                else:
                    cw = AHa.take(4096).rearrange("p (a n) -> p a n", a=2)
                    cb = P.buf("Cq3")
                P.dma(SP, cw, cpad_d[j][:, :, gq * 2048:(gq + 1) * 2048], reads=[bcpad[j]], writes=[cb])
                Cq.append((cw, cb))
            S = AFa.take(2 * 64 * TB).rearrange("p (r g t) -> p r g t", r=2, g=64)
            bS = P.buf("S")
            Sb = AHa.take(2 * 64 * 2 * TB).rearrange("p (r g t) -> p r g t", r=2, g=64)
            bSb = P.buf("Sb")
            cs_t = AFa.take(64 * TB).rearrange("p (g t) -> p g t", g=64)
            sn_t = AFa.take(64 * TB).rearrange("p (g t) -> p g t", g=64)
            rt = AFa.take(64 * TB).rearrange("p (g t) -> p g t", g=64)
            btab = P.buf("tab")
            ta = AFa.take(2 * 64 * TB).rearrange("p (r g t) -> p r g t", r=2, g=64)
            bta = P.buf("ta")
            tb_ = AFa.take(64 * TB).rearrange("p (g t) -> p g t", g=64)
            btb = P.buf("tb")
            if sample:
                sini = AFa.take(2 * 64 * SPC).rearrange("p (r g b) -> p r g b", r=2, g=64)
                bsini = P.buf("sini")
                P.dma(SP, sini, s5init[j], writes=[bsini])
                sfin = AFa.take(2 * 64 * SPC).rearrange("p (r g b) -> p r g b", r=2, g=64)
                bsfin = P.buf("sfin")
            taurow = taus[:, 1 if sample else 0, :]
            argv = ta[:, 0]
            P.op(DVE, lambda e: e.tensor_tensor(out=argv, in0=lamS[:, j, 4, :].unsqueeze(2).to_broadcast([128, 64, TB]), in1=taurow.unsqueeze(1).to_broadcast([128, 64, TB]), op=ALU.mult), reads=[blamS, bconst], writes=[bta])
            argf = argv.rearrange("p g t -> p (g t)")
            tmpf = ta[:, 1].rearrange("p g t -> p (g t)")
            snf = sn_t.rearrange("p g t -> p (g t)")
            csf = cs_t.rearrange("p g t -> p (g t)")
            for c0 in range(0, 64 * TB, 512):
                sin_of(snf[:, c0:c0 + 512], argf[:, c0:c0 + 512], 0.0, 512, tmpf[:, c0:c0 + 512], [bta], [btab], bta)
                sin_of(csf[:, c0:c0 + 512], argf[:, c0:c0 + 512], math.pi / 2.0, 512, tmpf[:, c0:c0 + 512], [bta], [btab], bta)
            P.op(DVE, lambda e: e.tensor_copy(out=rt, in_=lamS[:, j, 3, :].unsqueeze(2).to_broadcast([128, 64, TB])), reads=[blamS], writes=[btab])
            P.op(DVE, lambda e: e.memset(rt.rearrange("p g (b l) -> p g b l", b=nb)[:, :, :, 0:1], 0.0), writes=[btab])
            magb = lamS[:, j, 3, :].unsqueeze(1).unsqueeze(3).to_broadcast([128, 2, 64, nb])
            csb = cs_t.unsqueeze(1).to_broadcast([128, 2, 64, TB])
            for kc in range(KC):
                P.op(DVE, lambda e, kc=kc: e.tensor_scalar(out=u[:, kc, 0:T], in0=u[:, kc, 0:T], scalar1=vec[:, V_S5D + j, kc:kc + 1], scalar2=None, op0=ALU.mult), reads=[bu, bconst], writes=[bu])
            tcar = tb_.rearrange("p g t -> p (g t)")[:, 0:2 * 64 * nb].rearrange("p (r g b) -> p r g b", r=2, g=64)
            for blk in range(T // TB):
                t0 = blk * TB
                for gc in range(KC):
                    ps, pb = next_ps()
                    for ri in range(2):
                        for jj in range(4):
                            P.op(PE, lambda e, ps=ps, ri=ri, jj=jj, gc=gc, t0=t0: e.matmul(ps[:, (ri * 4 + jj) * TB:(ri * 4 + jj + 1) * TB], lhsT=Bp[:, ri, (gc * 4 + jj) * 128:(gc * 4 + jj + 1) * 128], rhs=xn[:, gc, t0:t0 + TB], start=True, stop=True), reads=[bBp, bxn], writes=[pb])
                    P.op(ACT, lambda e, ps=ps, gc=gc: e.activation(out=S[:, :, gc * 4:(gc + 1) * 4, :], in_=ps[:, 0:8 * TB].rearrange("p (r g t) -> p r g t", r=2, g=4), func=AF.Copy), reads=[pb], writes=[bS])
                P.op(DVE, lambda e: e.tensor_tensor(out=ta, in0=S, in1=csb, op=ALU.mult), reads=[bS, btab], writes=[bta])
                P.op(DVE, lambda e: e.tensor_tensor(out=tb_, in0=S[:, 1], in1=sn_t, op=ALU.mult), reads=[bS, btab], writes=[btb])
                P.op(DVE, lambda e: e.tensor_tensor(out=ta[:, 0], in0=ta[:, 0], in1=tb_, op=ALU.add), reads=[bta, btb], writes=[bta])
                P.op(DVE, lambda e: e.tensor_tensor(out=tb_, in0=S[:, 0], in1=sn_t, op=ALU.mult), reads=[bS, btab], writes=[btb])
                P.op(DVE, lambda e: e.tensor_tensor(out=ta[:, 1], in0=ta[:, 1], in1=tb_, op=ALU.subtract), reads=[bta, btb], writes=[bta])
                if sample:
                    car, bcar = sini[:, :, :, blk * nb:(blk + 1) * nb], bsini
                else:
                    car, bcar = s5S[:, j, :, :].unsqueeze(3), bs5S[j]
                ta0 = ta.rearrange("p r g (b l) -> p r g b l", b=nb)[:, :, :, :, 0]
                P.op(DVE, lambda e, car=car: e.tensor_tensor(out=tcar, in0=car, in1=magb, op=ALU.mult), reads=[bcar, blamS, btb], writes=[btb])
                P.op(DVE, lambda e, ta0=ta0: e.tensor_tensor(out=ta0, in0=ta0, in1=tcar, op=ALU.add), reads=[bta, btb], writes=[bta])
                for ri in range(2):
                    P.op(DVE, lambda e, ri=ri: e.tensor_tensor_scan(out=S[:, ri].rearrange("p g t -> p (g t)"), data0=rt.rearrange("p g t -> p (g t)"), data1=ta[:, ri].rearrange("p g t -> p (g t)"), initial=0.0, op0=ALU.mult, op1=ALU.add), reads=[bta, btab], writes=[bS])
                P.op(DVE, lambda e: e.tensor_tensor(out=ta, in0=S, in1=csb, op=ALU.mult), reads=[bS, btab], writes=[bta])
                P.op(DVE, lambda e: e.tensor_tensor(out=tb_, in0=S[:, 1], in1=sn_t, op=ALU.mult), reads=[bS, btab], writes=[btb])
                P.op(DVE, lambda e: e.tensor_tensor(out=ta[:, 0], in0=ta[:, 0], in1=tb_, op=ALU.subtract), reads=[bta, btb], writes=[bta])
                P.op(DVE, lambda e: e.tensor_tensor(out=tb_, in0=S[:, 0], in1=sn_t, op=ALU.mult), reads=[bS, btab], writes=[btb])
                P.op(DVE, lambda e: e.tensor_tensor(out=ta[:, 1], in0=ta[:, 1], in1=tb_, op=ALU.add), reads=[bta, btb], writes=[bta])
                taL = ta.rearrange("p r g (b l) -> p r g b l", b=nb)[:, :, :, :, L - 1]
                if sample:
                    P.op(DVE, lambda e, blk=blk, taL=taL: e.tensor_copy(out=sfin[:, :, :, blk * nb:(blk + 1) * nb], in_=taL), reads=[bta], writes=[bsfin])
                else:
                    P.op(DVE, lambda e, taL=taL: e.tensor_copy(out=s5S[:, j, :, :].unsqueeze(3), in_=taL), reads=[bta], writes=[bs5S[j]])
                hb = blk % 2
                P.op(POOL, lambda e, hb=hb: e.tensor_copy(out=Sb[:, :, :, hb * TB:(hb + 1) * TB], in_=ta), reads=[bta], writes=[bSb])
                if hb == 0:
                    continue
                psy, pby = next_ps()
                TB2 = 2 * TB
                for gq in range(4):
                    Cw, cb = Cq[gq]
                    for g4 in range(4):
                        gc = gq * 4 + g4
                        n = 0
                        for ri in range(2):
                            for jj in range(4):
                                P.op(PE, lambda e, Cw=Cw, ri=ri, jj=jj, g4=g4, gc=gc, n=n, psy=psy: e.matmul(psy[:, gc * TB2:(gc + 1) * TB2], lhsT=Cw[:, ri, (g4 * 4 + jj) * 128:(g4 * 4 + jj + 1) * 128], rhs=Sb[:, ri, gc * 4 + jj, :], start=(n == 0), stop=(n == 7)), reads=[cb, bSb], writes=[pby])
                                n += 1
                tq = t0 - TB
                P.op(DVE, lambda e, tq=tq, psy=psy: e.tensor_tensor(out=u[:, :, tq:tq + TB2], in0=u[:, :, tq:tq + TB2], in1=psy[:, 0:KC * TB2].rearrange("p (k t) -> p k t", k=KC), op=ALU.add), reads=[bu, pby], writes=[bu])
            if sample:
                P.dma(SP, o_s5s[j], sfin, reads=[bsfin])
            elif last:
                P.dma(SP, o_s5p[j], s5S[:, j, :, :], reads=[bs5S[j]])
            taf = ta.rearrange("p r g t -> p (r g t)")
            g1 = taf[:, 0:TP]
            g2 = taf[:, TP:2 * TP]
            bg1, bg2 = bta, bta
            for kc in range(KC):
                v = u[:, kc, 0:T]
                P.op(ACT, lambda e, v=v: e.activation(out=g1[:, 0:T], in_=v, func=AF.Square), reads=[bu], writes=[bg1])
                P.op(DVE, lambda e: e.tensor_scalar(out=g1[:, 0:T], in0=g1[:, 0:T], scalar1=0.044715, scalar2=1.0, op0=ALU.mult, op1=ALU.add), reads=[bg1], writes=[bg1])
                P.op(DVE, lambda e, v=v: e.tensor_tensor(out=g1[:, 0:T], in0=g1[:, 0:T], in1=v, op=ALU.mult), reads=[bg1, bu], writes=[bg1])
                P.op(ACT, lambda e: e.activation(out=g2[:, 0:T], in_=g1[:, 0:T], func=AF.Sigmoid, scale=1.5957691216057308), reads=[bg1], writes=[bg2])
                P.op(DVE, lambda e, v=v: e.tensor_tensor(out=v, in0=v, in1=g2[:, 0:T], op=ALU.mult), reads=[bg2, bu], writes=[bu])
                P.op(POOL, lambda e, v=v, kc=kc: e.tensor_copy(out=xn[:, kc, 0:T], in_=v), reads=[bu], writes=[bxn])

            def evg(jc, ps, pb):
                P.op(ACT, lambda e, ps=ps: e.activation(out=g2[:, 0:T], in_=ps[:, 0:T], func=AF.Sigmoid), reads=[pb], writes=[bg2])
                P.op(DVE, lambda e, jc=jc: e.tensor_tensor(out=g2[:, 0:T], in0=g2[:, 0:T], in1=u[:, jc, 0:T], op=ALU.mult), reads=[bg2, bu], writes=[bg2])
                P.op(DVE, lambda e, jc=jc: e.tensor_tensor(out=x[:, jc, 0:T], in0=x[:, jc, 0:T], in1=g2[:, 0:T], op=ALU.add), reads=[bg2, bx], writes=[bx])
            proj_fm(w_glu[j], KC, D, xn, bxn, T, evg)

        def hgrn_mixer(j, layer, T, sample, last, first):
            phase()
            rmsnorm(T, V_MIX + layer, xn, bxn)
            CS = 8 if sample else 64
            nch = T // CS
            W = hg_w_in[j]
            Sst = AFa.take(16 * 128).rearrange("p (h v) -> p h v", h=16)
            bSst = P.buf("Sst")
            if not sample:
                if first:
                    P.op(DVE, lambda e: e.memset(Sst, 0.0), writes=[bSst])
                else:
                    P.dma(SP, Sst, hgscr[j], reads=[bhgscr[j]], writes=[bSst])
            sgm = AFa.take(4 * TP).rearrange("p (h t) -> p h t", h=4)
            lgf = AFa.take(4 * TP).rearrange("p (h t) -> p h t", h=4)
            eg = AFa.take(4 * TP).rearrange("p (h t) -> p h t", h=4)
            ob = AFa.take(4 * TP).rearrange("p (h t) -> p h t", h=4)
            gs = AFa.take(4 * TP).rearrange("p (h t) -> p h t", h=4)
            qs = AFa.take(2 * TP).rearrange("p (h t) -> p h t", h=2)
            rr = AFa.take(TP)
            NSL = 8
            Sld = AFa.take(NSL * 128).rearrange("p (a v) -> p a v", a=NSL)
            stmp4 = AFa.take(4 * 128).rearrange("p (a v) -> p a v", a=4)
            qt = AHa.take(4 * TP).rearrange("p (h t) -> p h t", h=4)
            kt = AHa.take(4 * TP).rearrange("p (h t) -> p h t", h=4)
            og = AHa.take(4 * TP).rearrange("p (h t) -> p h t", h=4)
            osq = AHa.take(4 * TP).rearrange("p (h t) -> p h t", h=4)
            vtok = AHa.take(nch * 512).rearrange("p (c v) -> p c v", c=nch)
            kTs4 = AHa.take(4 * 128).rearrange("p (a v) -> p a v", a=4)
            PT4 = AHa.take(4 * 64).rearrange("p (a v) -> p a v", a=4)
            Sbf = AHa.take(4 * 128).rearrange("p (h v) -> p h v", h=4)
            for hg in range(4):
                bsgm, blgf, beg, bob, bgs, brr = (P.buf(n) for n in ("sgm", "lgf", "eg", "ob", "gs", "rr"))
                bqs = [P.buf("qs0"), P.buf("qs1")]
                bSld = [P.buf("Sld%d" % i) for i in range(NSL)]
                bqt, bkt, bog, bosq, bvtok = (P.buf(n) for n in ("qt", "kt", "og", "osq", "vtok"))
                bstmp4 = [P.buf("stmp%d" % i) for i in range(4)]
                bkTs4 = [P.buf("kTs%d" % i) for i in range(4)]
                bPT4 = [P.buf("PT%d" % i) for i in range(4)]
                bSbf4 = [P.buf("Sbf%d" % i) for i in range(4)]
                slctr = [0]
                h0 = hg * 4
                def evf(jh, ps, pb):
                    P.op(ACT, lambda e, jh=jh, ps=ps: e.activation(out=sgm[:, jh, 0:T], in_=ps[:, 0:T], func=AF.Sigmoid), reads=[pb], writes=[bsgm])
                    P.op(DVE, lambda e, jh=jh, h0=h0: e.tensor_scalar(out=sgm[:, jh, 0:T], in0=sgm[:, jh, 0:T], scalar1=oml[:, layer, h0 + jh:h0 + jh + 1], scalar2=lbv[:, layer, h0 + jh:h0 + jh + 1], op0=ALU.mult, op1=ALU.add), reads=[bsgm, bconst], writes=[bsgm])
                    P.op(ACT, lambda e, jh=jh: e.activation(out=lgf[:, jh, 0:T], in_=sgm[:, jh, 0:T], func=AF.Ln), reads=[bsgm], writes=[blgf])
                    P.op(DVE, lambda e, jh=jh: e.tensor_scalar(out=sgm[:, jh, 0:T], in0=sgm[:, jh, 0:T], scalar1=-1.0, scalar2=1.0, op0=ALU.mult, op1=ALU.add), reads=[bsgm], writes=[bsgm])
                proj_fm(W[:, D + hg * 512:D + (hg + 1) * 512], KC, 512, xn, bxn, T, evf)
                for jh in range(4):
                    for c in range(nch):
                        P.op(DVE, lambda e, jh=jh, c=c: e.tensor_tensor_scan(out=eg[:, jh, c * CS:(c + 1) * CS], data0=onesf[:, 0:CS], data1=lgf[:, jh, c * CS:(c + 1) * CS], initial=0.0, op0=ALU.mult, op1=ALU.add), reads=[blgf, bconst], writes=[beg])
                P.op(ACT, lambda e: e.activation(out=lgf[:, :, 0:T], in_=eg[:, :, 0:T], func=AF.Exp), reads=[beg], writes=[blgf])
                P.op(ACT, lambda e: e.activation(out=eg[:, :, 0:T], in_=eg[:, :, 0:T], func=AF.Exp, scale=-1.0), reads=[beg, blgf], writes=[beg])
                P.op(DVE, lambda e: e.tensor_tensor(out=kt[:, :, 0:T], in0=sgm[:, :, 0:T], in1=eg[:, :, 0:T], op=ALU.mult), reads=[bsgm, beg], writes=[bkt])
                def evq(jh, ps, pb):
                    s = jh % 2
                    P.op(ACT, lambda e, s=s, ps=ps: e.activation(out=qs[:, s, 0:T], in_=ps[:, 0:T], func=AF.Silu), reads=[pb], writes=[bqs[s]])
                    P.op(DVE, lambda e, s=s, jh=jh: e.tensor_tensor(out=qt[:, jh, 0:T], in0=qs[:, s, 0:T], in1=lgf[:, jh, 0:T], op=ALU.mult), reads=[bqs[s], blgf], writes=[bqt])
                proj_fm(W[:, hg * 512:(hg + 1) * 512], KC, 512, xn, bxn, T, evq)
                for half in range(2):
                    c0 = 2 * D + hg * 512 + half * 256
                    wt, wb = wload(W[:, c0:c0 + 256], KC, 256)
                    for c in range(nch):
                        ps, pb = next_ps()
                        for kc in range(KC):
                            P.op(PE, lambda e, ps=ps, wt=wt, kc=kc, c=c: e.matmul(ps[0:CS, 0:256], lhsT=xn[:, kc, c * CS:(c + 1) * CS], rhs=wt[:, kc, :], start=(kc == 0), stop=(kc == KC - 1)), reads=[wb, bxn], writes=[pb])
                        P.op(ACT, lambda e, ps=ps, c=c, half=half: e.activation(out=vtok[0:CS, c, half * 256:(half + 1) * 256], in_=ps[0:CS, 0:256], func=AF.Copy), reads=[pb], writes=[bvtok])
                def evg(jh, ps, pb):
                    P.op(ACT, lambda e, jh=jh, ps=ps: e.activation(out=gs[:, jh, 0:T], in_=ps[:, 0:T], func=AF.Silu), reads=[pb], writes=[bgs])
                proj_fm(W[:, 3 * D + hg * 512:3 * D + (hg + 1) * 512], KC, 512, xn, bxn, T, evg)
                if not sample:
                    for jh in range(4):
                        P.op(ACT, lambda e, jh=jh, h=h0 + jh: e.activation(out=Sbf[:, jh, :], in_=Sst[:, h, :], func=AF.Copy), reads=[bSst], writes=[bSbf4[jh]])
                for c in range(nch):
                    for jh in range(4):
                        h = h0 + jh
                        stmp, bstmp = stmp4[:, jh, :], bstmp4[jh]
                        kTs, bkTs = kTs4[:, jh, :], bkTs4[jh]
                        PT, bPT = PT4[:, jh, :], bPT4[jh]
                        bSbf = bSbf4[jh]
                        cs = slice(c * CS, (c + 1) * CS)
                        if sample:
                            sl = slctr[0] % NSL
                            slctr[0] += 1
                            P.dma(SP, Sld[:, sl, :], hgst[j, c, h], writes=[bSld[sl]])
                            P.op(ACT, lambda e, jh=jh, sl=sl: e.activation(out=Sbf[:, jh, :], in_=Sld[:, sl, :], func=AF.Copy), reads=[bSld[sl]], writes=[bSbf])
                            Scur, bScur = Sld[:, sl, :], bSld[sl]
                        else:
                            Scur, bScur = Sst[:, h, :], bSst
                        ps1, pb1 = next_ps()
                        P.op(PE, lambda e, ps1=ps1, jh=jh, cs=cs: e.matmul(ps1[0:CS, 0:CS], lhsT=kt[:, jh, cs], rhs=qt[:, jh, cs], start=True, stop=True), reads=[bkt, bqt], writes=[pb1])
                        P.op(DVE, lambda e, ps1=ps1, PT=PT: e.tensor_tensor(out=PT[0:CS, 0:CS], in0=ps1[0:CS, 0:CS], in1=maskT[0:CS, 0:CS], op=ALU.mult), reads=[pb1, bconst], writes=[bPT])
                        pv = psb[7][:].bitcast(BF16)
                        P.op(PE, lambda e, jh=jh, cs=cs, pv=pv: e.transpose(pv[0:CS, jh * 128:(jh + 1) * 128], kt[:, jh, cs], identb[:]), reads=[bkt, bconst], writes=[bps[7]])
                        P.op(ACT, lambda e, pv=pv, kTs=kTs, jh=jh: e.activation(out=kTs[0:CS, :], in_=pv[0:CS, jh * 128:(jh + 1) * 128], func=AF.Copy), reads=[bps[7]], writes=[bkTs])
                        ps2, pb2 = next_ps()
                        P.op(PE, lambda e, ps2=ps2, c=c, jh=jh, PT=PT: e.matmul(ps2[:, 0:CS], lhsT=vtok[0:CS, c, jh * 128:(jh + 1) * 128], rhs=PT[0:CS, 0:CS], start=True, stop=False), reads=[bvtok, bPT], writes=[pb2])
                        P.op(PE, lambda e, ps2=ps2, cs=cs, jh=jh: e.matmul(ps2[:, 0:CS], lhsT=Sbf[:, jh, :], rhs=qt[:, jh, cs], start=False, stop=True), reads=[bSbf, bqt], writes=[pb2])
                        P.op(ACT, lambda e, ps2=ps2, jh=jh, cs=cs: e.activation(out=ob[:, jh, cs], in_=ps2[:, 0:CS], func=AF.Copy), reads=[pb2], writes=[bob])
                        ps3, pb3 = next_ps()
                        P.op(PE, lambda e, ps3=ps3, c=c, jh=jh, kTs=kTs: e.matmul(ps3[:, 0:128], lhsT=kTs[0:CS, :], rhs=vtok[0:CS, c, jh * 128:(jh + 1) * 128], start=True, stop=True), reads=[bkTs, bvtok], writes=[pb3])
                        P.op(DVE, lambda e, ps3=ps3, Scur=Scur, stmp=stmp: e.tensor_tensor(out=stmp, in0=Scur, in1=ps3[:, 0:128], op=ALU.add), reads=[pb3, bScur], writes=[bstmp])
                        P.op(DVE, lambda e, Scur=Scur, jh=jh, c=c, stmp=stmp: e.tensor_scalar(out=Scur, in0=stmp, scalar1=lgf[:, jh, (c + 1) * CS - 1:(c + 1) * CS], scalar2=None, op0=ALU.mult), reads=[bstmp, blgf], writes=[bScur])
                        if sample:
                            P.dma(SP, o_hgs[j, c, h], Scur, reads=[bScur])
                        else:
                            P.op(ACT, lambda e, jh=jh, Scur=Scur: e.activation(out=Sbf[:, jh, :], in_=Scur, func=AF.Copy), reads=[bScur], writes=[bSbf])
                P.op(ACT, lambda e: e.activation(out=osq[:, :, 0:T], in_=ob[:, :, 0:T], func=AF.Square), reads=[bob], writes=[bosq])
                for jh in range(4):
                    h = h0 + jh
                    ps, pb = next_ps()
                    P.op(PE, lambda e, ps=ps, jh=jh: e.matmul(ps[:, 0:T], lhsT=onesb[:], rhs=osq[:, jh, 0:T], start=True, stop=True), reads=[bosq, bconst], writes=[pb])
                    P.op(DVE, lambda e, ps=ps: e.tensor_scalar(out=rr[:, 0:T], in0=ps[:, 0:T], scalar1=1.0 / 128.0, scalar2=EPS, op0=ALU.mult, op1=ALU.add), reads=[pb], writes=[brr])
                    P.op(ACT, lambda e: e.activation(out=rr[:, 0:T], in_=rr[:, 0:T], func=AF.Sqrt), reads=[brr], writes=[brr])
                    P.op(DVE, lambda e: e.reciprocal(out=rr[:, 0:T], in_=rr[:, 0:T]), reads=[brr], writes=[brr])
                    P.op(DVE, lambda e, jh=jh, h=h: e.scalar_tensor_tensor(out=ob[:, jh, 0:T], in0=ob[:, jh, 0:T], scalar=vec[:, V_GN + j, h:h + 1], in1=rr[:, 0:T], op0=ALU.mult, op1=ALU.mult), reads=[bob, brr, bconst], writes=[bob])
                    P.op(DVE, lambda e, jh=jh: e.tensor_tensor(out=og[:, jh, 0:T], in0=ob[:, jh, 0:T], in1=gs[:, jh, 0:T], op=ALU.mult), reads=[bob, bgs], writes=[bog])
                proj_fm(hg_w_out[j][hg * 512:(hg + 1) * 512, :], 4, D, og, bog, T, add_to_x(T))
            if not sample:
                P.dma(SP, hgscr[j], Sst, reads=[bSst], writes=[bhgscr[j]])
                if last:
                    P.dma(SP, o_hgp[j].rearrange("h k v -> k h v"), Sst, reads=[bSst])

        def run_pass(T, src, dst, sample, first, last):
            phase()
            xt = AFa.take(D)
            bxt = P.buf("xt")
            for tt in range(T // 128):
                P.dma(SP, xt, src[tt * 128:(tt + 1) * 128, :], writes=[bxt])
                transpose_blocks(lambda i, tt=tt: x[:, i, tt * 128:(tt + 1) * 128], lambda i: xt[:, i * 128:(i + 1) * 128], KC, bxt, bx)
            for layer in range(4):
                j = layer // 2
                if layer % 2 == 0:
                    if CFG["s5"]:
                        s5_mixer(j, T, sample, last)
                else:
                    if CFG["hg"]:
                        hgrn_mixer(j, layer, T, sample, last, first)
                if CFG["xa"]:
                    xattn(layer, T, sample, first and not sample)
                if CFG["ffn"]:
                    ffn(layer, T)
            phase()
            yf = AFa.take(KC * TP).rearrange("p (k t) -> p k t", k=KC)
            byf = P.buf("yf")
            rmsnorm(T, V_FIN, xn, bxn, out_f32=yf, out_f32_buf=byf)
            yt = AFa.take(D)
            byt = P.buf("yt")
            for tt in range(T // 128):
                transpose_blocks(lambda i: yt[:, i * 128:(i + 1) * 128], lambda i, tt=tt: yf[:, i, tt * 128:(tt + 1) * 128], KC, byf, byt)
                P.dma(SP, dst[tt * 128:(tt + 1) * 128, :], yt, reads=[byt])

        for p in range(CFG["npass"]):
            run_pass(TP, xp[p * TP:(p + 1) * TP, :], o_yp[p * TP:(p + 1) * TP, :], False, p == 0, p == CFG["npass"] - 1)
        if CFG["sample"]:
            run_pass(128, xsm, o_ys, True, False, False)
        print("sbuf remaining", nc.sbuf_bytes_remaining)
        stats = P.emit()
        print("ops", stats)
    return nc


def _vec_layout(v):
    return np.ascontiguousarray(v.reshape(KC, 128).T)


_PROG = None


def kernel(_return_maps=False, **inp):
    f = np.float32
    g = {k: np.asarray(v) for k, v in inp.items()}
    vec = np.zeros((128, NV, 16), f)
    for l in range(4):
        vec[:, V_MIX + l] = _vec_layout(g["norm_mix"][l])
        vec[:, V_XA + l] = _vec_layout(g["norm_xattn"][l])
        vec[:, V_MEM + l] = _vec_layout(g["norm_mem_in"][l])
        vec[:, V_FFN + l] = _vec_layout(g["norm_ffn"][l])
        vec[:, V_LB + l] = _vec_layout(g["hg_lower_bounds"][l])
    vec[:, V_FIN] = _vec_layout(g["norm_final"])
    for j in range(2):
        vec[:, V_S5D + j] = _vec_layout(g["s5_d"][j])
        vec[:, V_GN + j] = _vec_layout(g["hg_g_norm"][j])
    ident = np.eye(128, dtype=f)
    maskT = np.triu(np.ones((64, 64), f))
    taus = np.zeros((128, 2, 16), f)
    taus[:, 0, :] = np.arange(1, 17, dtype=f)[None]
    taus[:, 1, :] = np.tile(np.arange(1, 9, dtype=f), 2)[None]
    lam_sl = np.zeros((2, 128, 3, 64), f)
    lam_pl = np.zeros((2, 128, 3, 8192), f)
    b_pl = np.zeros((2, 128, 2, 8192), f)
    c_pl = np.zeros((2, 128, 2, 8192), f)
    gidx = np.zeros((16, 4, 2), np.int64)
    for gc in range(16):
        for jj in range(4):
            for gh in range(2):
                gidx[gc, jj, gh] = 8 * gc + 4 * gh + jj
    for j in range(2):
        ld = np.broadcast_to(g["s5_log_dt"][j][:, None], (128, 64))
        for i, src in enumerate((g["s5_lam_re"][j], g["s5_lam_im"][j], ld)):
            a = src[gidx]
            lam_sl[j, :, i, :] = a.transpose(3, 2, 0, 1).reshape(128, 64)
            pl = a.transpose(0, 1, 3, 2).reshape(8192)
            lam_pl[j, :, i, :] = pl[None, :]
        for i, src in enumerate((g["s5_b_re"][j], g["s5_b_im"][j])):
            a = src[gidx]
            t = np.zeros((8, 16, 16, 4, 64, 2), f)
            for jj in range(4):
                for gh in range(2):
                    t[4 * gh + jj, :, :, jj, :, gh] = a[:, jj, gh].transpose(2, 0, 1)
            b_pl[j, :, i, :] = t.reshape(128, 8192)
        for i, src in enumerate((g["s5_c_re"][j], g["s5_c_im"][j])):
            a = src[gidx]
            t = np.zeros((64, 2, 16, 4, 8, 16), f)
            for jj in range(4):
                for gh in range(2):
                    t[:, gh, :, jj, 4 * gh + jj, :] = a[:, jj, gh].transpose(2, 0, 1)
            c_pl[j, :, i, :] = t.reshape(128, 8192)

    def s5_state_layout(re, im):
        out = np.zeros((128, 2, 64, re.shape[0]), f)
        for i, s in enumerate((re, im)):
            a = s[:, gidx]
            out[:, i] = a.transpose(4, 3, 1, 2, 0).reshape(128, 64, re.shape[0])
        return out

    def s5_state_unlayout(arr):
        B = arr.shape[-1]
        res = []
        for i in range(2):
            a = arr[:, i].reshape(64, 2, 16, 4, B)
            o = np.zeros((B, 128, 64), f)
            o[:, gidx] = a.transpose(4, 2, 3, 1, 0)
            res.append(o)
        return res

    in_maps = []
    for c in range(NCORE):
        s = c % 4
        sl = slice(c * SPC, (c + 1) * SPC)
        s5i = np.stack([s5_state_layout(g["state_s5_re"][j, sl], g["state_s5_im"][j, sl]) for j in range(2)])
        in_maps.append(dict(
            xp=np.ascontiguousarray(g["x_prompt"][s]), xsm=np.ascontiguousarray(g["x_sample"][sl].reshape(128, D)),
            memp=np.ascontiguousarray(g["mem_prompt"][s]),
            ck=np.ascontiguousarray(g["cache_mem_k"][:, sl].reshape(4, SPC, NMEM, D)),
            cv=np.ascontiguousarray(g["cache_mem_v"][:, sl].reshape(4, SPC, NMEM, D)),
            hgst=np.ascontiguousarray(g["state_hgrn"][:, sl]),
            vecs=vec, ident=ident, maskT=maskT, taus=taus, lam_sl=lam_sl, lam_pl=lam_pl, b_pl=b_pl, c_pl=c_pl, s5init=s5i,
            s5_w_glu=g["s5_w_glu"], hg_w_in=g["hg_w_in"], hg_w_out=g["hg_w_out"],
            x_w_q=g["x_w_q"], x_w_k=g["x_w_k"], x_w_v=g["x_w_v"], x_w_o=g["x_w_o"],
            ffn_w_in=g["ffn_w_in"], ffn_w_out=g["ffn_w_out"]))
    if _return_maps:
        return in_maps
    global _PROG
    if _PROG is None:
        _PROG = build_program()
    nc = _PROG
    res = run_bass_kernel_spmd(nc, in_maps, core_ids=list(range(NCORE))).results
    y_prompt = np.stack([res[s]["o_yp"] for s in range(4)]).astype(f)
    y_sample = np.concatenate([res[c]["o_ys"].reshape(SPC, 8, D) for c in range(NCORE)]).astype(f)
    s5p = [[s5_state_unlayout(res[s]["o_s5p"][j][..., None]) for s in range(4)] for j in range(2)]
    s5_re_p = np.stack([np.concatenate([s5p[j][s][0] for s in range(4)]) for j in range(2)])
    s5_im_p = np.stack([np.concatenate([s5p[j][s][1] for s in range(4)]) for j in range(2)])
    hg_p = np.stack([np.stack([res[s]["o_hgp"][j] for s in range(4)]) for j in range(2)])
    mk = np.stack([np.stack([res[s]["o_mk"][l].reshape(NMEM, 4, 512) for s in range(4)]) for l in range(4)])
    mv = np.stack([np.stack([res[s]["o_mv"][l].reshape(NMEM, 4, 512) for s in range(4)]) for l in range(4)])
    s5s = [[s5_state_unlayout(res[c]["o_s5s"][j]) for c in range(NCORE)] for j in range(2)]
    s5_re_s = np.stack([np.concatenate([s5s[j][c][0] for c in range(NCORE)]) for j in range(2)])
    s5_im_s = np.stack([np.concatenate([s5s[j][c][1] for c in range(NCORE)]) for j in range(2)])
    hg_s = np.stack([np.concatenate([res[c]["o_hgs"][j] for c in range(NCORE)]) for j in range(2)])
    return (y_prompt, y_sample, s5_re_p.astype(f), s5_im_p.astype(f), hg_p.astype(f), mk.astype(f), mv.astype(f),
            s5_re_s.astype(f), s5_im_s.astype(f), hg_s.astype(f))
```
